# Optimizing a Trainium2 kernel written in Bass

```python
import jax, jax.numpy as jnp
from jax import lax
import numpy as np

D_MODEL = 1024
BATCH = 4
SEQ = 4096
DEPTH = 4

GRID_W = 64
CTX_LEN = 256
N_MIXERS = 2
CHUNK = 128
A_WIDTH = 2 * D_MODEL
A_GROUPS = 8
A_GW = A_WIDTH // A_GROUPS
HEAD_DIM = 64
N_HEADS = D_MODEL // HEAD_DIM
N_KV_HEADS = N_HEADS // 4
GQA_GROUP = N_HEADS // N_KV_HEADS
B_WIDTH = N_HEADS * HEAD_DIM
KV_WIDTH = N_KV_HEADS * HEAD_DIM
WINDOW = 128
ATT_BLOCK = 128
BAND = ATT_BLOCK + 2 * WINDOW
ROPE_AXIS_DIM = HEAD_DIM // 2
ROPE_BASE = 10000.0
LN_EPS = 1e-5
NEG_INF = -1e30

kernel_name = "hybrid_gmlp_swa_prefix_dit"


def layer_norm(x, g, b):
    xf = x.astype(jnp.float32)
    mu = jnp.mean(xf, axis=-1, keepdims=True)
    var = jnp.mean(jnp.square(xf - mu), axis=-1, keepdims=True)
    y = (xf - mu) * lax.rsqrt(var + LN_EPS) * g.astype(jnp.float32) + b.astype(jnp.float32)
    return y.astype(x.dtype)


def chunk_spatial_mix(v, w_s, b_s):
    bsz, length, _ = v.shape
    vc = v.reshape(bsz, length // CHUNK, CHUNK, A_GROUPS, A_GW)
    s = jnp.einsum('gpq,bnqgc->bnpgc', w_s, vc) + b_s.T[None, None, :, :, None]
    return s.reshape(bsz, length, A_WIDTH)


def mixer_a(h, w_in, ln_g, ln_b, w_s, b_s, w_out):
    u, v, z = jnp.split(h @ w_in, 3, axis=-1)
    u = jax.nn.gelu(u)
    v = layer_norm(jax.nn.gelu(v), ln_g, ln_b)
    y = u * chunk_spatial_mix(v, w_s, b_s)
    return (y * jax.nn.silu(z)) @ w_out


def axial_rope_angles(length):
    rows = length // GRID_W
    row = jnp.repeat(jnp.arange(rows), GRID_W).astype(jnp.float32)
    col = jnp.tile(jnp.arange(GRID_W), rows).astype(jnp.float32)
    n_freq = ROPE_AXIS_DIM // 2
    inv = ROPE_BASE ** (-jnp.arange(n_freq, dtype=jnp.float32) / n_freq)
    return row[:, None] * inv[None, :], col[:, None] * inv[None, :]


def rotate_axis(x, ang):
    x1, x2 = jnp.split(x, 2, axis=-1)
    cos = jnp.cos(ang)[None, :, None, :].astype(x.dtype)
    sin = jnp.sin(ang)[None, :, None, :].astype(x.dtype)
    return jnp.concatenate([x1 * cos - x2 * sin, x2 * cos + x1 * sin], axis=-1)


def apply_axial_rope(x, ang_row, ang_col):
    xr, xc = jnp.split(x, 2, axis=-1)
    return jnp.concatenate([rotate_axis(xr, ang_row), rotate_axis(xc, ang_col)], axis=-1)


def project_qkvz(h, w_in):
    bsz, length, _ = h.shape
    q, k, v, z = jnp.split(h @ w_in, [B_WIDTH, B_WIDTH + KV_WIDTH, B_WIDTH + 2 * KV_WIDTH], axis=-1)
    return (q.reshape(bsz, length, N_HEADS, HEAD_DIM),
            k.reshape(bsz, length, N_KV_HEADS, HEAD_DIM),
            v.reshape(bsz, length, N_KV_HEADS, HEAD_DIM), z)


def project_kv(h, w_in):
    bsz, length, _ = h.shape
    k, v = jnp.split(h @ w_in[:, B_WIDTH:B_WIDTH + 2 * KV_WIDTH], 2, axis=-1)
    return (k.reshape(bsz, length, N_KV_HEADS, HEAD_DIM),
            v.reshape(bsz, length, N_KV_HEADS, HEAD_DIM))


def windowed_attention_with_context(q, k, v, k_ctx, v_ctx, sink):
    bsz, length = q.shape[:2]
    n_ctx = k_ctx.shape[1]
    nb = length // ATT_BLOCK
    scale = HEAD_DIM ** -0.5
    qg = q.reshape(bsz, length, N_KV_HEADS, GQA_GROUP, HEAD_DIM)
    pad = ((0, 0), (WINDOW, WINDOW), (0, 0), (0, 0))
    k_pad = jnp.pad(k, pad)
    v_pad = jnp.pad(v, pad)
    sink_l = sink.astype(jnp.float32).reshape(1, N_KV_HEADS, GQA_GROUP, 1, 1)
    qi = jnp.arange(ATT_BLOCK)[:, None]
    kj = jnp.arange(BAND)[None, :]
    rel = kj - WINDOW - qi

    def block(i):
        start = i * ATT_BLOCK
        qb = lax.dynamic_slice_in_dim(qg, start, ATT_BLOCK, axis=1)
        kb = lax.dynamic_slice_in_dim(k_pad, start, BAND, axis=1)
        vb = lax.dynamic_slice_in_dim(v_pad, start, BAND, axis=1)
        key_pos = start - WINDOW + kj
        valid = (jnp.abs(rel) <= WINDOW) & (key_pos >= 0) & (key_pos < length)
        s_loc = jnp.einsum('bqhgd,bkhd->bhgqk', qb, kb).astype(jnp.float32) * scale
        s_loc = jnp.where(valid, s_loc, NEG_INF)
        s_ctx = jnp.einsum('bqhgd,bkhd->bhgqk', qb, k_ctx).astype(jnp.float32) * scale
        s_sink = jnp.broadcast_to(sink_l, s_loc.shape[:-1] + (1,))
        p = jax.nn.softmax(jnp.concatenate([s_loc, s_ctx, s_sink], axis=-1), axis=-1)
        p_loc = p[..., :BAND].astype(v.dtype)
        p_ctx = p[..., BAND:BAND + n_ctx].astype(v.dtype)
        return (jnp.einsum('bhgqk,bkhd->bqhgd', p_loc, vb)
                + jnp.einsum('bhgqk,bkhd->bqhgd', p_ctx, v_ctx))

    out = lax.map(block, jnp.arange(nb))
    return jnp.moveaxis(out, 0, 1).reshape(bsz, length, B_WIDTH)


def context_attention(q_ctx, k_ctx, v_ctx, sink):
    bsz, n_ctx = q_ctx.shape[:2]
    scale = HEAD_DIM ** -0.5
    qg = q_ctx.reshape(bsz, n_ctx, N_KV_HEADS, GQA_GROUP, HEAD_DIM)
    s = jnp.einsum('bqhgd,bkhd->bhgqk', qg, k_ctx).astype(jnp.float32) * scale
    s_sink = jnp.broadcast_to(sink.astype(jnp.float32).reshape(1, N_KV_HEADS, GQA_GROUP, 1, 1),
                              s.shape[:-1] + (1,))
    p = jax.nn.softmax(jnp.concatenate([s, s_sink], axis=-1), axis=-1)
    o = jnp.einsum('bhgqk,bkhd->bqhgd', p[..., :n_ctx].astype(v_ctx.dtype), v_ctx)
    return o.reshape(bsz, n_ctx, B_WIDTH)


def mixer_b(h, hc, w_in, sink, w_out, ang_row, ang_col, need_ctx_out):
    q, k, v, z = project_qkvz(h, w_in)
    q = apply_axial_rope(q, ang_row, ang_col)
    k = apply_axial_rope(k, ang_row, ang_col)
    if need_ctx_out:
        qc, kc, vc, zc = project_qkvz(hc, w_in)
    else:
        kc, vc = project_kv(hc, w_in)
    o = windowed_attention_with_context(q, k, v, kc, vc, sink)
    out = (o * jax.nn.silu(z)) @ w_out
    out_c = None
    if need_ctx_out:
        oc = context_attention(qc, kc, vc, sink)
        out_c = (oc * jax.nn.silu(zc)) @ w_out
    return out, out_c


def setup_inputs(seed: int = 0) -> dict:
    key = jax.random.key(seed)
    ks = jax.random.split(key, 20)
    n_a = (DEPTH + N_MIXERS - 1) // N_MIXERS
    n_b = DEPTH // N_MIXERS
    beta = (8.0 * DEPTH) ** -0.25
    nrm = jax.random.normal
    f32 = jnp.float32
    return {
        "x": nrm(ks[0], (BATCH, SEQ, D_MODEL), f32),
        "c": nrm(ks[1], (BATCH, D_MODEL), f32),
        "ctx": nrm(ks[2], (BATCH, CTX_LEN, D_MODEL), f32),
        "c_ctx": nrm(ks[3], (D_MODEL,), f32),
        "ada_w": nrm(ks[4], (DEPTH, D_MODEL, 3 * D_MODEL), f32) * (0.5 * D_MODEL ** -0.5),
        "ada_b": nrm(ks[5], (DEPTH, 3 * D_MODEL), f32) * 0.01,
        "ln_g": 1.0 + 0.05 * nrm(ks[6], (DEPTH, D_MODEL), f32),
        "ln_b": 0.01 * nrm(ks[7], (DEPTH, D_MODEL), f32),
        "a_w_in": nrm(ks[8], (n_a, D_MODEL, 3 * A_WIDTH), f32) * D_MODEL ** -0.5,
        "a_ln_g": 1.0 + 0.05 * nrm(ks[9], (n_a, A_WIDTH), f32),
        "a_ln_b": 0.01 * nrm(ks[10], (n_a, A_WIDTH), f32),
        "a_w_s": nrm(ks[11], (n_a, A_GROUPS, CHUNK, CHUNK), f32) * CHUNK ** -0.5,
        "a_b_s": 1.0 + 0.05 * nrm(ks[12], (n_a, A_GROUPS, CHUNK), f32),
        "a_w_out": nrm(ks[13], (n_a, A_WIDTH, D_MODEL), f32) * (A_WIDTH ** -0.5 * beta),
        "b_w_in": nrm(ks[14], (n_b, D_MODEL, 2 * B_WIDTH + 2 * KV_WIDTH), f32) * D_MODEL ** -0.5,
        "b_sink": 0.5 * nrm(ks[15], (n_b, N_HEADS), f32),
        "b_w_out": nrm(ks[16], (n_b, B_WIDTH, D_MODEL), f32) * (B_WIDTH ** -0.5 * beta),
    }


def reference(x, c, ctx, c_ctx, ada_w, ada_b, ln_g, ln_b,
              a_w_in, a_ln_g, a_ln_b, a_w_s, a_b_s, a_w_out,
              b_w_in, b_sink, b_w_out):
    alpha = (2.0 * DEPTH) ** 0.25
    length = x.shape[1]
    ang_row, ang_col = axial_rope_angles(length)
    cond = jax.nn.silu(c)
    cond_ctx = jax.nn.silu(c_ctx)
    for i in range(DEPTH):
        j = i // N_MIXERS
        need_ctx_out = i < DEPTH - 1
        shift, scale, gate = jnp.split(cond @ ada_w[i] + ada_b[i], 3, axis=-1)
        shift_c, scale_c, gate_c = jnp.split(cond_ctx @ ada_w[i] + ada_b[i], 3, axis=-1)
        h = x * (1.0 + scale[:, None, :]) + shift[:, None, :]
        hc = ctx * (1.0 + scale_c) + shift_c
        if i % N_MIXERS == 0:
            out = mixer_a(h, a_w_in[j], a_ln_g[j], a_ln_b[j], a_w_s[j], a_b_s[j], a_w_out[j])
            out_c = (mixer_a(hc, a_w_in[j], a_ln_g[j], a_ln_b[j], a_w_s[j], a_b_s[j], a_w_out[j])
                     if need_ctx_out else None)
        else:
            out, out_c = mixer_b(h, hc, b_w_in[j], b_sink[j], b_w_out[j], ang_row, ang_col, need_ctx_out)
        x = layer_norm(alpha * x + gate[:, None, :] * out, ln_g[i], ln_b[i])
        if need_ctx_out:
            ctx = layer_norm(alpha * ctx + gate_c * out_c, ln_g[i], ln_b[i])
    return x
```

```python
import numpy as np
from contextlib import ExitStack
import concourse.bass as bass
import concourse.mybir as mybir
from concourse.bass_utils import run_bass_kernel_spmd

F32 = mybir.dt.float32
BF16 = mybir.dt.bfloat16
AF = mybir.ActivationFunctionType
ALU = mybir.AluOpType

P = 128
D = 1024
KD = 8
NLAT = 18
NT = 20
CTX0 = 18
DEPTH = 4
ALPHA = (2.0 * DEPTH) ** 0.25
LN_EPS = 1e-5
GELU_C = 0.7978845608028654
NRING = 4


class Op:
    __slots__ = ("eng", "fn", "deps", "sig", "sem", "val", "dkey", "idx", "raw")

    def __init__(self, eng, fn, dkey=None):
        self.eng = eng
        self.fn = fn
        self.deps = []
        self.sig = False
        self.sem = None
        self.val = None
        self.dkey = dkey
        self.raw = set()


class Prog:
    ENGS = ("pe", "act", "dve", "pool", "sp")

    def __init__(self, nc):
        self.nc = nc
        self.streams = {e: [] for e in self.ENGS}
        self.last_w = {}
        self.readers = {}
        self.final_waits = []

    def add(self, eng, fn, reads=(), writes=(), dkey=None):
        op = Op(eng, fn, dkey)
        deps = []
        rawset = set()
        for r in reads:
            w = self.last_w.get(r)
            if w is not None:
                deps.append(w)
                rawset.add(id(w))
        for w_ in writes:
            lw = self.last_w.get(w_)
            if lw is not None:
                deps.append(lw)
            deps.extend(self.readers.get(w_, ()))
        seen = set()
        for d in deps:
            if id(d) in seen or d is op:
                continue
            seen.add(id(d))
            if d.eng == eng and d.dkey is None and dkey is None:
                if eng == "pe":
                    continue
            op.deps.append(d)
            d.sig = True
        for r in reads:
            self.readers.setdefault(r, []).append(op)
        for w_ in writes:
            self.last_w[w_] = op
            self.readers[w_] = []
        self.streams[eng].append(op)
        return op

    def barrier(self, marker_fn):
        last = {}
        for e in ("pe", "act", "dve"):
            s = [o for o in self.streams[e] if o.fn is not None]
            if s:
                last[e] = s[-1]
        for e in ("pe", "act", "dve"):
            op = Op(e, None)
            for e2, l in last.items():
                if e2 != e:
                    op.deps.append(l)
                    l.sig = True
            self.streams[e].append(op)
        self.add("dve", marker_fn, writes=["BAR"])

    def emit(self):
        nc = self.nc
        with ExitStack() as st:
            sems = {}

            def getsem(key):
                if key not in sems:
                    sems[key] = st.enter_context(nc.semaphore("s%d" % len(sems)))
                return sems[key]

            cnt = {}
            for e in self.ENGS:
                for op in self.streams[e]:
                    if op.fn is None:
                        continue
                    if op.dkey is not None:
                        op.sig = True
                        k = ("dma", op.dkey)
                        cnt[k] = cnt.get(k, 0) + 16
                        op.sem = getsem(k)
                        op.val = cnt[k]
                    elif op.sig:
                        k = ("eng", e)
                        cnt[k] = cnt.get(k, 0) + 1
                        op.sem = getsem(k)
                        op.val = cnt[k]
            fin = [(o.sem, o.val) for o in self.final_waits]
            block = st.enter_context(nc.Block())

            def run(ename, eng):
                waited = {}
                for op in self.streams[ename]:
                    need = {}
                    for d in op.deps:
                        k = id(d.sem)
                        if waited.get(k, 0) >= d.val:
                            continue
                        if k not in need or need[k][1] < d.val:
                            need[k] = (d.sem, d.val)
                    for k, (s, v) in need.items():
                        eng.wait_ge(s, v)
                        waited[k] = v
                    if op.fn is None:
                        continue
                    ins = op.fn(eng)
                    if op.sig:
                        ins.then_inc(op.sem, 16 if op.dkey is not None else 1)
                if ename == "sp":
                    best = {}
                    for s, v in fin:
                        if id(s) not in best or best[id(s)][1] < v:
                            best[id(s)] = (s, v)
                    for s, v in best.values():
                        eng.wait_ge(s, v)

            @block.tensor
            def _(e):
                run("pe", e)

            @block.scalar
            def _(e):
                run("act", e)

            @block.vector
            def _(e):
                run("dve", e)

            @block.gpsimd
            def _(e):
                run("pool", e)

            @block.sync
            def _(e):
                run("sp", e)


class Builder:
    def __init__(self, layers, dump_all=False, stage=99):
        self.stage = stage
        self.layers = list(layers)
        self.dump_all = dump_all
        self.nc = bass.Bass("TRN2", target_bir_lowering=False)
        self.pg = Prog(self.nc)
        self.st = ExitStack()
        self.ring_n = 0
        self.ring_pending = []
        self.psn = 0
        self.pstn = 0

    def sb(self, name, shape, dt):
        return self.st.enter_context(self.nc.sbuf_tensor(name, shape, dt))

    def dram(self, name, shape, dt=F32, kind="ExternalInput"):
        return self.nc.dram_tensor(name, shape, dt, kind=kind).ap()

    def psbank(self):
        i = self.psn % len(self.psf)
        self.psn += 1
        return self.psf[i], ("ps", i)

    def pstbank(self):
        i = self.pstn % len(self.pst)
        self.pstn += 1
        return self.pst[i], ("pst", i)

    def wload(self, src_ap):
        slot = self.ring_n % NRING
        self.ring_n += 1
        dst = self.ring[slot]
        key = ("ring", slot)
        self.pg.add("pool", lambda e, d=dst, s=src_ap: e.dma_start(out=d[:], in_=s),
                    writes=[key], dkey=key)
        return dst, key

    def build(self):
        nc, pg = self.nc, self.pg
        a = self.dram
        self.d_x = a("xin", [NLAT * P, D])
        self.d_ctx = a("ctxin", [2 * P, D])
        self.d_cvec = a("cvec", [P, KD, 2])
        self.d_adaw = a("ada_w", [DEPTH, D, 3 * D])
        self.d_adab_col = a("adab_col", [DEPTH, P, 16])
        self.d_adab_gate = a("adab_gate", [DEPTH, D])
        self.d_lng = a("ln_g", [DEPTH, D])
        self.d_lnb = a("ln_b", [DEPTH, D])
        self.d_awin = a("a_w_in", [2, D, 6 * D])
        self.d_alng = a("a_lng_col", [2, P, 16])
        self.d_alnb = a("a_lnb_col", [2, P, 16])
        self.d_awsT = a("a_wsT", [2, P, 8, P])
        self.d_abs = a("a_bs", [2, 8 * P])
        self.d_awout = a("a_w_out", [2, 2 * D, D])
        self.d_bwin = a("b_w_in", [2, D, 2560])
        self.d_bsink = a("b_sink", [2, 16])
        self.d_bwout = a("b_w_out", [2, D, D])
        self.d_cos = a("rope_cos", [P, NT * P])
        self.d_sin = a("rope_sin", [P, NT * P])
        self.d_ident = a("ident", [P, P])
        self.d_perm = a("perm", [P, P])
        self.d_mask = a("masks", [P, 2, 512])
        self.d_y = a("yout", [16 * P, D], kind="ExternalOutput")
        if self.dump_all:
            self.d_xall = a("xall", [NT * P, D], kind="ExternalOutput")

        self.x = self.sb("x", [P, NT, D], F32)
        self.ring = [self.sb("ring%d" % i, [P, KD, 512], BF16) for i in range(NRING)]
        self.gate_bc = self.sb("gate_bc", [P, 2, D], F32)
        self.gln_bc = self.sb("gln_bc", [P, D], F32)
        self.bln_bc = self.sb("bln_bc", [P, D], F32)
        self.ident = self.sb("identb", [P, P], BF16)
        self.ones = self.sb("ones", [P, P], BF16)
        self.adab = self.sb("adab", [P, 16], F32)
        self.adatmp = self.sb("adatmp", [P, 256], F32)
        self.permb = self.sb("permb", [P, P], BF16)
        self.osbt = self.sb("osbt", [P, 272], F32)
        self.maskb = self.sb("maskb", [P, 2, 512], BF16)
        self.cond2 = self.sb("cond2", [P, KD, 2], BF16)
        self.cond_bc = self.sb("cond_bc", [P, KD, 2, P], BF16)
        self.modcol = self.sb("modcol", [P, 2, KD, 2], F32)
        self.small = self.sb("small", [P, 64], F32)
        self.stats = self.sb("stats", [P, 4, 6], F32)
        self.mv = self.sb("mv", [P, 4], F32)
        self.arena = self.sb("arena", [P, 33 * 1024], BF16)
        self.psf = [self.st.enter_context(nc.psum_tensor("psf%d" % i, [P, 512], F32)) for i in range(6)]
        self.pst = [self.st.enter_context(nc.psum_tensor("pst%d" % i, [P, 1024], BF16)) for i in range(2)]

        self.prologue()
        for li, l in enumerate(self.layers):
            if li > 0:
                pg.barrier(lambda e: e.memset(self.small[:, 63:64], 0.0))
            if l % 2 == 0:
                self.layer_a(l)
            else:
                self.layer_b(l)
        self.epilogue()
        pg.emit()
        self.st.close()
        return nc

    def prologue(self):
        pg = self.pg
        x = self.x
        xin = self.d_x.rearrange("(t p) d -> p t d", p=P)
        for b in range(5):
            t0 = b * 4
            if b < 4:
                src = xin[:, t0:t0 + 4, :]
                dst = x[:, t0:t0 + 4, :]
                pg.add("sp", lambda e, s=src, d=dst: e.dma_start(out=d, in_=s),
                       writes=[("x", t) for t in range(t0, t0 + 4)], dkey=("xin", b))
            else:
                src = xin[:, 16:18, :]
                dst = x[:, 16:18, :]
                pg.add("sp", lambda e, s=src, d=dst: e.dma_start(out=d, in_=s),
                       writes=[("x", 16), ("x", 17)], dkey=("xin", b))
                src2 = self.d_ctx.rearrange("(t p) d -> p t d", p=P)
                dst2 = x[:, 18:20, :]
                pg.add("sp", lambda e, s=src2, d=dst2: e.dma_start(out=d, in_=s),
                       writes=[("x", 18), ("x", 19)], dkey=("xin", 5))
        idf = self.small
        self.identf = self.sb("identf", [P, P], F32)
        pg.add("sp", lambda e: e.dma_start(out=self.identf[:], in_=self.d_ident), writes=["identf"], dkey="c0")
        pg.add("dve", lambda e: e.tensor_copy(out=self.ident[:], in_=self.identf[:]), reads=["identf"], writes=["ident"])
        pg.add("dve", lambda e: e.memset(self.ones[:], 1.0), writes=["ones"])
        pg.add("pool", lambda e: e.dma_start(out=self.permb[:], in_=self.d_perm), writes=["permb"], dkey="permb")
        pg.add("pool", lambda e: e.dma_start(out=self.maskb[:], in_=self.d_mask), writes=["maskb"], dkey="maskb")
        self.cv = self.sb("cv", [P, KD, 2], F32)
        self.cv2 = self.sb("cv2", [P, KD, 2], F32)
        pg.add("sp", lambda e: e.dma_start(out=self.cv[:], in_=self.d_cvec), writes=["cv"], dkey="c1")
        pg.add("act", lambda e: e.activation(out=self.cv2[:], in_=self.cv[:], func=AF.Tanh, scale=0.5),
               reads=["cv"], writes=["cv2"])
        pg.add("dve", lambda e: e.scalar_tensor_tensor(out=self.cv2[:], in0=self.cv2[:], scalar=1.0, in1=self.cv[:],
                                                       op0=ALU.add, op1=ALU.mult),
               reads=["cv", "cv2"], writes=["cv2"])
        pg.add("dve", lambda e: e.tensor_scalar(out=self.cond2[:], in0=self.cv2[:], scalar1=0.5, scalar2=None,
                                                op0=ALU.mult),
               reads=["cv2"], writes=["cond2"])
        pg.add("dve", lambda e: e.tensor_copy(out=self.cond_bc[:],
                                              in_=self.cond2[:].unsqueeze(3).to_broadcast([P, KD, 2, P])),
               reads=["cond2"], writes=["cond_bc"])

    def ada(self, l):
        pg = self.pg
        wv = self.d_adaw[l].rearrange("(k p) c -> p k c", p=P)
        adab = self.adab
        pg.add("sp", lambda e: e.dma_start(out=adab[:], in_=self.d_adab_col[l]), writes=["adab"], dkey="adab")
        for v in range(2):
            pg.add("sp", lambda e, v=v: e.dma_start(out=self.gate_bc[:, v, :],
                                                    in_=self.d_adab_gate[l].partition_broadcast(P)),
                   writes=[("gate", v)], dkey=("gate", v))
        pg.add("sp", lambda e: e.dma_start(out=self.gln_bc[:], in_=self.d_lng[l].partition_broadcast(P)),
               writes=["gln"], dkey="gln")
        pg.add("sp", lambda e: e.dma_start(out=self.bln_bc[:], in_=self.d_lnb[l].partition_broadcast(P)),
               writes=["bln"], dkey="bln")
        for pc in range(6):
            if self.stage in (1, 1.25) and pc >= 4:
                break
            if self.stage == 1.5 and pc < 4:
                continue
            w, wk = self.wload(wv[:, :, pc * 512:(pc + 1) * 512])
            if pc < 4:
                ps, pk = self.psbank()
                kind = pc // 2

                def mm(e, w=w, ps=ps):
                    ins = None
                    for ch in range(4):
                        for k in range(KD):
                            ins = e.matmul(ps[:, ch * 64:(ch + 1) * 64], lhsT=w[:, k, ch * P:(ch + 1) * P],
                                           rhs=self.cond_bc[:, k, :, 0:32], start=(k == 0), stop=(k == KD - 1))
                    return ins
                pg.add("pe", mm, reads=[wk, "cond_bc"], writes=[pk])
                if self.stage == 1.25:
                    continue
                c0 = (pc % 2) * 4
                tmp = self.adatmp
                pg.add("dve", lambda e, ps=ps, tmp=tmp: e.tensor_copy(out=tmp[:], in_=ps[:, 0:256]),
                       reads=[pk], writes=["adatmp"])
                tv = tmp[:].rearrange("p (c v r) -> p c v r", c=4, v=2)
                for v in range(2):
                    dst = self.modcol[:, kind, c0:c0 + 4, v]
                    src = tv[:, :, v, 0]
                    bias = self.adab[:, kind * 8 + c0: kind * 8 + c0 + 4]
                    if kind == 0:
                        pg.add("dve", lambda e, dst=dst, src=src, bias=bias: e.tensor_tensor(out=dst, in0=src, in1=bias, op=ALU.add),
                               reads=["adatmp", "adab"], writes=[("modcol", pc)])
                    else:
                        pg.add("dve", lambda e, dst=dst, src=src, bias=bias: e.scalar_tensor_tensor(
                            out=dst, in0=src, scalar=1.0, in1=bias, op0=ALU.add, op1=ALU.add),
                            reads=["adatmp", "adab"], writes=[("modcol", pc)])
            else:
                half = pc - 4
                for v in range(2):
                    ps, pk = self.psbank()

                    def mm(e, w=w, ps=ps, v=v):
                        ins = None
                        for k in range(KD):
                            ins = e.matmul(ps[:, :], lhsT=self.cond_bc[:, k, v, :], rhs=w[:, k, :],
                                           start=(k == 0), stop=(k == KD - 1))
                        return ins
                    pg.add("pe", mm, reads=[wk, "cond_bc"], writes=[pk])
                    dst = self.gate_bc[:, v, half * 512:(half + 1) * 512]
                    pg.add("dve", lambda e, dst=dst, ps=ps: e.tensor_tensor(out=dst, in0=ps[:, :], in1=dst, op=ALU.add),
                           reads=[pk, ("gate", v)], writes=[("gate", v)])

    def make_hT(self, tiles, hT, hkey):
        pg = self.pg
        for slot, t in enumerate(tiles):
            v = 1 if t >= CTX0 else 0
            xb = self.xb[slot % self.nbuf]
            xk = ("xb", slot % self.nbuf)
            pg.add("act", lambda e, xb=xb, t=t: e.copy(out=xb[:], in_=self.x[:, t, :]),
                   reads=[("x", t)], writes=[xk])
            pst, ptk = self.pstbank()

            def tr(e, xb=xb, pst=pst):
                ins = None
                for k in range(KD):
                    ins = e.transpose(pst[:, k * P:(k + 1) * P], xb[:, k * P:(k + 1) * P], self.ident[:])
                return ins
            pg.add("pe", tr, reads=[xk, "ident"], writes=[ptk])
            for k in range(KD):
                dst = hT[:, k, slot * P:(slot + 1) * P]
                src = pst[:, k * P:(k + 1) * P]
                sc = self.modcol[:, 1, k, v:v + 1]
                sh = self.modcol[:, 0, k, v:v + 1]
                pg.add("dve", lambda e, dst=dst, src=src, sc=sc, sh=sh: e.tensor_scalar(
                    out=dst, in0=src, scalar1=sc, scalar2=sh, op0=ALU.mult, op1=ALU.add),
                    reads=[ptk] + [("modcol", i) for i in range(4)], writes=[(hkey, slot)])

    def tail(self, t, pieces, nk, lhs_fn, lhs_reads):
        pg = self.pg
        v = 1 if t >= CTX0 else 0
        r = self.rbuf[self.rn % self.nbuf]
        rk = ("r", self.rn % self.nbuf)
        self.rn += 1
        for half in range(2):
            ps, pk = self.psbank()
            plist = pieces[half]

            def mm(e, ps=ps, plist=plist):
                ins = None
                n = len(plist) * KD
                i = 0
                for (w, _) in plist:
                    for k in range(KD):
                        ins = e.matmul(ps[:, :], lhsT=lhs_fn(i), rhs=w[:, k, :], start=(i == 0), stop=(i == n - 1))
                        i += 1
                return ins
            pg.add("pe", mm, reads=[k_ for (_, k_) in plist] + lhs_reads, writes=[pk])
            sl = slice(half * 512, (half + 1) * 512)
            pg.add("dve", lambda e, ps=ps, sl=sl, r=r, v=v: e.tensor_tensor(
                out=r[:, sl], in0=ps[:, :], in1=self.gate_bc[:, v, sl], op=ALU.mult),
                reads=[pk, ("gate", v)], writes=[rk])
            pg.add("dve", lambda e, sl=sl, r=r, t=t: e.scalar_tensor_tensor(
                out=r[:, sl], in0=self.x[:, t, sl], scalar=ALPHA, in1=r[:, sl], op0=ALU.mult, op1=ALU.add),
                reads=[rk, ("x", t)], writes=[rk])
        self.ln_stats(r, rk, 2)
        pg.add("act", lambda e, r=r: e.activation(out=r[:], in_=r[:], func=AF.Identity,
                                                  bias=self.mv[:, 3:4], scale=self.mv[:, 2:3]),
               reads=[rk, "mv"], writes=[rk])
        pg.add("dve", lambda e, r=r: e.tensor_tensor(out=r[:], in0=r[:], in1=self.gln_bc[:], op=ALU.mult),
               reads=[rk, "gln"], writes=[rk])
        pg.add("dve", lambda e, r=r, t=t: e.tensor_tensor(out=self.x[:, t, :], in0=r[:], in1=self.bln_bc[:], op=ALU.add),
               reads=[rk, "bln"], writes=[("x", t)])

    def ln_stats(self, buf, key, nchunks):
        pg = self.pg
        for c in range(nchunks):
            pg.add("dve", lambda e, c=c: e.bn_stats(out=self.stats[:, c, :], in_=buf[:, c * 512:(c + 1) * 512]),
                   reads=[key], writes=[("stats", c)])
        pg.add("dve", lambda e: e.bn_aggr(out=self.mv[:, 0:2], in_=self.stats[:, 0:nchunks, :]),
               reads=[("stats", c) for c in range(nchunks)], writes=["mv01"])
        pg.add("dve", lambda e: e.tensor_scalar(out=self.mv[:, 2:3], in0=self.mv[:, 1:2], scalar1=LN_EPS, scalar2=None,
                                                op0=ALU.add),
               reads=["mv01"], writes=["mv2"])
        pg.add("act", lambda e: e.activation(out=self.mv[:, 2:3], in_=self.mv[:, 2:3], func=AF.Sqrt),
               reads=["mv2"], writes=["mv2"])
        pg.add("dve", lambda e: e.reciprocal(out=self.mv[:, 2:3], in_=self.mv[:, 2:3]),
               reads=["mv2"], writes=["mv2"])
        pg.add("dve", lambda e: e.scalar_tensor_tensor(out=self.mv[:, 3:4], in0=self.mv[:, 0:1], scalar=-1.0,
                                                       in1=self.mv[:, 2:3], op0=ALU.mult, op1=ALU.mult),
               reads=["mv01", "mv2"], writes=["mv"])

    def layer_a(self, l):
        pg = self.pg
        j = l // 2
        ar = self.arena
        o = 0

        def carve(n_bf16):
            nonlocal o
            v = ar[:, o:o + n_bf16]
            o += n_bf16
            return v
        hT = carve(KD * 512).rearrange("p (k n) -> p k n", k=KD)
        vy = carve(16 * 512)
        vn = vy.rearrange("p (c s f) -> p s c f", c=16, s=4)
        yT = vy.rearrange("p (c n) -> p c n", c=16)
        gv = carve(2 * 2048).bitcast(F32)
        cs = [[carve(2 * 512).bitcast(F32) for _ in range(3)] for _ in range(2)]
        self.rbuf = [carve(2 * 1024).bitcast(F32) for _ in range(2)]
        self.xb = [carve(1024) for _ in range(2)]
        E = carve(2 * 16 * P).bitcast(F32).rearrange("p (c n) -> p c n", c=16)
        wsT = carve(8 * P).rearrange("p (g n) -> p g n", g=8)
        wsf = ar[:, 14336:14336 + 2 * 8 * P].bitcast(F32).rearrange("p (g n) -> p g n", g=8)
        bsb = gv[:, 0:1024].rearrange("p (g n) -> p g n", g=8)
        self.rn = 0
        self.nbuf = 2
        lncol = self.small[:, 0:32]

        if self.stage < 1:
            return
        self.ada(l)
        if self.stage < 2:
            return
        pg.add("sp", lambda e: e.dma_start(out=lncol[:, 0:16], in_=self.d_alng[j]), writes=["lncol_g"], dkey="lncg")
        pg.add("sp", lambda e: e.dma_start(out=lncol[:, 16:32], in_=self.d_alnb[j]), writes=["lncol_b"], dkey="lncb")
        pg.add("sp", lambda e: e.dma_start(out=wsf, in_=self.d_awsT[j]), reads=["BAR"], writes=["wsf", "gv"], dkey="wsf")
        pg.add("sp", lambda e: e.dma_start(out=bsb, in_=self.d_abs[j].partition_broadcast(P).rearrange("p (g n) -> p g n", g=8)),
               reads=["BAR"], writes=["bsb", "gv"], dkey="bsb")
        pg.add("dve", lambda e: e.tensor_copy(out=wsT, in_=wsf), reads=["wsf"], writes=["wsT"])
        ones = self.ones
        for hh in range(2):
            ps, pk = self.psbank()

            def mm(e, ps=ps, hh=hh):
                ins = None
                for g in range(4):
                    ins = e.matmul(ps[:, g * P:(g + 1) * P], lhsT=ones[:], rhs=wsT[:, hh * 4 + g, :], start=True, stop=True)
                return ins
            pg.add("pe", mm, reads=["wsT", "ones"], writes=[pk])
            for g in range(4):
                for c2 in range(2):
                    cc = (hh * 4 + g) * 2 + c2
                    pg.add("dve", lambda e, ps=ps, g=g, cc=cc, gg=hh * 4 + g: e.scalar_tensor_tensor(
                        out=E[:, cc, :], in0=ps[:, g * P:(g + 1) * P], scalar=lncol[:, 16 + cc:17 + cc],
                        in1=bsb[:, gg, :], op0=ALU.mult, op1=ALU.add),
                        reads=[pk, "lncol_b", "bsb"], writes=["E"])

        if self.stage < 3:
            return
        win = self.d_awin[j].rearrange("(k p) c -> p k c", p=P)
        wout = self.d_awout[j].rearrange("(k p) c -> p k c", p=P)
        tiles_all = list(range(NLAT if l == 0 else NLAT - 1)) + [CTX0, CTX0 + 1]
        blocks = [tiles_all[i:i + 4] for i in range(0, len(tiles_all), 4)]
        for tiles in blocks:
            nb = len(tiles)
            ntok = nb * P
            self.make_hT(tiles, hT, "hT")
            if self.stage < 4:
                return
            hreads = [("hT", s) for s in range(nb)]
            wv = [self.wload(win[:, :, 2048 + pv * 512: 2048 + (pv + 1) * 512]) for pv in range(4)]
            for slot, t in enumerate(tiles):
                for pv in range(4):
                    w, wk = wv[pv]
                    ps, pk = self.psbank()

                    def mm(e, ps=ps, w=w, slot=slot):
                        ins = None
                        for k in range(KD):
                            ins = e.matmul(ps[:, :], lhsT=hT[:, k, slot * P:(slot + 1) * P], rhs=w[:, k, :],
                                           start=(k == 0), stop=(k == KD - 1))
                        return ins
                    pg.add("pe", mm, reads=[wk, ("hT", slot)], writes=[pk])
                    pg.add("act", lambda e, ps=ps, pv=pv: e.activation(out=gv[:, pv * 512:(pv + 1) * 512], in_=ps[:, :],
                                                                       func=AF.Gelu_apprx_tanh),
                           reads=[pk], writes=["gv", "bsb", "wsf"])
                self.ln_stats(gv, "gv", 4)
                pg.add("act", lambda e, slot=slot: e.activation(
                    out=vn[:, slot, :, :], in_=gv[:, :].rearrange("p (c f) -> p c f", c=16), func=AF.Identity,
                    bias=self.mv[:, 3:4], scale=self.mv[:, 2:3]),
                    reads=["gv", "mv"], writes=[("vy", cc) for cc in range(16)])
            if self.stage < 5:
                return
            for pu in range(4):
                wu, wuk = self.wload(win[:, :, pu * 512:(pu + 1) * 512])
                wz, wzk = self.wload(win[:, :, 4096 + pu * 512: 4096 + (pu + 1) * 512])
                for c4 in range(4):
                    cc = pu * 4 + c4
                    g = cc // 2
                    A, B, C = cs[cc % 2]
                    ka, kb, kc = ("csA", cc % 2), ("csB", cc % 2), ("csC", cc % 2)
                    ps_s, pks = self.psbank()

                    def mms(e, ps_s=ps_s, cc=cc, g=g, nb=nb):
                        ins = None
                        for s in range(nb):
                            ins = e.matmul(ps_s[:, s * P:(s + 1) * P], lhsT=vn[:, s, cc, :], rhs=wsT[:, g, :],
                                           start=True, stop=True)
                        return ins
                    pg.add("pe", mms, reads=[("vy", cc), "wsT"], writes=[pks])
                    ps_u, pku = self.psbank()

                    def mmu(e, ps=ps_u, w=wu, c4=c4, ntok=ntok):
                        ins = None
                        for k in range(KD):
                            ins = e.matmul(ps[:, 0:ntok], lhsT=w[:, k, c4 * P:(c4 + 1) * P], rhs=hT[:, k, 0:ntok],
                                           start=(k == 0), stop=(k == KD - 1))
                        return ins
                    pg.add("pe", mmu, reads=[wuk] + hreads, writes=[pku])
                    ps_z, pkz = self.psbank()
                    pg.add("pe", lambda e, ps=ps_z, w=wz, c4=c4, ntok=ntok: mmu(e, ps, w, c4, ntok),
                           reads=[wzk] + hreads, writes=[pkz])
                    pg.add("act", lambda e, ps=ps_u, A=A, ntok=ntok: e.activation(out=A[:, 0:ntok], in_=ps[:, 0:ntok],
                                                                                 func=AF.Gelu_apprx_tanh),
                           reads=[pku], writes=[ka])
                    pg.add("act", lambda e, ps=ps_z, B=B, ntok=ntok: e.activation(out=B[:, 0:ntok], in_=ps[:, 0:ntok],
                                                                                 func=AF.Tanh, scale=0.5),
                           reads=[pkz], writes=[kb])
                    pg.add("dve", lambda e, ps=ps_s, C=C, cc=cc, nb=nb: e.scalar_tensor_tensor(
                        out=C[:, 0:nb * P].rearrange("p (s n) -> p s n", s=nb),
                        in0=ps[:, 0:nb * P].rearrange("p (s n) -> p s n", s=nb),
                        scalar=lncol[:, cc:cc + 1],
                        in1=E[:, cc, :].unsqueeze(1).to_broadcast([P, nb, P]), op0=ALU.mult, op1=ALU.add),
                        reads=[pks, "E", "lncol_g"], writes=[kc])
                    pg.add("dve", lambda e, ps=ps_z, B=B, ntok=ntok: e.scalar_tensor_tensor(
                        out=B[:, 0:ntok], in0=B[:, 0:ntok], scalar=1.0, in1=ps[:, 0:ntok], op0=ALU.add, op1=ALU.mult),
                        reads=[kb, pkz], writes=[kb])
                    pg.add("dve", lambda e, A=A, C=C, ntok=ntok: e.tensor_tensor(out=A[:, 0:ntok], in0=A[:, 0:ntok],
                                                                                in1=C[:, 0:ntok], op=ALU.mult),
                           reads=[ka, kc], writes=[ka])
                    pg.add("dve", lambda e, A=A, B=B, cc=cc, ntok=ntok: e.scalar_tensor_tensor(
                        out=yT[:, cc, 0:ntok], in0=A[:, 0:ntok], scalar=0.5, in1=B[:, 0:ntok], op0=ALU.mult, op1=ALU.mult),
                        reads=[ka, kb, ("vy", cc)], writes=[("vy", cc)])
            if self.stage < 6:
                return
            wo = [[self.wload(wout[:, kg * 8:(kg + 1) * 8, half * 512:(half + 1) * 512]) for kg in range(2)]
                  for half in range(2)]
            for slot, t in enumerate(tiles):
                self.tail(t, wo, 16, lambda i, slot=slot: yT[:, i, slot * P:(slot + 1) * P],
                          [("vy", cc) for cc in range(16)])

    def layer_b(self, l):
        pg = self.pg
        j = l // 2
        ar = self.arena
        o = 0

        def carve(n_bf16):
            nonlocal o
            v = ar[:, o:o + n_bf16]
            o += n_bf16
            return v
        hT = carve(KD * 512).rearrange("p (k n) -> p k n", k=KD)
        KT = carve(2 * NT * P).rearrange("p (c n) -> p c n", c=2)
        Vflat = carve(NT * 4 * 68)
        Va = Vflat.rearrange("p (t h d) -> p t h d", t=NT, h=4)
        QT = carve(KD * 512).rearrange("p (c n) -> p c n", c=8)
        szT = carve(KD * 512).rearrange("p (c n) -> p c n", c=8)
        PT0 = carve(5 * 512).rearrange("p (j n) -> p j n", j=5)
        PTs = [PT0, ar[:, 0:5 * 512].rearrange("p (j n) -> p j n", j=5)]
        self.ptn = 0
        onyt = carve(2048)
        on = onyt[:, 0:1024]
        yT = onyt[:, 1024:2048].rearrange("p (c n) -> p c n", c=8)
        zt = onyt.bitcast(F32)[:, 0:512]
        self.rbuf = [carve(2 * 1024).bitcast(F32)] * 2
        RA = self.rbuf[0][:, 0:512]
        RB = self.rbuf[0][:, 512:1024]
        self.xb = [carve(1024)] * 2
        rope = carve(2 * 1024).bitcast(F32)
        cosb = rope[:, 0:512]
        sinb = rope[:, 512:1024]
        qsb = [carve(512) for _ in range(2)]
        osb = self.osbt
        self.rn = 0
        self.nbuf = 1
        esink = self.small[:, 32:48]
        den = self.small[:, 48:52]

        if self.stage < 1:
            return
        self.ada(l)
        pg.add("sp", lambda e: e.dma_start(out=esink, in_=self.d_bsink[j].partition_broadcast(P)),
               writes=["esink"], dkey="esink")
        pg.add("act", lambda e: e.activation(out=esink, in_=esink, func=AF.Exp), reads=["esink"], writes=["esink"])
        pg.add("dve", lambda e: e.memset(Vflat, 1.0), reads=["BAR"], writes=[("V", t) for t in range(NT)])

        if self.stage < 2:
            return
        win = self.d_bwin[j].rearrange("(k p) c -> p k c", p=P)
        wout = self.d_bwout[j].rearrange("(k p) c -> p k c", p=P)
        tmax = 17 if l == 1 else 16
        kv_tiles = list(range(tmax + 1)) + [CTX0, CTX0 + 1]
        q_tiles = (list(range(17)) + [CTX0, CTX0 + 1]) if l == 1 else list(range(16))
        self.qn = 0

        def load_rope(tiles):
            for slot, t in enumerate(tiles):
                pg.add("sp", lambda e, slot=slot, t=t: e.dma_start(out=cosb[:, slot * P:(slot + 1) * P],
                                                                 in_=self.d_cos[:, t * P:(t + 1) * P]),
                       reads=["BAR"], writes=["rope"], dkey="ropec")
                pg.add("sp", lambda e, slot=slot, t=t: e.dma_start(out=sinb[:, slot * P:(slot + 1) * P],
                                                                 in_=self.d_sin[:, t * P:(t + 1) * P]),
                       reads=["BAR"], writes=["rope"], dkey="ropes")

        def proj_rope(w, wk, col0, ntok, nb, dst, dkeys):
            ps, pk = self.psbank()

            def mm(e, ps=ps):
                ins = None
                for k in range(KD):
                    ins = e.matmul(ps[:, 0:ntok], lhsT=w[:, k, col0:col0 + P], rhs=hT[:, k, 0:ntok],
                                   start=(k == 0), stop=(k == KD - 1))
                return ins
            if self.stage == 2.1:
                return
            pg.add("pe", mm, reads=[wk] + [("hT", s_) for s_ in range(nb)], writes=[pk])
            qb = qsb[self.qn % 2]
            qk = ("qsb", self.qn % 2)
            self.qn += 1
            pg.add("act", lambda e, ps=ps, qb=qb: e.copy(out=qb[:, 0:ntok], in_=ps[:, 0:ntok]), reads=[pk], writes=[qk])
            ps2, pk2 = self.psbank()
            pg.add("pe", lambda e, ps2=ps2, qb=qb: e.matmul(ps2[:, 0:ntok], lhsT=self.permb[:], rhs=qb[:, 0:ntok],
                                                           start=True, stop=True),
                   reads=[qk, "permb"], writes=[pk2])
            if self.stage == 2.2:
                return
            rk = ("r", 0)
            pg.add("dve", lambda e, ps=ps: e.tensor_tensor(out=RA[:, 0:ntok], in0=ps[:, 0:ntok], in1=cosb[:, 0:ntok], op=ALU.mult),
                   reads=[pk, "rope", qk], writes=[rk])
            pg.add("dve", lambda e, ps2=ps2: e.tensor_tensor(out=RB[:, 0:ntok], in0=ps2[:, 0:ntok], in1=sinb[:, 0:ntok], op=ALU.mult),
                   reads=[pk2, "rope"], writes=[rk])
            pg.add("dve", lambda e: e.tensor_tensor(out=dst, in0=RA[:, 0:ntok], in1=RB[:, 0:ntok], op=ALU.add),
                   reads=[rk], writes=dkeys)

        for b0 in range(0, len(kv_tiles), 4):
            tiles = kv_tiles[b0:b0 + 4]
            nb = len(tiles)
            ntok = nb * P
            self.make_hT(tiles, hT, "hT")
            load_rope(tiles)
            wkv, wkvk = self.wload(win[:, :, 1024:1536])
            for p_ in range(2):
                runs = []
                s0 = 0
                for s_ in range(1, nb + 1):
                    if s_ == nb or tiles[s_] != tiles[s_ - 1] + 1:
                        runs.append((s0, s_))
                        s0 = s_
                if len(runs) == 1:
                    dst = KT[:, p_, tiles[0] * P: tiles[0] * P + ntok]
                    proj_rope(wkv, wkvk, p_ * P, ntok, nb, dst, [("KT", t) for t in tiles])
                else:
                    tmp = szT[:, p_, 0:ntok]
                    proj_rope(wkv, wkvk, p_ * P, ntok, nb, tmp, [("sz", p_)])
                    for (a_, b_) in runs:
                        pg.add("dve", lambda e, a_=a_, b_=b_, p_=p_, tiles=tiles: e.tensor_copy(
                            out=KT[:, p_, tiles[a_] * P: tiles[a_] * P + (b_ - a_) * P], in_=szT[:, p_, a_ * P:b_ * P]),
                            reads=[("sz", p_)], writes=[("KT", t) for t in tiles[a_:b_]])
            for slot, t in enumerate(tiles):
                if self.stage in (2.1, 2.2, 2.3):
                    break
                ps, pk = self.psbank()

                def mmv(e, ps=ps, slot=slot, w=wkv):
                    ins = None
                    for k in range(KD):
                        ins = e.matmul(ps[:, 0:256], lhsT=hT[:, k, slot * P:(slot + 1) * P], rhs=w[:, k, 256:512],
                                       start=(k == 0), stop=(k == KD - 1))
                    return ins
                pg.add("pe", mmv, reads=[wkvk, ("hT", slot)], writes=[pk])
                if self.stage == 2.4:
                    continue
                pg.add("act", lambda e, ps=ps, t=t: e.copy(out=Va[:, t, :, 0:64],
                                                           in_=ps[:, 0:256].rearrange("p (h d) -> p h d", h=4)),
                       reads=[pk], writes=[("V", t)])

        if self.stage < 3:
            return
        for b0 in range(0, len(q_tiles), 4):
            tiles = q_tiles[b0:b0 + 4]
            nb = len(tiles)
            ntok = nb * P
            self.make_hT(tiles, hT, "hT")
            load_rope(tiles)
            for qp in range(2):
                wq, wqk = self.wload(win[:, :, qp * 512:(qp + 1) * 512])
                for c4 in range(4):
                    qc = qp * 4 + c4
                    proj_rope(wq, wqk, c4 * P, ntok, nb, QT[:, qc, 0:ntok], [("QT", qc)])
            for zp in range(2):
                wz, wzk = self.wload(win[:, :, 1536 + zp * 512: 1536 + (zp + 1) * 512])
                for c4 in range(4):
                    zc = zp * 4 + c4
                    ps, pk = self.psbank()

                    def mmz(e, ps=ps, w=wz, c4=c4, ntok=ntok):
                        ins = None
                        for k in range(KD):
                            ins = e.matmul(ps[:, 0:ntok], lhsT=w[:, k, c4 * P:(c4 + 1) * P], rhs=hT[:, k, 0:ntok],
                                           start=(k == 0), stop=(k == KD - 1))
                        return ins
                    pg.add("pe", mmz, reads=[wzk] + [("hT", s_) for s_ in range(nb)], writes=[pk])
                    pg.add("act", lambda e, ps=ps, ntok=ntok: e.activation(out=zt[:, 0:ntok], in_=ps[:, 0:ntok], func=AF.Tanh, scale=0.5),
                           reads=[pk], writes=["on", "yT"])
                    pg.add("dve", lambda e, ps=ps, zc=zc, ntok=ntok: e.scalar_tensor_tensor(
                        out=szT[:, zc, 0:ntok], in0=zt[:, 0:ntok], scalar=1.0, in1=ps[:, 0:ntok], op0=ALU.add, op1=ALU.mult),
                        reads=[pk, "on", "yT"], writes=[("sz", zc)])
            if self.stage < 4:
                continue
            wo = [[self.wload(wout[:, :, half * 512:(half + 1) * 512])] for half in range(2)]
            for slot, t in enumerate(tiles):
                if t >= CTX0:
                    J = [(CTX0, None), (CTX0 + 1, None)]
                else:
                    J = []
                    if t >= 1:
                        J.append((t - 1, 0))
                    J.append((t, None))
                    if t + 1 <= tmax:
                        J.append((t + 1, 1))
                    J += [(CTX0, None), (CTX0 + 1, None)]
                for kh in range(4):
                    p_, e_ = kh // 2, kh % 2
                    rows = slice(e_ * 64, (e_ + 1) * 64)
                    pb = self.ptn % 2
                    self.ptn += 1
                    PT = PTs[pb]
                    ptx = [("hT", s_) for s_ in range(4)] if pb == 1 else []
                    for ji, (jt, mi) in enumerate(J):
                        ps, pk = self.psbank()
                        pg.add("pe", lambda e, ps=ps, jt=jt, p_=p_, rows=rows, slot=slot: e.matmul(
                            ps[:, :].rearrange("p (g n) -> p g n", g=4), lhsT=KT[rows, p_, jt * P:(jt + 1) * P],
                            rhs=QT[rows, p_ * 4:(p_ + 1) * 4, slot * P:(slot + 1) * P], start=True, stop=True),
                            reads=[("KT", jt)] + [("QT", p_ * 4 + g) for g in range(4)], writes=[pk])
                        pg.add("act", lambda e, ps=ps, ji=ji, PT=PT: e.activation(out=PT[:, ji, :], in_=ps[:, :], func=AF.Exp, scale=0.125),
                               reads=[pk], writes=[("PT", pb, ji)] + ptx)
                        if mi is not None:
                            pg.add("dve", lambda e, ji=ji, mi=mi, PT=PT: e.tensor_tensor(out=PT[:, ji, :], in0=PT[:, ji, :],
                                                                                 in1=self.maskb[:, mi, :], op=ALU.mult),
                                   reads=[("PT", pb, ji), "maskb"], writes=[("PT", pb, ji)] + ptx)
                    if self.stage < 5:
                        continue
                    ps_o, pko = self.psbank()

                    def mmo(e, ps_o=ps_o, J=J, kh=kh, PT=PT):
                        ins = None
                        for g in range(4):
                            for ji, (jt, mi) in enumerate(J):
                                ins = e.matmul(ps_o[:, g * 68:g * 68 + 65], lhsT=PT[:, ji, g * P:(g + 1) * P],
                                               rhs=Va[:, jt, kh, 0:65], start=(ji == 0), stop=(ji == len(J) - 1))
                        return ins
                    pg.add("pe", mmo, reads=[("PT", pb, ji) for ji in range(len(J))] + ptx + [("V", jt) for jt, _ in J], writes=[pko])
                    for g in range(4):
                        pg.add("act", lambda e, ps_o=ps_o, g=g: e.copy(out=osb[:, g * 68:g * 68 + 65], in_=ps_o[:, g * 68:g * 68 + 65]),
                               reads=[pko], writes=["osb"])
                    ov = osb[:, 0:272].rearrange("p (g d) -> p g d", g=4)
                    pg.add("dve", lambda e, kh=kh, ov=ov: e.tensor_tensor(out=den, in0=ov[:, :, 64], in1=esink[:, 4 * kh:4 * kh + 4],
                                                                         op=ALU.add),
                           reads=["osb", "esink"], writes=["den"])
                    pg.add("dve", lambda e: e.reciprocal(out=den, in_=den), reads=["den"], writes=["den"])
                    for g in range(4):
                        h = 4 * kh + g
                        pg.add("dve", lambda e, g=g, h=h, ov=ov: e.tensor_scalar(
                            out=on[:, h * 64:(h + 1) * 64], in0=ov[:, g, 0:64], scalar1=den[:, g:g + 1], scalar2=None, op0=ALU.mult),
                            reads=["osb", "den"], writes=["on"])
                if self.stage < 6:
                    continue
                pst, ptk = self.pstbank()

                def tr(e, pst=pst):
                    ins = None
                    for c in range(8):
                        ins = e.transpose(pst[:, c * P:(c + 1) * P], on[:, c * P:(c + 1) * P], self.ident[:])
                    return ins
                pg.add("pe", tr, reads=["on", "ident"], writes=[ptk])
                pg.add("dve", lambda e, pst=pst, slot=slot: e.scalar_tensor_tensor(
                    out=yT, in0=pst[:, :].rearrange("p (c n) -> p c n", c=8), scalar=0.5,
                    in1=szT[:, :, slot * P:(slot + 1) * P], op0=ALU.mult, op1=ALU.mult),
                    reads=[ptk] + [("sz", zc) for zc in range(8)], writes=["yT"])
                self.tail(t, wo, 8, lambda i: yT[:, i, :], ["yT"])

    def epilogue(self):
        pg = self.pg
        yv = self.d_y.rearrange("(t p) d -> p t d", p=P)
        for b in range(4):
            t0 = b * 4
            op = pg.add("sp", lambda e, t0=t0: e.dma_start(out=yv[:, t0:t0 + 4, :], in_=self.x[:, t0:t0 + 4, :]),
                        reads=[("x", t) for t in range(t0, t0 + 4)], writes=[("y", b)], dkey="yout")
            pg.final_waits.append(op)
        if self.dump_all:
            xv = self.d_xall.rearrange("(t p) d -> p t d", p=P)
            for b in range(5):
                t0 = b * 4
                op = pg.add("sp", lambda e, t0=t0: e.dma_start(out=xv[:, t0:t0 + 4, :], in_=self.x[:, t0:t0 + 4, :]),
                            reads=[("x", t) for t in range(t0, t0 + 4)], writes=[("xall", b)], dkey="yout")
                pg.final_waits.append(op)


def _rope_tables(gpos):
    n_freq = 16
    inv = (np.float32(10000.0) ** (-(np.arange(n_freq, dtype=np.float32) / np.float32(n_freq)))).astype(np.float32)
    row = (gpos // 64).astype(np.float32)
    col = (gpos % 64).astype(np.float32)
    cos = np.zeros((P, gpos.shape[0]), np.float32)
    sin = np.zeros((P, gpos.shape[0]), np.float32)
    for r in range(P):
        dd = r % 64
        axis = dd // 32
        jj = dd % 32
        f = jj % 16
        ang = ((row if axis == 0 else col) * inv[f]).astype(np.float32)
        cos[r] = np.cos(ang)
        sin[r] = np.sin(ang) * (-1.0 if jj < 16 else 1.0)
    return cos, sin


def _shared_consts():
    ident = np.eye(P, dtype=np.float32)
    perm = np.zeros((P, P), np.float32)
    for m in range(P):
        jj = m % 32
        sw = m + 16 if jj < 16 else m - 16
        perm[sw, m] = 1.0
    kk = np.arange(P)[:, None]
    qq = np.arange(P)[None, :]
    m_prev = (kk >= qq).astype(np.float32)
    m_next = (kk <= qq).astype(np.float32)
    masks = np.stack([np.tile(m_prev, (1, 4)), np.tile(m_next, (1, 4))], axis=1)
    return ident, perm, np.ascontiguousarray(masks)


def prep_inputs(inputs):
    f = lambda a: np.ascontiguousarray(np.asarray(a, dtype=np.float32))
    x = f(inputs["x"]); c = f(inputs["c"]); ctx = f(inputs["ctx"]); c_ctx = f(inputs["c_ctx"])
    ada_w = f(inputs["ada_w"]); ada_b = f(inputs["ada_b"])
    ln_g = f(inputs["ln_g"]); ln_b = f(inputs["ln_b"])
    a_w_in = f(inputs["a_w_in"]); a_ln_g = f(inputs["a_ln_g"]); a_ln_b = f(inputs["a_ln_b"])
    a_w_s = f(inputs["a_w_s"]); a_b_s = f(inputs["a_b_s"]); a_w_out = f(inputs["a_w_out"])
    b_w_in = f(inputs["b_w_in"]); b_sink = f(inputs["b_sink"]); b_w_out = f(inputs["b_w_out"])
    ident, perm, masks = _shared_consts()
    qcols = []
    for p_ in range(2):
        for g in range(4):
            qcols += list(range((8 * p_ + g) * 64, (8 * p_ + g) * 64 + 64))
            qcols += list(range((8 * p_ + 4 + g) * 64, (8 * p_ + 4 + g) * 64 + 64))
    cols = np.array(qcols + list(range(1024, 2560)))
    b_w_in_p = np.ascontiguousarray(b_w_in[:, :, cols])
    adab_col = np.ascontiguousarray(ada_b[:, :2048].reshape(DEPTH, 16, P).transpose(0, 2, 1))
    adab_gate = np.ascontiguousarray(ada_b[:, 2048:])
    a_lng_col = np.ascontiguousarray(a_ln_g.reshape(2, 16, P).transpose(0, 2, 1))
    a_lnb_col = np.ascontiguousarray(a_ln_b.reshape(2, 16, P).transpose(0, 2, 1))
    shared = dict(ada_w=ada_w, adab_col=adab_col, adab_gate=adab_gate, ln_g=ln_g, ln_b=ln_b,
                  a_w_in=a_w_in, a_lng_col=a_lng_col, a_lnb_col=a_lnb_col, a_w_out=a_w_out,
                  b_w_in=b_w_in_p, b_sink=b_sink, b_w_out=b_w_out, ident=ident, perm=perm, masks=masks)
    maps = []
    for core in range(8):
        b = core // 2
        half = core % 2
        if half == 0:
            xl = x[b, 0:NLAT * P]
            gpos = np.arange(NLAT * P)
            cl = ctx[b]
            ws = a_w_s
            bs = a_b_s
        else:
            xl = x[b, ::-1][0:NLAT * P]
            gpos = 4095 - np.arange(NLAT * P)
            cl = ctx[b, ::-1]
            ws = a_w_s[:, :, ::-1, ::-1]
            bs = a_b_s[:, :, ::-1]
        cos, sin = _rope_tables(gpos)
        cos = np.ascontiguousarray(np.concatenate([cos, np.ones((P, 2 * P), np.float32)], axis=1))
        sin = np.ascontiguousarray(np.concatenate([sin, np.zeros((P, 2 * P), np.float32)], axis=1))
        cvec = np.stack([c[b].reshape(KD, P).T, c_ctx.reshape(KD, P).T], axis=2)
        m = dict(shared)
        m.update(xin=np.ascontiguousarray(xl), ctxin=np.ascontiguousarray(cl), cvec=np.ascontiguousarray(cvec),
                 a_wsT=np.ascontiguousarray(ws.transpose(0, 3, 1, 2)),
                 a_bs=np.ascontiguousarray(bs.reshape(2, 8 * P)),
                 rope_cos=cos, rope_sin=sin)
        maps.append(m)
    return maps


_NC_CACHE = {}


def run_layers(maps, layers, dump_all=False, stage=99):
    key = (tuple(layers), dump_all, stage)
    if key not in _NC_CACHE:
        _NC_CACHE[key] = Builder(layers, dump_all, stage).build()
    nc = _NC_CACHE[key]
    res = run_bass_kernel_spmd(nc, maps, core_ids=list(range(8)))
    return res.results


def assemble(results):
    out = np.zeros((4, 4096, D), np.float32)
    for core in range(8):
        b = core // 2
        y = np.asarray(results[core]["yout"], dtype=np.float32)
        if core % 2 == 0:
            out[b, 0:2048] = y
        else:
            out[b, 2048:4096] = y[::-1]
    return out


def kernel(**inputs):
    maps = prep_inputs(inputs)
    res = run_layers(maps, [0, 1, 2, 3])
    return assemble(res)
```

```python
import numpy as np
from contextlib import ExitStack
import concourse.bass as bass
import concourse.mybir as mybir
from concourse.bass_utils import run_bass_kernel_spmd

F32 = mybir.dt.float32
BF16 = mybir.dt.bfloat16
AF = mybir.ActivationFunctionType
ALU = mybir.AluOpType

P = 128
D = 1024
KD = 8
NLAT = 18
NT = 20
CTX0 = 18
DEPTH = 4
ALPHA = (2.0 * DEPTH) ** 0.25
LN_EPS = 1e-5
GELU_C = 0.7978845608028654
NRING = 4


class Op:
    __slots__ = ("eng", "fn", "deps", "sig", "sem", "val", "dkey", "idx", "raw")

    def __init__(self, eng, fn, dkey=None):
        self.eng = eng
        self.fn = fn
        self.deps = []
        self.sig = False
        self.sem = None
        self.val = None
        self.dkey = dkey
        self.raw = set()


class Prog:
    ENGS = ("pe", "act", "dve", "pool", "sp")

    def __init__(self, nc):
        self.nc = nc
        self.streams = {e: [] for e in self.ENGS}
        self.last_w = {}
        self.readers = {}
        self.final_waits = []

    def add(self, eng, fn, reads=(), writes=(), dkey=None):
        op = Op(eng, fn, dkey)
        deps = []
        rawset = set()
        for r in reads:
            w = self.last_w.get(r)
            if w is not None:
                deps.append(w)
                rawset.add(id(w))
        for w_ in writes:
            lw = self.last_w.get(w_)
            if lw is not None:
                deps.append(lw)
            deps.extend(self.readers.get(w_, ()))
        seen = set()
        for d in deps:
            if id(d) in seen or d is op:
                continue
            seen.add(id(d))
            if d.eng == eng and d.dkey is None and dkey is None:
                if eng == "pe":
                    continue
            op.deps.append(d)
            d.sig = True
        for r in reads:
            self.readers.setdefault(r, []).append(op)
        for w_ in writes:
            self.last_w[w_] = op
            self.readers[w_] = []
        self.streams[eng].append(op)
        return op

    def barrier(self, marker_fn):
        last = {}
        for e in ("pe", "act", "dve"):
            s = [o for o in self.streams[e] if o.fn is not None]
            if s:
                last[e] = s[-1]
        for e in ("pe", "act", "dve"):
            op = Op(e, None)
            for e2, l in last.items():
                if e2 != e:
                    op.deps.append(l)
                    l.sig = True
            self.streams[e].append(op)
        self.add("dve", marker_fn, writes=["BAR"])

    def emit(self):
        nc = self.nc
        with ExitStack() as st:
            sems = {}

            def getsem(key):
                if key not in sems:
                    sems[key] = st.enter_context(nc.semaphore("s%d" % len(sems)))
                return sems[key]

            cnt = {}
            for e in self.ENGS:
                for op in self.streams[e]:
                    if op.fn is None:
                        continue
                    if op.dkey is not None:
                        op.sig = True
                        k = ("dma", op.dkey)
                        cnt[k] = cnt.get(k, 0) + 16
                        op.sem = getsem(k)
                        op.val = cnt[k]
                    elif op.sig:
                        k = ("eng", e)
                        cnt[k] = cnt.get(k, 0) + 1
                        op.sem = getsem(k)
                        op.val = cnt[k]
            fin = [(o.sem, o.val) for o in self.final_waits]
            block = st.enter_context(nc.Block())

            def run(ename, eng):
                waited = {}
                for op in self.streams[ename]:
                    need = {}
                    for d in op.deps:
                        k = id(d.sem)
                        if waited.get(k, 0) >= d.val:
                            continue
                        if k not in need or need[k][1] < d.val:
                            need[k] = (d.sem, d.val)
                    for k, (s, v) in need.items():
                        eng.wait_ge(s, v)
                        waited[k] = v
                    if op.fn is None:
                        continue
                    ins = op.fn(eng)
                    if op.sig:
                        ins.then_inc(op.sem, 16 if op.dkey is not None else 1)
                if ename == "sp":
                    best = {}
                    for s, v in fin:
                        if id(s) not in best or best[id(s)][1] < v:
                            best[id(s)] = (s, v)
                    for s, v in best.values():
                        eng.wait_ge(s, v)

            @block.tensor
            def _(e):
                run("pe", e)

            @block.scalar
            def _(e):
                run("act", e)

            @block.vector
            def _(e):
                run("dve", e)

            @block.gpsimd
            def _(e):
                run("pool", e)

            @block.sync
            def _(e):
                run("sp", e)


class Builder:
    def __init__(self, layers, dump_all=False, stage=99):
        self.stage = stage
        self.layers = list(layers)
        self.dump_all = dump_all
        self.nc = bass.Bass("TRN2", target_bir_lowering=False)
        self.pg = Prog(self.nc)
        self.st = ExitStack()
        self.ring_n = 0
        self.ring_pending = []
        self.psn = 0
        self.pstn = 0

    def sb(self, name, shape, dt):
        return self.st.enter_context(self.nc.sbuf_tensor(name, shape, dt))

    def dram(self, name, shape, dt=F32, kind="ExternalInput"):
        return self.nc.dram_tensor(name, shape, dt, kind=kind).ap()

    def psbank(self):
        i = self.psn % len(self.psf)
        self.psn += 1
        return self.psf[i], ("ps", i)

    def pstbank(self):
        i = self.pstn % len(self.pst)
        self.pstn += 1
        return self.pst[i], ("pst", i)

    def wload(self, src_ap):
        slot = self.ring_n % NRING
        self.ring_n += 1
        dst = self.ring[slot]
        key = ("ring", slot)
        self.pg.add("pool", lambda e, d=dst, s=src_ap: e.dma_start(out=d[:], in_=s),
                    writes=[key], dkey=key)
        return dst, key

    def build(self):
        nc, pg = self.nc, self.pg
        a = self.dram
        self.d_x = a("xin", [NLAT * P, D])
        self.d_ctx = a("ctxin", [2 * P, D])
        self.d_cvec = a("cvec", [P, KD, 2])
        self.d_adaw = a("ada_w", [DEPTH, D, 3 * D])
        self.d_adab_col = a("adab_col", [DEPTH, P, 16])
        self.d_adab_gate = a("adab_gate", [DEPTH, D])
        self.d_lng = a("ln_g", [DEPTH, D])
        self.d_lnb = a("ln_b", [DEPTH, D])
        self.d_awin = a("a_w_in", [2, D, 6 * D])
        self.d_alng = a("a_lng_col", [2, P, 16])
        self.d_alnb = a("a_lnb_col", [2, P, 16])
        self.d_awsT = a("a_wsT", [2, P, 8, P])
        self.d_abs = a("a_bs", [2, 8 * P])
        self.d_awout = a("a_w_out", [2, 2 * D, D])
        self.d_bwin = a("b_w_in", [2, D, 2560])
        self.d_bsink = a("b_sink", [2, 16])
        self.d_bwout = a("b_w_out", [2, D, D])
        self.d_cos = a("rope_cos", [P, NT * P])
        self.d_sin = a("rope_sin", [P, NT * P])
        self.d_ident = a("ident", [P, P])
        self.d_perm = a("perm", [P, P])
        self.d_mask = a("masks", [P, 2, 512])
        self.d_y = a("yout", [16 * P, D], kind="ExternalOutput")
        if self.dump_all:
            self.d_xall = a("xall", [NT * P, D], kind="ExternalOutput")

        self.x = self.sb("x", [P, NT, D], F32)
        self.ring = [self.sb("ring%d" % i, [P, KD, 512], BF16) for i in range(NRING)]
        self.gate_bc = self.sb("gate_bc", [P, 2, D], F32)
        self.gln_bc = self.sb("gln_bc", [P, D], F32)
        self.bln_bc = self.sb("bln_bc", [P, D], F32)
        self.ident = self.sb("identb", [P, P], BF16)
        self.ones = self.sb("ones", [P, P], BF16)
        self.adab = self.sb("adab", [P, 16], F32)
        self.adatmp = self.sb("adatmp", [P, 256], F32)
        self.permb = self.sb("permb", [P, P], BF16)
        self.osbt = self.sb("osbt", [P, 272], F32)
        self.maskb = self.sb("maskb", [P, 2, 512], BF16)
        self.cond2 = self.sb("cond2", [P, KD, 2], BF16)
        self.cond_bc = self.sb("cond_bc", [P, KD, 2, P], BF16)
        self.modcol = self.sb("modcol", [P, 2, KD, 2], F32)
        self.small = self.sb("small", [P, 64], F32)
        self.stats = self.sb("stats", [P, 4, 6], F32)
        self.mv = self.sb("mv", [P, 4], F32)
        self.arena = self.sb("arena", [P, 33 * 1024], BF16)
        self.psf = [self.st.enter_context(nc.psum_tensor("psf%d" % i, [P, 512], F32)) for i in range(6)]
        self.pst = [self.st.enter_context(nc.psum_tensor("pst%d" % i, [P, 1024], BF16)) for i in range(2)]

        self.prologue()
        for li, l in enumerate(self.layers):
            if li > 0:
                pg.barrier(lambda e: e.memset(self.small[:, 63:64], 0.0))
            if l % 2 == 0:
                self.layer_a(l)
            else:
                self.layer_b(l)
        self.epilogue()
        pg.emit()
        self.st.close()
        return nc

    def prologue(self):
        pg = self.pg
        x = self.x
        xin = self.d_x.rearrange("(t p) d -> p t d", p=P)
        for b in range(5):
            t0 = b * 4
            if b < 4:
                src = xin[:, t0:t0 + 4, :]
                dst = x[:, t0:t0 + 4, :]
                pg.add("sp", lambda e, s=src, d=dst: e.dma_start(out=d, in_=s),
                       writes=[("x", t) for t in range(t0, t0 + 4)], dkey=("xin", b))
            else:
                src = xin[:, 16:18, :]
                dst = x[:, 16:18, :]
                pg.add("sp", lambda e, s=src, d=dst: e.dma_start(out=d, in_=s),
                       writes=[("x", 16), ("x", 17)], dkey=("xin", b))
                src2 = self.d_ctx.rearrange("(t p) d -> p t d", p=P)
                dst2 = x[:, 18:20, :]
                pg.add("sp", lambda e, s=src2, d=dst2: e.dma_start(out=d, in_=s),
                       writes=[("x", 18), ("x", 19)], dkey=("xin", 5))
        idf = self.small
        self.identf = self.sb("identf", [P, P], F32)
        pg.add("sp", lambda e: e.dma_start(out=self.identf[:], in_=self.d_ident), writes=["identf"], dkey="c0")
        pg.add("dve", lambda e: e.tensor_copy(out=self.ident[:], in_=self.identf[:]), reads=["identf"], writes=["ident"])
        pg.add("dve", lambda e: e.memset(self.ones[:], 1.0), writes=["ones"])
        pg.add("pool", lambda e: e.dma_start(out=self.permb[:], in_=self.d_perm), writes=["permb"], dkey="permb")
        pg.add("pool", lambda e: e.dma_start(out=self.maskb[:], in_=self.d_mask), writes=["maskb"], dkey="maskb")
        self.cv = self.sb("cv", [P, KD, 2], F32)
        self.cv2 = self.sb("cv2", [P, KD, 2], F32)
        pg.add("sp", lambda e: e.dma_start(out=self.cv[:], in_=self.d_cvec), writes=["cv"], dkey="c1")
        pg.add("act", lambda e: e.activation(out=self.cv2[:], in_=self.cv[:], func=AF.Tanh, scale=0.5),
               reads=["cv"], writes=["cv2"])
        pg.add("dve", lambda e: e.scalar_tensor_tensor(out=self.cv2[:], in0=self.cv2[:], scalar=1.0, in1=self.cv[:],
                                                       op0=ALU.add, op1=ALU.mult),
               reads=["cv", "cv2"], writes=["cv2"])
        pg.add("dve", lambda e: e.tensor_scalar(out=self.cond2[:], in0=self.cv2[:], scalar1=0.5, scalar2=None,
                                                op0=ALU.mult),
               reads=["cv2"], writes=["cond2"])
        pg.add("dve", lambda e: e.tensor_copy(out=self.cond_bc[:],
                                              in_=self.cond2[:].unsqueeze(3).to_broadcast([P, KD, 2, P])),
               reads=["cond2"], writes=["cond_bc"])

    def ada(self, l):
        pg = self.pg
        wv = self.d_adaw[l].rearrange("(k p) c -> p k c", p=P)
        adab = self.adab
        pg.add("sp", lambda e: e.dma_start(out=adab[:], in_=self.d_adab_col[l]), writes=["adab"], dkey="adab")
        for v in range(2):
            pg.add("sp", lambda e, v=v: e.dma_start(out=self.gate_bc[:, v, :],
                                                    in_=self.d_adab_gate[l].partition_broadcast(P)),
                   writes=[("gate", v)], dkey=("gate", v))
        pg.add("sp", lambda e: e.dma_start(out=self.gln_bc[:], in_=self.d_lng[l].partition_broadcast(P)),
               writes=["gln"], dkey="gln")
        pg.add("sp", lambda e: e.dma_start(out=self.bln_bc[:], in_=self.d_lnb[l].partition_broadcast(P)),
               writes=["bln"], dkey="bln")
        for pc in range(6):
            if self.stage in (1, 1.25) and pc >= 4:
                break
            if self.stage == 1.5 and pc < 4:
                continue
            w, wk = self.wload(wv[:, :, pc * 512:(pc + 1) * 512])
            if pc < 4:
                ps, pk = self.psbank()
                kind = pc // 2

                def mm(e, w=w, ps=ps):
                    ins = None
                    for ch in range(4):
                        for k in range(KD):
                            ins = e.matmul(ps[:, ch * 64:(ch + 1) * 64], lhsT=w[:, k, ch * P:(ch + 1) * P],
                                           rhs=self.cond_bc[:, k, :, 0:32], start=(k == 0), stop=(k == KD - 1))
                    return ins
                pg.add("pe", mm, reads=[wk, "cond_bc"], writes=[pk])
                if self.stage == 1.25:
                    continue
                c0 = (pc % 2) * 4
                tmp = self.adatmp
                pg.add("dve", lambda e, ps=ps, tmp=tmp: e.tensor_copy(out=tmp[:], in_=ps[:, 0:256]),
                       reads=[pk], writes=["adatmp"])
                tv = tmp[:].rearrange("p (c v r) -> p c v r", c=4, v=2)
                for v in range(2):
                    dst = self.modcol[:, kind, c0:c0 + 4, v]
                    src = tv[:, :, v, 0]
                    bias = self.adab[:, kind * 8 + c0: kind * 8 + c0 + 4]
                    if kind == 0:
                        pg.add("dve", lambda e, dst=dst, src=src, bias=bias: e.tensor_tensor(out=dst, in0=src, in1=bias, op=ALU.add),
                               reads=["adatmp", "adab"], writes=[("modcol", pc)])
                    else:
                        pg.add("dve", lambda e, dst=dst, src=src, bias=bias: e.scalar_tensor_tensor(
                            out=dst, in0=src, scalar=1.0, in1=bias, op0=ALU.add, op1=ALU.add),
                            reads=["adatmp", "adab"], writes=[("modcol", pc)])
            else:
                half = pc - 4
                for v in range(2):
                    ps, pk = self.psbank()

                    def mm(e, w=w, ps=ps, v=v):
                        ins = None
                        for k in range(KD):
                            ins = e.matmul(ps[:, :], lhsT=self.cond_bc[:, k, v, :], rhs=w[:, k, :],
                                           start=(k == 0), stop=(k == KD - 1))
                        return ins
                    pg.add("pe", mm, reads=[wk, "cond_bc"], writes=[pk])
                    dst = self.gate_bc[:, v, half * 512:(half + 1) * 512]
                    pg.add("dve", lambda e, dst=dst, ps=ps: e.tensor_tensor(out=dst, in0=ps[:, :], in1=dst, op=ALU.add),
                           reads=[pk, ("gate", v)], writes=[("gate", v)])

    def make_hT(self, tiles, hT, hkey):
        pg = self.pg
        for slot, t in enumerate(tiles):
            v = 1 if t >= CTX0 else 0
            xb = self.xb[slot % self.nbuf]
            xk = ("xb", slot % self.nbuf)
            pg.add("act", lambda e, xb=xb, t=t: e.copy(out=xb[:], in_=self.x[:, t, :]),
                   reads=[("x", t)], writes=[xk])
            pst, ptk = self.pstbank()

            def tr(e, xb=xb, pst=pst):
                ins = None
                for k in range(KD):
                    ins = e.transpose(pst[:, k * P:(k + 1) * P], xb[:, k * P:(k + 1) * P], self.ident[:])
                return ins
            pg.add("pe", tr, reads=[xk, "ident"], writes=[ptk])
            for k in range(KD):
                dst = hT[:, k, slot * P:(slot + 1) * P]
                src = pst[:, k * P:(k + 1) * P]
                sc = self.modcol[:, 1, k, v:v + 1]
                sh = self.modcol[:, 0, k, v:v + 1]
                pg.add("dve", lambda e, dst=dst, src=src, sc=sc, sh=sh: e.tensor_scalar(
                    out=dst, in0=src, scalar1=sc, scalar2=sh, op0=ALU.mult, op1=ALU.add),
                    reads=[ptk] + [("modcol", i) for i in range(4)], writes=[(hkey, slot)])

    def tail(self, t, pieces, nk, lhs_fn, lhs_reads):
        pg = self.pg
        v = 1 if t >= CTX0 else 0
        r = self.rbuf[self.rn % self.nbuf]
        rk = ("r", self.rn % self.nbuf)
        self.rn += 1
        for half in range(2):
            ps, pk = self.psbank()
            plist = pieces[half]

            def mm(e, ps=ps, plist=plist):
                ins = None
                n = len(plist) * KD
                i = 0
                for (w, _) in plist:
                    for k in range(KD):
                        ins = e.matmul(ps[:, :], lhsT=lhs_fn(i), rhs=w[:, k, :], start=(i == 0), stop=(i == n - 1))
                        i += 1
                return ins
            pg.add("pe", mm, reads=[k_ for (_, k_) in plist] + lhs_reads, writes=[pk])
            sl = slice(half * 512, (half + 1) * 512)
            pg.add("dve", lambda e, ps=ps, sl=sl, r=r, v=v: e.tensor_tensor(
                out=r[:, sl], in0=ps[:, :], in1=self.gate_bc[:, v, sl], op=ALU.mult),
                reads=[pk, ("gate", v)], writes=[rk])
            pg.add("dve", lambda e, sl=sl, r=r, t=t: e.scalar_tensor_tensor(
                out=r[:, sl], in0=self.x[:, t, sl], scalar=ALPHA, in1=r[:, sl], op0=ALU.mult, op1=ALU.add),
                reads=[rk, ("x", t)], writes=[rk])
        self.ln_stats(r, rk, 2)
        pg.add("act", lambda e, r=r: e.activation(out=r[:], in_=r[:], func=AF.Identity,
                                                  bias=self.mv[:, 3:4], scale=self.mv[:, 2:3]),
               reads=[rk, "mv"], writes=[rk])
        pg.add("dve", lambda e, r=r: e.tensor_tensor(out=r[:], in0=r[:], in1=self.gln_bc[:], op=ALU.mult),
               reads=[rk, "gln"], writes=[rk])
        pg.add("dve", lambda e, r=r, t=t: e.tensor_tensor(out=self.x[:, t, :], in0=r[:], in1=self.bln_bc[:], op=ALU.add),
               reads=[rk, "bln"], writes=[("x", t)])

    def ln_stats(self, buf, key, nchunks):
        pg = self.pg
        for c in range(nchunks):
            pg.add("dve", lambda e, c=c: e.bn_stats(out=self.stats[:, c, :], in_=buf[:, c * 512:(c + 1) * 512]),
                   reads=[key], writes=[("stats", c)])
        pg.add("dve", lambda e: e.bn_aggr(out=self.mv[:, 0:2], in_=self.stats[:, 0:nchunks, :]),
               reads=[("stats", c) for c in range(nchunks)], writes=["mv01"])
        pg.add("dve", lambda e: e.tensor_scalar(out=self.mv[:, 2:3], in0=self.mv[:, 1:2], scalar1=LN_EPS, scalar2=None,
                                                op0=ALU.add),
               reads=["mv01"], writes=["mv2"])
        pg.add("act", lambda e: e.activation(out=self.mv[:, 2:3], in_=self.mv[:, 2:3], func=AF.Sqrt),
               reads=["mv2"], writes=["mv2"])
        pg.add("dve", lambda e: e.reciprocal(out=self.mv[:, 2:3], in_=self.mv[:, 2:3]),
               reads=["mv2"], writes=["mv2"])
        pg.add("dve", lambda e: e.scalar_tensor_tensor(out=self.mv[:, 3:4], in0=self.mv[:, 0:1], scalar=-1.0,
                                                       in1=self.mv[:, 2:3], op0=ALU.mult, op1=ALU.mult),
               reads=["mv01", "mv2"], writes=["mv"])

    def layer_a(self, l):
        pg = self.pg
        j = l // 2
        ar = self.arena
        o = 0

        def carve(n_bf16):
            nonlocal o
            v = ar[:, o:o + n_bf16]
            o += n_bf16
            return v
        hT = carve(KD * 512).rearrange("p (k n) -> p k n", k=KD)
        vy = carve(16 * 512)
        vn = vy.rearrange("p (c s f) -> p s c f", c=16, s=4)
        yT = vy.rearrange("p (c n) -> p c n", c=16)
        gv = carve(2 * 2048).bitcast(F32)
        cs = [[carve(2 * 512).bitcast(F32) for _ in range(3)] for _ in range(2)]
        self.rbuf = [carve(2 * 1024).bitcast(F32) for _ in range(2)]
        self.xb = [carve(1024) for _ in range(2)]
        E = carve(2 * 16 * P).bitcast(F32).rearrange("p (c n) -> p c n", c=16)
        wsT = carve(8 * P).rearrange("p (g n) -> p g n", g=8)
        wsf = ar[:, 14336:14336 + 2 * 8 * P].bitcast(F32).rearrange("p (g n) -> p g n", g=8)
        bsb = gv[:, 0:1024].rearrange("p (g n) -> p g n", g=8)
        self.rn = 0
        self.nbuf = 2
        lncol = self.small[:, 0:32]

        if self.stage < 1:
            return
        self.ada(l)
        if self.stage < 2:
            return
        pg.add("sp", lambda e: e.dma_start(out=lncol[:, 0:16], in_=self.d_alng[j]), writes=["lncol_g"], dkey="lncg")
        pg.add("sp", lambda e: e.dma_start(out=lncol[:, 16:32], in_=self.d_alnb[j]), writes=["lncol_b"], dkey="lncb")
        pg.add("sp", lambda e: e.dma_start(out=wsf, in_=self.d_awsT[j]), reads=["BAR"], writes=["wsf", "gv"], dkey="wsf")
        pg.add("sp", lambda e: e.dma_start(out=bsb, in_=self.d_abs[j].partition_broadcast(P).rearrange("p (g n) -> p g n", g=8)),
               reads=["BAR"], writes=["bsb", "gv"], dkey="bsb")
        pg.add("dve", lambda e: e.tensor_copy(out=wsT, in_=wsf), reads=["wsf"], writes=["wsT"])
        ones = self.ones
        for hh in range(2):
            ps, pk = self.psbank()

            def mm(e, ps=ps, hh=hh):
                ins = None
                for g in range(4):
                    ins = e.matmul(ps[:, g * P:(g + 1) * P], lhsT=ones[:], rhs=wsT[:, hh * 4 + g, :], start=True, stop=True)
                return ins
            pg.add("pe", mm, reads=["wsT", "ones"], writes=[pk])
            for g in range(4):
                for c2 in range(2):
                    cc = (hh * 4 + g) * 2 + c2
                    pg.add("dve", lambda e, ps=ps, g=g, cc=cc, gg=hh * 4 + g: e.scalar_tensor_tensor(
                        out=E[:, cc, :], in0=ps[:, g * P:(g + 1) * P], scalar=lncol[:, 16 + cc:17 + cc],
                        in1=bsb[:, gg, :], op0=ALU.mult, op1=ALU.add),
                        reads=[pk, "lncol_b", "bsb"], writes=["E"])

        if self.stage < 3:
            return
        win = self.d_awin[j].rearrange("(k p) c -> p k c", p=P)
        wout = self.d_awout[j].rearrange("(k p) c -> p k c", p=P)
        tiles_all = list(range(NLAT if l == 0 else NLAT - 1)) + [CTX0, CTX0 + 1]
        blocks = [tiles_all[i:i + 4] for i in range(0, len(tiles_all), 4)]
        for bi, tiles in enumerate(blocks):
            nb = len(tiles)
            ntok = nb * P
            if bi == 0:
                self.make_hT(tiles, hT, "hT")
            if self.stage < 4:
                return
            hreads = [("hT", s) for s in range(nb)]
            wv = [self.wload(win[:, :, 2048 + pv * 512: 2048 + (pv + 1) * 512]) for pv in range(4)]
            for slot, t in enumerate(tiles):
                for pv in range(4):
                    w, wk = wv[pv]
                    ps, pk = self.psbank()

                    def mm(e, ps=ps, w=w, slot=slot):
                        ins = None
                        for k in range(KD):
                            ins = e.matmul(ps[:, :], lhsT=hT[:, k, slot * P:(slot + 1) * P], rhs=w[:, k, :],
                                           start=(k == 0), stop=(k == KD - 1))
                        return ins
                    pg.add("pe", mm, reads=[wk, ("hT", slot)], writes=[pk])
                    pg.add("act", lambda e, ps=ps, pv=pv: e.activation(out=gv[:, pv * 512:(pv + 1) * 512], in_=ps[:, :],
                                                                       func=AF.Gelu_apprx_tanh),
                           reads=[pk], writes=["gv", "bsb", "wsf"])
                self.ln_stats(gv, "gv", 4)
                pg.add("act", lambda e, slot=slot: e.activation(
                    out=vn[:, slot, :, :], in_=gv[:, :].rearrange("p (c f) -> p c f", c=16), func=AF.Identity,
                    bias=self.mv[:, 3:4], scale=self.mv[:, 2:3]),
                    reads=["gv", "mv"], writes=[("vy", cc) for cc in range(16)])
            if self.stage < 5:
                return
            for pu in range(4):
                wu, wuk = self.wload(win[:, :, pu * 512:(pu + 1) * 512])
                wz, wzk = self.wload(win[:, :, 4096 + pu * 512: 4096 + (pu + 1) * 512])
                for c4 in range(4):
                    cc = pu * 4 + c4
                    g = cc // 2
                    A, B, C = cs[cc % 2]
                    ka, kb, kc = ("csA", cc % 2), ("csB", cc % 2), ("csC", cc % 2)
                    ps_s, pks = self.psbank()

                    def mms(e, ps_s=ps_s, cc=cc, g=g, nb=nb):
                        ins = None
                        for s in range(nb):
                            ins = e.matmul(ps_s[:, s * P:(s + 1) * P], lhsT=vn[:, s, cc, :], rhs=wsT[:, g, :],
                                           start=True, stop=True)
                        return ins
                    pg.add("pe", mms, reads=[("vy", cc), "wsT"], writes=[pks])
                    ps_u, pku = self.psbank()

                    def mmu(e, ps=ps_u, w=wu, c4=c4, ntok=ntok):
                        ins = None
                        for k in range(KD):
                            ins = e.matmul(ps[:, 0:ntok], lhsT=w[:, k, c4 * P:(c4 + 1) * P], rhs=hT[:, k, 0:ntok],
                                           start=(k == 0), stop=(k == KD - 1))
                        return ins
                    pg.add("pe", mmu, reads=[wuk] + hreads, writes=[pku])
                    ps_z, pkz = self.psbank()
                    pg.add("pe", lambda e, ps=ps_z, w=wz, c4=c4, ntok=ntok: mmu(e, ps, w, c4, ntok),
                           reads=[wzk] + hreads, writes=[pkz])
                    pg.add("act", lambda e, ps=ps_u, A=A, ntok=ntok: e.activation(out=A[:, 0:ntok], in_=ps[:, 0:ntok],
                                                                                 func=AF.Gelu_apprx_tanh),
                           reads=[pku], writes=[ka])
                    pg.add("act", lambda e, ps=ps_z, B=B, ntok=ntok: e.activation(out=B[:, 0:ntok], in_=ps[:, 0:ntok],
                                                                                 func=AF.Tanh, scale=0.5),
                           reads=[pkz], writes=[kb])
                    pg.add("dve", lambda e, ps=ps_s, C=C, cc=cc, nb=nb: e.scalar_tensor_tensor(
                        out=C[:, 0:nb * P].rearrange("p (s n) -> p s n", s=nb),
                        in0=ps[:, 0:nb * P].rearrange("p (s n) -> p s n", s=nb),
                        scalar=lncol[:, cc:cc + 1],
                        in1=E[:, cc, :].unsqueeze(1).to_broadcast([P, nb, P]), op0=ALU.mult, op1=ALU.add),
                        reads=[pks, "E", "lncol_g"], writes=[kc])
                    pg.add("dve", lambda e, ps=ps_z, B=B, ntok=ntok: e.scalar_tensor_tensor(
                        out=B[:, 0:ntok], in0=B[:, 0:ntok], scalar=1.0, in1=ps[:, 0:ntok], op0=ALU.add, op1=ALU.mult),
                        reads=[kb, pkz], writes=[kb])
                    pg.add("dve", lambda e, A=A, C=C, ntok=ntok: e.tensor_tensor(out=A[:, 0:ntok], in0=A[:, 0:ntok],
                                                                                in1=C[:, 0:ntok], op=ALU.mult),
                           reads=[ka, kc], writes=[ka])
                    pg.add("dve", lambda e, A=A, B=B, cc=cc, ntok=ntok: e.scalar_tensor_tensor(
                        out=yT[:, cc, 0:ntok], in0=A[:, 0:ntok], scalar=0.5, in1=B[:, 0:ntok], op0=ALU.mult, op1=ALU.mult),
                        reads=[ka, kb, ("vy", cc)], writes=[("vy", cc)])
            if self.stage < 6:
                return
            wo = [[self.wload(wout[:, kg * 8:(kg + 1) * 8, half * 512:(half + 1) * 512]) for kg in range(2)]
                  for half in range(2)]
            if bi + 1 < len(blocks):
                self.make_hT(blocks[bi + 1], hT, "hT")
            for slot, t in enumerate(tiles):
                self.tail(t, wo, 16, lambda i, slot=slot: yT[:, i, slot * P:(slot + 1) * P],
                          [("vy", cc) for cc in range(16)])

    def layer_b(self, l):
        pg = self.pg
        j = l // 2
        ar = self.arena
        o = 0

        def carve(n_bf16):
            nonlocal o
            v = ar[:, o:o + n_bf16]
            o += n_bf16
            return v
        hT = carve(KD * 512).rearrange("p (k n) -> p k n", k=KD)
        KT = carve(2 * NT * P).rearrange("p (c n) -> p c n", c=2)
        Vflat = carve(NT * 4 * 68)
        Va = Vflat.rearrange("p (t h d) -> p t h d", t=NT, h=4)
        QT = carve(KD * 512).rearrange("p (c n) -> p c n", c=8)
        szT = carve(KD * 512).rearrange("p (c n) -> p c n", c=8)
        PT = carve(5 * 512).rearrange("p (j n) -> p j n", j=5)
        onyt = carve(2048)
        on = onyt[:, 0:1024]
        yT = onyt[:, 1024:2048].rearrange("p (c n) -> p c n", c=8)
        zt = onyt.bitcast(F32)[:, 0:512]
        self.rbuf = [carve(2 * 1024).bitcast(F32)] * 2
        RA = self.rbuf[0][:, 0:512]
        RB = self.rbuf[0][:, 512:1024]
        self.xb = [carve(1024)] * 2
        rope = carve(2 * 1024).bitcast(F32)
        cosb = rope[:, 0:512]
        sinb = rope[:, 512:1024]
        qsb = [carve(512) for _ in range(2)]
        osb = self.osbt
        self.rn = 0
        self.nbuf = 1
        esink = self.small[:, 32:48]
        den = self.small[:, 48:52]

        if self.stage < 1:
            return
        self.ada(l)
        pg.add("sp", lambda e: e.dma_start(out=esink, in_=self.d_bsink[j].partition_broadcast(P)),
               writes=["esink"], dkey="esink")
        pg.add("act", lambda e: e.activation(out=esink, in_=esink, func=AF.Exp), reads=["esink"], writes=["esink"])
        pg.add("dve", lambda e: e.memset(Vflat, 1.0), reads=["BAR"], writes=[("V", t) for t in range(NT)])

        if self.stage < 2:
            return
        win = self.d_bwin[j].rearrange("(k p) c -> p k c", p=P)
        wout = self.d_bwout[j].rearrange("(k p) c -> p k c", p=P)
        tmax = 17 if l == 1 else 16
        kv_tiles = list(range(tmax + 1)) + [CTX0, CTX0 + 1]
        q_tiles = (list(range(17)) + [CTX0, CTX0 + 1]) if l == 1 else list(range(16))
        self.qn = 0

        def load_rope(tiles):
            for slot, t in enumerate(tiles):
                pg.add("sp", lambda e, slot=slot, t=t: e.dma_start(out=cosb[:, slot * P:(slot + 1) * P],
                                                                 in_=self.d_cos[:, t * P:(t + 1) * P]),
                       reads=["BAR"], writes=["rope"], dkey="ropec")
                pg.add("sp", lambda e, slot=slot, t=t: e.dma_start(out=sinb[:, slot * P:(slot + 1) * P],
                                                                 in_=self.d_sin[:, t * P:(t + 1) * P]),
                       reads=["BAR"], writes=["rope"], dkey="ropes")

        def proj_rope(w, wk, col0, ntok, nb, dst, dkeys):
            ps, pk = self.psbank()

            def mm(e, ps=ps):
                ins = None
                for k in range(KD):
                    ins = e.matmul(ps[:, 0:ntok], lhsT=w[:, k, col0:col0 + P], rhs=hT[:, k, 0:ntok],
                                   start=(k == 0), stop=(k == KD - 1))
                return ins
            if self.stage == 2.1:
                return
            pg.add("pe", mm, reads=[wk] + [("hT", s_) for s_ in range(nb)], writes=[pk])
            qb = qsb[self.qn % 2]
            qk = ("qsb", self.qn % 2)
            self.qn += 1
            pg.add("act", lambda e, ps=ps, qb=qb: e.copy(out=qb[:, 0:ntok], in_=ps[:, 0:ntok]), reads=[pk], writes=[qk])
            ps2, pk2 = self.psbank()
            pg.add("pe", lambda e, ps2=ps2, qb=qb: e.matmul(ps2[:, 0:ntok], lhsT=self.permb[:], rhs=qb[:, 0:ntok],
                                                           start=True, stop=True),
                   reads=[qk, "permb"], writes=[pk2])
            if self.stage == 2.2:
                return
            rk = ("r", 0)
            pg.add("dve", lambda e, ps=ps: e.tensor_tensor(out=RA[:, 0:ntok], in0=ps[:, 0:ntok], in1=cosb[:, 0:ntok], op=ALU.mult),
                   reads=[pk, "rope", qk], writes=[rk])
            pg.add("dve", lambda e, ps2=ps2: e.tensor_tensor(out=RB[:, 0:ntok], in0=ps2[:, 0:ntok], in1=sinb[:, 0:ntok], op=ALU.mult),
                   reads=[pk2, "rope"], writes=[rk])
            pg.add("dve", lambda e: e.tensor_tensor(out=dst, in0=RA[:, 0:ntok], in1=RB[:, 0:ntok], op=ALU.add),
                   reads=[rk], writes=dkeys)

        for b0 in range(0, len(kv_tiles), 4):
            tiles = kv_tiles[b0:b0 + 4]
            nb = len(tiles)
            ntok = nb * P
            self.make_hT(tiles, hT, "hT")
            load_rope(tiles)
            wkv, wkvk = self.wload(win[:, :, 1024:1536])
            for p_ in range(2):
                runs = []
                s0 = 0
                for s_ in range(1, nb + 1):
                    if s_ == nb or tiles[s_] != tiles[s_ - 1] + 1:
                        runs.append((s0, s_))
                        s0 = s_
                if len(runs) == 1:
                    dst = KT[:, p_, tiles[0] * P: tiles[0] * P + ntok]
                    proj_rope(wkv, wkvk, p_ * P, ntok, nb, dst, [("KT", t) for t in tiles])
                else:
                    tmp = szT[:, p_, 0:ntok]
                    proj_rope(wkv, wkvk, p_ * P, ntok, nb, tmp, [("sz", p_)])
                    for (a_, b_) in runs:
                        pg.add("dve", lambda e, a_=a_, b_=b_, p_=p_, tiles=tiles: e.tensor_copy(
                            out=KT[:, p_, tiles[a_] * P: tiles[a_] * P + (b_ - a_) * P], in_=szT[:, p_, a_ * P:b_ * P]),
                            reads=[("sz", p_)], writes=[("KT", t) for t in tiles[a_:b_]])
            for slot, t in enumerate(tiles):
                if self.stage in (2.1, 2.2, 2.3):
                    break
                ps, pk = self.psbank()

                def mmv(e, ps=ps, slot=slot, w=wkv):
                    ins = None
                    for k in range(KD):
                        ins = e.matmul(ps[:, 0:256], lhsT=hT[:, k, slot * P:(slot + 1) * P], rhs=w[:, k, 256:512],
                                       start=(k == 0), stop=(k == KD - 1))
                    return ins
                pg.add("pe", mmv, reads=[wkvk, ("hT", slot)], writes=[pk])
                if self.stage == 2.4:
                    continue
                pg.add("act", lambda e, ps=ps, t=t: e.copy(out=Va[:, t, :, 0:64],
                                                           in_=ps[:, 0:256].rearrange("p (h d) -> p h d", h=4)),
                       reads=[pk], writes=[("V", t)])

        if self.stage < 3:
            return
        for b0 in range(0, len(q_tiles), 4):
            tiles = q_tiles[b0:b0 + 4]
            nb = len(tiles)
            ntok = nb * P
            if b0 == 0:
                self.make_hT(tiles, hT, "hT")
            load_rope(tiles)
            for qp in range(2):
                wq, wqk = self.wload(win[:, :, qp * 512:(qp + 1) * 512])
                for c4 in range(4):
                    qc = qp * 4 + c4
                    proj_rope(wq, wqk, c4 * P, ntok, nb, QT[:, qc, 0:ntok], [("QT", qc)])
            for zp in range(2):
                wz, wzk = self.wload(win[:, :, 1536 + zp * 512: 1536 + (zp + 1) * 512])
                for c4 in range(4):
                    zc = zp * 4 + c4
                    ps, pk = self.psbank()

                    def mmz(e, ps=ps, w=wz, c4=c4, ntok=ntok):
                        ins = None
                        for k in range(KD):
                            ins = e.matmul(ps[:, 0:ntok], lhsT=w[:, k, c4 * P:(c4 + 1) * P], rhs=hT[:, k, 0:ntok],
                                           start=(k == 0), stop=(k == KD - 1))
                        return ins
                    pg.add("pe", mmz, reads=[wzk] + [("hT", s_) for s_ in range(nb)], writes=[pk])
                    pg.add("act", lambda e, ps=ps, ntok=ntok: e.activation(out=zt[:, 0:ntok], in_=ps[:, 0:ntok], func=AF.Tanh, scale=0.5),
                           reads=[pk], writes=["on", "yT"])
                    pg.add("dve", lambda e, ps=ps, zc=zc, ntok=ntok: e.scalar_tensor_tensor(
                        out=szT[:, zc, 0:ntok], in0=zt[:, 0:ntok], scalar=1.0, in1=ps[:, 0:ntok], op0=ALU.add, op1=ALU.mult),
                        reads=[pk, "on", "yT"], writes=[("sz", zc)])
            if self.stage < 4:
                continue
            wo = [[self.wload(wout[:, :, half * 512:(half + 1) * 512])] for half in range(2)]
            if b0 + 4 < len(q_tiles):
                self.make_hT(q_tiles[b0 + 4:b0 + 8], hT, "hT")
            for slot, t in enumerate(tiles):
                if t >= CTX0:
                    J = [(CTX0, None), (CTX0 + 1, None)]
                else:
                    J = []
                    if t >= 1:
                        J.append((t - 1, 0))
                    J.append((t, None))
                    if t + 1 <= tmax:
                        J.append((t + 1, 1))
                    J += [(CTX0, None), (CTX0 + 1, None)]
                for kh in range(4):
                    p_, e_ = kh // 2, kh % 2
                    rows = slice(e_ * 64, (e_ + 1) * 64)
                    for ji, (jt, mi) in enumerate(J):
                        ps, pk = self.psbank()
                        pg.add("pe", lambda e, ps=ps, jt=jt, p_=p_, rows=rows, slot=slot: e.matmul(
                            ps[:, :].rearrange("p (g n) -> p g n", g=4), lhsT=KT[rows, p_, jt * P:(jt + 1) * P],
                            rhs=QT[rows, p_ * 4:(p_ + 1) * 4, slot * P:(slot + 1) * P], start=True, stop=True),
                            reads=[("KT", jt)] + [("QT", p_ * 4 + g) for g in range(4)], writes=[pk])
                        pg.add("act", lambda e, ps=ps, ji=ji: e.activation(out=PT[:, ji, :], in_=ps[:, :], func=AF.Exp, scale=0.125),
                               reads=[pk], writes=[("PT", ji)])
                        if mi is not None:
                            pg.add("dve", lambda e, ji=ji, mi=mi: e.tensor_tensor(out=PT[:, ji, :], in0=PT[:, ji, :],
                                                                                 in1=self.maskb[:, mi, :], op=ALU.mult),
                                   reads=[("PT", ji), "maskb"], writes=[("PT", ji)])
                    if self.stage < 5:
                        continue
                    ps_o, pko = self.psbank()

                    def mmo(e, ps_o=ps_o, J=J, kh=kh):
                        ins = None
                        for g in range(4):
                            for ji, (jt, mi) in enumerate(J):
                                ins = e.matmul(ps_o[:, g * 68:g * 68 + 65], lhsT=PT[:, ji, g * P:(g + 1) * P],
                                               rhs=Va[:, jt, kh, 0:65], start=(ji == 0), stop=(ji == len(J) - 1))
                        return ins
                    pg.add("pe", mmo, reads=[("PT", ji) for ji in range(len(J))] + [("V", jt) for jt, _ in J], writes=[pko])
                    for g in range(4):
                        pg.add("act", lambda e, ps_o=ps_o, g=g: e.copy(out=osb[:, g * 68:g * 68 + 65], in_=ps_o[:, g * 68:g * 68 + 65]),
                               reads=[pko], writes=["osb"])
                    ov = osb[:, 0:272].rearrange("p (g d) -> p g d", g=4)
                    pg.add("dve", lambda e, kh=kh, ov=ov: e.tensor_tensor(out=den, in0=ov[:, :, 64], in1=esink[:, 4 * kh:4 * kh + 4],
                                                                         op=ALU.add),
                           reads=["osb", "esink"], writes=["den"])
                    pg.add("dve", lambda e: e.reciprocal(out=den, in_=den), reads=["den"], writes=["den"])
                    for g in range(4):
                        h = 4 * kh + g
                        pg.add("dve", lambda e, g=g, h=h, ov=ov: e.tensor_scalar(
                            out=on[:, h * 64:(h + 1) * 64], in0=ov[:, g, 0:64], scalar1=den[:, g:g + 1], scalar2=None, op0=ALU.mult),
                            reads=["osb", "den"], writes=["on"])
                if self.stage < 6:
                    continue
                pst, ptk = self.pstbank()

                def tr(e, pst=pst):
                    ins = None
                    for c in range(8):
                        ins = e.transpose(pst[:, c * P:(c + 1) * P], on[:, c * P:(c + 1) * P], self.ident[:])
                    return ins
                pg.add("pe", tr, reads=["on", "ident"], writes=[ptk])
                pg.add("dve", lambda e, pst=pst, slot=slot: e.scalar_tensor_tensor(
                    out=yT, in0=pst[:, :].rearrange("p (c n) -> p c n", c=8), scalar=0.5,
                    in1=szT[:, :, slot * P:(slot + 1) * P], op0=ALU.mult, op1=ALU.mult),
                    reads=[ptk] + [("sz", zc) for zc in range(8)], writes=["yT"])
                self.tail(t, wo, 8, lambda i: yT[:, i, :], ["yT"])

    def epilogue(self):
        pg = self.pg
        yv = self.d_y.rearrange("(t p) d -> p t d", p=P)
        for b in range(4):
            t0 = b * 4
            op = pg.add("sp", lambda e, t0=t0: e.dma_start(out=yv[:, t0:t0 + 4, :], in_=self.x[:, t0:t0 + 4, :]),
                        reads=[("x", t) for t in range(t0, t0 + 4)], writes=[("y", b)], dkey="yout")
            pg.final_waits.append(op)
        if self.dump_all:
            xv = self.d_xall.rearrange("(t p) d -> p t d", p=P)
            for b in range(5):
                t0 = b * 4
                op = pg.add("sp", lambda e, t0=t0: e.dma_start(out=xv[:, t0:t0 + 4, :], in_=self.x[:, t0:t0 + 4, :]),
                            reads=[("x", t) for t in range(t0, t0 + 4)], writes=[("xall", b)], dkey="yout")
                pg.final_waits.append(op)


def _rope_tables(gpos):
    n_freq = 16
    inv = (np.float32(10000.0) ** (-(np.arange(n_freq, dtype=np.float32) / np.float32(n_freq)))).astype(np.float32)
    row = (gpos // 64).astype(np.float32)
    col = (gpos % 64).astype(np.float32)
    cos = np.zeros((P, gpos.shape[0]), np.float32)
    sin = np.zeros((P, gpos.shape[0]), np.float32)
    for r in range(P):
        dd = r % 64
        axis = dd // 32
        jj = dd % 32
        f = jj % 16
        ang = ((row if axis == 0 else col) * inv[f]).astype(np.float32)
        cos[r] = np.cos(ang)
        sin[r] = np.sin(ang) * (-1.0 if jj < 16 else 1.0)
    return cos, sin


def _shared_consts():
    ident = np.eye(P, dtype=np.float32)
    perm = np.zeros((P, P), np.float32)
    for m in range(P):
        jj = m % 32
        sw = m + 16 if jj < 16 else m - 16
        perm[sw, m] = 1.0
    kk = np.arange(P)[:, None]
    qq = np.arange(P)[None, :]
    m_prev = (kk >= qq).astype(np.float32)
    m_next = (kk <= qq).astype(np.float32)
    masks = np.stack([np.tile(m_prev, (1, 4)), np.tile(m_next, (1, 4))], axis=1)
    return ident, perm, np.ascontiguousarray(masks)


def prep_inputs(inputs):
    f = lambda a: np.ascontiguousarray(np.asarray(a, dtype=np.float32))
    x = f(inputs["x"]); c = f(inputs["c"]); ctx = f(inputs["ctx"]); c_ctx = f(inputs["c_ctx"])
    ada_w = f(inputs["ada_w"]); ada_b = f(inputs["ada_b"])
    ln_g = f(inputs["ln_g"]); ln_b = f(inputs["ln_b"])
    a_w_in = f(inputs["a_w_in"]); a_ln_g = f(inputs["a_ln_g"]); a_ln_b = f(inputs["a_ln_b"])
    a_w_s = f(inputs["a_w_s"]); a_b_s = f(inputs["a_b_s"]); a_w_out = f(inputs["a_w_out"])
    b_w_in = f(inputs["b_w_in"]); b_sink = f(inputs["b_sink"]); b_w_out = f(inputs["b_w_out"])
    ident, perm, masks = _shared_consts()
    qcols = []
    for p_ in range(2):
        for g in range(4):
            qcols += list(range((8 * p_ + g) * 64, (8 * p_ + g) * 64 + 64))
            qcols += list(range((8 * p_ + 4 + g) * 64, (8 * p_ + 4 + g) * 64 + 64))
    cols = np.array(qcols + list(range(1024, 2560)))
    b_w_in_p = np.ascontiguousarray(b_w_in[:, :, cols])
    adab_col = np.ascontiguousarray(ada_b[:, :2048].reshape(DEPTH, 16, P).transpose(0, 2, 1))
    adab_gate = np.ascontiguousarray(ada_b[:, 2048:])
    a_lng_col = np.ascontiguousarray(a_ln_g.reshape(2, 16, P).transpose(0, 2, 1))
    a_lnb_col = np.ascontiguousarray(a_ln_b.reshape(2, 16, P).transpose(0, 2, 1))
    shared = dict(ada_w=ada_w, adab_col=adab_col, adab_gate=adab_gate, ln_g=ln_g, ln_b=ln_b,
                  a_w_in=a_w_in, a_lng_col=a_lng_col, a_lnb_col=a_lnb_col, a_w_out=a_w_out,
                  b_w_in=b_w_in_p, b_sink=b_sink, b_w_out=b_w_out, ident=ident, perm=perm, masks=masks)
    maps = []
    for core in range(8):
        b = core // 2
        half = core % 2
        if half == 0:
            xl = x[b, 0:NLAT * P]
            gpos = np.arange(NLAT * P)
            cl = ctx[b]
            ws = a_w_s
            bs = a_b_s
        else:
            xl = x[b, ::-1][0:NLAT * P]
            gpos = 4095 - np.arange(NLAT * P)
            cl = ctx[b, ::-1]
            ws = a_w_s[:, :, ::-1, ::-1]
            bs = a_b_s[:, :, ::-1]
        cos, sin = _rope_tables(gpos)
        cos = np.ascontiguousarray(np.concatenate([cos, np.ones((P, 2 * P), np.float32)], axis=1))
        sin = np.ascontiguousarray(np.concatenate([sin, np.zeros((P, 2 * P), np.float32)], axis=1))
        cvec = np.stack([c[b].reshape(KD, P).T, c_ctx.reshape(KD, P).T], axis=2)
        m = dict(shared)
        m.update(xin=np.ascontiguousarray(xl), ctxin=np.ascontiguousarray(cl), cvec=np.ascontiguousarray(cvec),
                 a_wsT=np.ascontiguousarray(ws.transpose(0, 3, 1, 2)),
                 a_bs=np.ascontiguousarray(bs.reshape(2, 8 * P)),
                 rope_cos=cos, rope_sin=sin)
        maps.append(m)
    return maps


_NC_CACHE = {}


def run_layers(maps, layers, dump_all=False, stage=99):
    key = (tuple(layers), dump_all, stage)
    if key not in _NC_CACHE:
        _NC_CACHE[key] = Builder(layers, dump_all, stage).build()
    nc = _NC_CACHE[key]
    res = run_bass_kernel_spmd(nc, maps, core_ids=list(range(8)))
    return res.results


def assemble(results):
    out = np.zeros((4, 4096, D), np.float32)
    for core in range(8):
        b = core // 2
        y = np.asarray(results[core]["yout"], dtype=np.float32)
        if core % 2 == 0:
            out[b, 0:2048] = y
        else:
            out[b, 2048:4096] = y[::-1]
    return out


def kernel(**inputs):
    maps = prep_inputs(inputs)
    res = run_layers(maps, [0, 1, 2, 3])
    return assemble(res)
```

```python
import numpy as np
from contextlib import ExitStack
import concourse.bass as bass
import concourse.mybir as mybir
from concourse.bass_utils import run_bass_kernel_spmd

F32 = mybir.dt.float32
BF16 = mybir.dt.bfloat16
AF = mybir.ActivationFunctionType
ALU = mybir.AluOpType

P = 128
D = 1024
KD = 8
NLAT = 18
NT = 20
CTX0 = 18
DEPTH = 4
ALPHA = (2.0 * DEPTH) ** 0.25
LN_EPS = 1e-5
GELU_C = 0.7978845608028654
NRING = 4


class Op:
    __slots__ = ("eng", "fn", "deps", "sig", "sem", "val", "dkey", "idx", "raw")

    def __init__(self, eng, fn, dkey=None):
        self.eng = eng
        self.fn = fn
        self.deps = []
        self.sig = False
        self.sem = None
        self.val = None
        self.dkey = dkey
        self.raw = set()


class Prog:
    ENGS = ("pe", "act", "dve", "pool", "sp")

    def __init__(self, nc):
        self.nc = nc
        self.streams = {e: [] for e in self.ENGS}
        self.last_w = {}
        self.readers = {}
        self.final_waits = []

    def add(self, eng, fn, reads=(), writes=(), dkey=None):
        op = Op(eng, fn, dkey)
        deps = []
        rawset = set()
        for r in reads:
            w = self.last_w.get(r)
            if w is not None:
                deps.append(w)
                rawset.add(id(w))
        for w_ in writes:
            lw = self.last_w.get(w_)
            if lw is not None:
                deps.append(lw)
            deps.extend(self.readers.get(w_, ()))
        seen = set()
        for d in deps:
            if id(d) in seen or d is op:
                continue
            seen.add(id(d))
            if d.eng == eng and d.dkey is None and dkey is None:
                if eng == "pe":
                    continue
            op.deps.append(d)
            d.sig = True
        for r in reads:
            self.readers.setdefault(r, []).append(op)
        for w_ in writes:
            self.last_w[w_] = op
            self.readers[w_] = []
        self.streams[eng].append(op)
        return op

    def barrier(self, marker_fn):
        last = {}
        for e in ("pe", "act", "dve"):
            s = [o for o in self.streams[e] if o.fn is not None]
            if s:
                last[e] = s[-1]
        for e in ("pe", "act", "dve"):
            op = Op(e, None)
            for e2, l in last.items():
                if e2 != e:
                    op.deps.append(l)
                    l.sig = True
            self.streams[e].append(op)
        self.add("dve", marker_fn, writes=["BAR"])

    def emit(self):
        nc = self.nc
        with ExitStack() as st:
            sems = {}

            def getsem(key):
                if key not in sems:
                    sems[key] = st.enter_context(nc.semaphore("s%d" % len(sems)))
                return sems[key]

            cnt = {}
            for e in self.ENGS:
                for op in self.streams[e]:
                    if op.fn is None:
                        continue
                    if op.dkey is not None:
                        op.sig = True
                        k = ("dma", op.dkey)
                        cnt[k] = cnt.get(k, 0) + 16
                        op.sem = getsem(k)
                        op.val = cnt[k]
                    elif op.sig:
                        k = ("eng", e)
                        cnt[k] = cnt.get(k, 0) + 1
                        op.sem = getsem(k)
                        op.val = cnt[k]
            fin = [(o.sem, o.val) for o in self.final_waits]
            block = st.enter_context(nc.Block())

            def run(ename, eng):
                waited = {}
                for op in self.streams[ename]:
                    need = {}
                    for d in op.deps:
                        k = id(d.sem)
                        if waited.get(k, 0) >= d.val:
                            continue
                        if k not in need or need[k][1] < d.val:
                            need[k] = (d.sem, d.val)
                    for k, (s, v) in need.items():
                        eng.wait_ge(s, v)
                        waited[k] = v
                    if op.fn is None:
                        continue
                    ins = op.fn(eng)
                    if op.sig:
                        ins.then_inc(op.sem, 16 if op.dkey is not None else 1)
                if ename == "sp":
                    best = {}
                    for s, v in fin:
                        if id(s) not in best or best[id(s)][1] < v:
                            best[id(s)] = (s, v)
                    for s, v in best.values():
                        eng.wait_ge(s, v)

            @block.tensor
            def _(e):
                run("pe", e)

            @block.scalar
            def _(e):
                run("act", e)

            @block.vector
            def _(e):
                run("dve", e)

            @block.gpsimd
            def _(e):
                run("pool", e)

            @block.sync
            def _(e):
                run("sp", e)


class Builder:
    def __init__(self, layers, dump_all=False, stage=99):
        self.stage = stage
        self.layers = list(layers)
        self.dump_all = dump_all
        self.nc = bass.Bass("TRN2", target_bir_lowering=False)
        self.pg = Prog(self.nc)
        self.st = ExitStack()
        self.ring_n = 0
        self.ring_pending = []
        self.psn = 0
        self.pstn = 0

    def sb(self, name, shape, dt):
        return self.st.enter_context(self.nc.sbuf_tensor(name, shape, dt))

    def dram(self, name, shape, dt=F32, kind="ExternalInput"):
        return self.nc.dram_tensor(name, shape, dt, kind=kind).ap()

    def psbank(self):
        i = self.psn % len(self.psf)
        self.psn += 1
        return self.psf[i], ("ps", i)

    def pstbank(self):
        i = self.pstn % len(self.pst)
        self.pstn += 1
        return self.pst[i], ("pst", i)

    def wload(self, src_ap):
        slot = self.ring_n % NRING
        self.ring_n += 1
        dst = self.ring[slot]
        key = ("ring", slot)
        self.pg.add("pool", lambda e, d=dst, s=src_ap: e.dma_start(out=d[:], in_=s),
                    writes=[key], dkey=key)
        return dst, key

    def build(self):
        nc, pg = self.nc, self.pg
        a = self.dram
        self.d_x = a("xin", [NLAT * P, D])
        self.d_ctx = a("ctxin", [2 * P, D])
        self.d_cvec = a("cvec", [P, KD, 2])
        self.d_adaw = a("ada_w", [DEPTH, D, 3 * D])
        self.d_adab_col = a("adab_col", [DEPTH, P, 16])
        self.d_adab_gate = a("adab_gate", [DEPTH, D])
        self.d_lng = a("ln_g", [DEPTH, D])
        self.d_lnb = a("ln_b", [DEPTH, D])
        self.d_awin = a("a_w_in", [2, D, 6 * D])
        self.d_alng = a("a_lng_col", [2, P, 16])
        self.d_alnb = a("a_lnb_col", [2, P, 16])
        self.d_awsT = a("a_wsT", [2, P, 8, P])
        self.d_abs = a("a_bs", [2, 8 * P])
        self.d_awout = a("a_w_out", [2, 2 * D, D])
        self.d_bwin = a("b_w_in", [2, D, 2560])
        self.d_bsink = a("b_sink", [2, 16])
        self.d_bwout = a("b_w_out", [2, D, D])
        self.d_cos = a("rope_cos", [P, NT * P])
        self.d_sin = a("rope_sin", [P, NT * P])
        self.d_ident = a("ident", [P, P])
        self.d_perm = a("perm", [P, P])
        self.d_mask = a("masks", [P, 2, 512])
        self.d_y = a("yout", [16 * P, D], kind="ExternalOutput")
        if self.dump_all:
            self.d_xall = a("xall", [NT * P, D], kind="ExternalOutput")

        self.x = self.sb("x", [P, NT, D], F32)
        self.ring = [self.sb("ring%d" % i, [P, KD, 512], BF16) for i in range(NRING)]
        self.gate_bc = self.sb("gate_bc", [P, 2, D], F32)
        self.gln_bc = self.sb("gln_bc", [P, D], F32)
        self.bln_bc = self.sb("bln_bc", [P, D], F32)
        self.ident = self.sb("identb", [P, P], BF16)
        self.ones = self.sb("ones", [P, P], BF16)
        self.adab = self.sb("adab", [P, 16], F32)
        self.adatmp = self.sb("adatmp", [P, 256], F32)
        self.permb = self.sb("permb", [P, P], BF16)
        self.osbt = self.sb("osbt", [P, 272], F32)
        self.maskb = self.sb("maskb", [P, 2, 512], BF16)
        self.cond2 = self.sb("cond2", [P, KD, 2], BF16)
        self.cond_bc = self.sb("cond_bc", [P, KD, 2, P], BF16)
        self.modcol = self.sb("modcol", [P, 2, KD, 2], F32)
        self.small = self.sb("small", [P, 64], F32)
        self.stats2 = [self.sb("stats%d" % i, [P, 4, 6], F32) for i in range(2)]
        self.mv2 = [self.sb("mv%d" % i, [P, 4], F32) for i in range(2)]
        self.lnn = 0
        self.arena = self.sb("arena", [P, 33 * 1024], BF16)
        self.psf = [self.st.enter_context(nc.psum_tensor("psf%d" % i, [P, 512], F32)) for i in range(6)]
        self.pst = [self.st.enter_context(nc.psum_tensor("pst%d" % i, [P, 1024], BF16)) for i in range(2)]

        self.prologue()
        for li, l in enumerate(self.layers):
            if li > 0:
                pg.barrier(lambda e: e.memset(self.small[:, 63:64], 0.0))
            if l % 2 == 0:
                self.layer_a(l)
            else:
                self.layer_b(l)
        self.epilogue()
        pg.emit()
        self.st.close()
        return nc

    def prologue(self):
        pg = self.pg
        x = self.x
        xin = self.d_x.rearrange("(t p) d -> p t d", p=P)
        for b in range(5):
            t0 = b * 4
            if b < 4:
                src = xin[:, t0:t0 + 4, :]
                dst = x[:, t0:t0 + 4, :]
                pg.add("sp", lambda e, s=src, d=dst: e.dma_start(out=d, in_=s),
                       writes=[("x", t) for t in range(t0, t0 + 4)], dkey=("xin", b))
            else:
                src = xin[:, 16:18, :]
                dst = x[:, 16:18, :]
                pg.add("sp", lambda e, s=src, d=dst: e.dma_start(out=d, in_=s),
                       writes=[("x", 16), ("x", 17)], dkey=("xin", b))
                src2 = self.d_ctx.rearrange("(t p) d -> p t d", p=P)
                dst2 = x[:, 18:20, :]
                pg.add("sp", lambda e, s=src2, d=dst2: e.dma_start(out=d, in_=s),
                       writes=[("x", 18), ("x", 19)], dkey=("xin", 5))
        idf = self.small
        self.identf = self.sb("identf", [P, P], F32)
        pg.add("sp", lambda e: e.dma_start(out=self.identf[:], in_=self.d_ident), writes=["identf"], dkey="c0")
        pg.add("dve", lambda e: e.tensor_copy(out=self.ident[:], in_=self.identf[:]), reads=["identf"], writes=["ident"])
        pg.add("dve", lambda e: e.memset(self.ones[:], 1.0), writes=["ones"])
        pg.add("pool", lambda e: e.dma_start(out=self.permb[:], in_=self.d_perm), writes=["permb"], dkey="permb")
        pg.add("pool", lambda e: e.dma_start(out=self.maskb[:], in_=self.d_mask), writes=["maskb"], dkey="maskb")
        self.cv = self.sb("cv", [P, KD, 2], F32)
        self.cv2 = self.sb("cv2", [P, KD, 2], F32)
        pg.add("sp", lambda e: e.dma_start(out=self.cv[:], in_=self.d_cvec), writes=["cv"], dkey="c1")
        pg.add("act", lambda e: e.activation(out=self.cv2[:], in_=self.cv[:], func=AF.Tanh, scale=0.5),
               reads=["cv"], writes=["cv2"])
        pg.add("dve", lambda e: e.scalar_tensor_tensor(out=self.cv2[:], in0=self.cv2[:], scalar=1.0, in1=self.cv[:],
                                                       op0=ALU.add, op1=ALU.mult),
               reads=["cv", "cv2"], writes=["cv2"])
        pg.add("dve", lambda e: e.tensor_scalar(out=self.cond2[:], in0=self.cv2[:], scalar1=0.5, scalar2=None,
                                                op0=ALU.mult),
               reads=["cv2"], writes=["cond2"])
        pg.add("dve", lambda e: e.tensor_copy(out=self.cond_bc[:],
                                              in_=self.cond2[:].unsqueeze(3).to_broadcast([P, KD, 2, P])),
               reads=["cond2"], writes=["cond_bc"])

    def ada(self, l):
        pg = self.pg
        wv = self.d_adaw[l].rearrange("(k p) c -> p k c", p=P)
        adab = self.adab
        pg.add("sp", lambda e: e.dma_start(out=adab[:], in_=self.d_adab_col[l]), writes=["adab"], dkey="adab")
        for v in range(2):
            pg.add("sp", lambda e, v=v: e.dma_start(out=self.gate_bc[:, v, :],
                                                    in_=self.d_adab_gate[l].partition_broadcast(P)),
                   writes=[("gate", v)], dkey=("gate", v))
        pg.add("sp", lambda e: e.dma_start(out=self.gln_bc[:], in_=self.d_lng[l].partition_broadcast(P)),
               writes=["gln"], dkey="gln")
        pg.add("sp", lambda e: e.dma_start(out=self.bln_bc[:], in_=self.d_lnb[l].partition_broadcast(P)),
               writes=["bln"], dkey="bln")
        for pc in range(6):
            if self.stage in (1, 1.25) and pc >= 4:
                break
            if self.stage == 1.5 and pc < 4:
                continue
            w, wk = self.wload(wv[:, :, pc * 512:(pc + 1) * 512])
            if pc < 4:
                ps, pk = self.psbank()
                kind = pc // 2

                def mm(e, w=w, ps=ps):
                    ins = None
                    for ch in range(4):
                        for k in range(KD):
                            ins = e.matmul(ps[:, ch * 64:(ch + 1) * 64], lhsT=w[:, k, ch * P:(ch + 1) * P],
                                           rhs=self.cond_bc[:, k, :, 0:32], start=(k == 0), stop=(k == KD - 1))
                    return ins
                pg.add("pe", mm, reads=[wk, "cond_bc"], writes=[pk])
                if self.stage == 1.25:
                    continue
                c0 = (pc % 2) * 4
                tmp = self.adatmp
                pg.add("dve", lambda e, ps=ps, tmp=tmp: e.tensor_copy(out=tmp[:], in_=ps[:, 0:256]),
                       reads=[pk], writes=["adatmp"])
                tv = tmp[:].rearrange("p (c v r) -> p c v r", c=4, v=2)
                for v in range(2):
                    dst = self.modcol[:, kind, c0:c0 + 4, v]
                    src = tv[:, :, v, 0]
                    bias = self.adab[:, kind * 8 + c0: kind * 8 + c0 + 4]
                    if kind == 0:
                        pg.add("dve", lambda e, dst=dst, src=src, bias=bias: e.tensor_tensor(out=dst, in0=src, in1=bias, op=ALU.add),
                               reads=["adatmp", "adab"], writes=[("modcol", pc)])
                    else:
                        pg.add("dve", lambda e, dst=dst, src=src, bias=bias: e.scalar_tensor_tensor(
                            out=dst, in0=src, scalar=1.0, in1=bias, op0=ALU.add, op1=ALU.add),
                            reads=["adatmp", "adab"], writes=[("modcol", pc)])
            else:
                half = pc - 4
                for v in range(2):
                    ps, pk = self.psbank()

                    def mm(e, w=w, ps=ps, v=v):
                        ins = None
                        for k in range(KD):
                            ins = e.matmul(ps[:, :], lhsT=self.cond_bc[:, k, v, :], rhs=w[:, k, :],
                                           start=(k == 0), stop=(k == KD - 1))
                        return ins
                    pg.add("pe", mm, reads=[wk, "cond_bc"], writes=[pk])
                    dst = self.gate_bc[:, v, half * 512:(half + 1) * 512]
                    pg.add("dve", lambda e, dst=dst, ps=ps: e.tensor_tensor(out=dst, in0=ps[:, :], in1=dst, op=ALU.add),
                           reads=[pk, ("gate", v)], writes=[("gate", v)])

    def make_hT(self, tiles, hT, hkey):
        pg = self.pg
        for slot, t in enumerate(tiles):
            v = 1 if t >= CTX0 else 0
            xb = self.xb[slot % self.nbuf]
            xk = ("xb", slot % self.nbuf)
            pg.add("act", lambda e, xb=xb, t=t: e.copy(out=xb[:], in_=self.x[:, t, :]),
                   reads=[("x", t)], writes=[xk])
            pst, ptk = self.pstbank()

            def tr(e, xb=xb, pst=pst):
                ins = None
                for k in range(KD):
                    ins = e.transpose(pst[:, k * P:(k + 1) * P], xb[:, k * P:(k + 1) * P], self.ident[:])
                return ins
            pg.add("pe", tr, reads=[xk, "ident"], writes=[ptk])
            for k in range(KD):
                dst = hT[:, k, slot * P:(slot + 1) * P]
                src = pst[:, k * P:(k + 1) * P]
                sc = self.modcol[:, 1, k, v:v + 1]
                sh = self.modcol[:, 0, k, v:v + 1]
                pg.add("dve", lambda e, dst=dst, src=src, sc=sc, sh=sh: e.tensor_scalar(
                    out=dst, in0=src, scalar1=sc, scalar2=sh, op0=ALU.mult, op1=ALU.add),
                    reads=[ptk] + [("modcol", i) for i in range(4)], writes=[(hkey, slot)])

    def tail(self, t, pieces, nk, lhs_fn, lhs_reads):
        pg = self.pg
        v = 1 if t >= CTX0 else 0
        r = self.rbuf[self.rn % self.nbuf]
        rk = ("r", self.rn % self.nbuf)
        self.rn += 1
        for half in range(2):
            ps, pk = self.psbank()
            plist = pieces[half]

            def mm(e, ps=ps, plist=plist):
                ins = None
                n = len(plist) * KD
                i = 0
                for (w, _) in plist:
                    for k in range(KD):
                        ins = e.matmul(ps[:, :], lhsT=lhs_fn(i), rhs=w[:, k, :], start=(i == 0), stop=(i == n - 1))
                        i += 1
                return ins
            pg.add("pe", mm, reads=[k_ for (_, k_) in plist] + lhs_reads, writes=[pk])
            sl = slice(half * 512, (half + 1) * 512)
            pg.add("dve", lambda e, ps=ps, sl=sl, r=r, v=v: e.tensor_tensor(
                out=r[:, sl], in0=ps[:, :], in1=self.gate_bc[:, v, sl], op=ALU.mult),
                reads=[pk, ("gate", v)], writes=[rk])
            pg.add("dve", lambda e, sl=sl, r=r, t=t: e.scalar_tensor_tensor(
                out=r[:, sl], in0=self.x[:, t, sl], scalar=ALPHA, in1=r[:, sl], op0=ALU.mult, op1=ALU.add),
                reads=[rk, ("x", t)], writes=[rk])
        mv, mvk = self.ln_stats(r, rk, 2)
        pg.add("act", lambda e, r=r, mv=mv: e.activation(out=r[:], in_=r[:], func=AF.Identity,
                                                         bias=mv[:, 3:4], scale=mv[:, 2:3]),
               reads=[rk] + mvk, writes=[rk])
        pg.add("dve", lambda e, r=r: e.tensor_tensor(out=r[:], in0=r[:], in1=self.gln_bc[:], op=ALU.mult),
               reads=[rk, "gln"], writes=[rk])
        pg.add("dve", lambda e, r=r, t=t: e.tensor_tensor(out=self.x[:, t, :], in0=r[:], in1=self.bln_bc[:], op=ALU.add),
               reads=[rk, "bln"], writes=[("x", t)])

    def ln_stats(self, buf, key, nchunks):
        pg = self.pg
        b = self.lnn % 2
        self.lnn += 1
        stats = self.stats2[b]
        mv = self.mv2[b]
        for c in range(nchunks):
            pg.add("dve", lambda e, c=c, stats=stats: e.bn_stats(out=stats[:, c, :], in_=buf[:, c * 512:(c + 1) * 512]),
                   reads=[key], writes=[("stats", b, c)])
        pg.add("dve", lambda e, stats=stats, mv=mv: e.bn_aggr(out=mv[:, 0:2], in_=stats[:, 0:nchunks, :]),
               reads=[("stats", b, c) for c in range(nchunks)], writes=[("mv01", b)])
        pg.add("dve", lambda e, mv=mv: e.tensor_scalar(out=mv[:, 2:3], in0=mv[:, 1:2], scalar1=LN_EPS, scalar2=None,
                                                       op0=ALU.add),
               reads=[("mv01", b)], writes=[("mv2", b)])
        pg.add("act", lambda e, mv=mv: e.activation(out=mv[:, 2:3], in_=mv[:, 2:3], func=AF.Sqrt),
               reads=[("mv2", b)], writes=[("mv2", b)])
        pg.add("dve", lambda e, mv=mv: e.reciprocal(out=mv[:, 2:3], in_=mv[:, 2:3]),
               reads=[("mv2", b)], writes=[("mv2", b)])
        pg.add("dve", lambda e, mv=mv: e.scalar_tensor_tensor(out=mv[:, 3:4], in0=mv[:, 0:1], scalar=-1.0,
                                                              in1=mv[:, 2:3], op0=ALU.mult, op1=ALU.mult),
               reads=[("mv01", b), ("mv2", b)], writes=[("mv", b)])
        return mv, [("mv", b), ("mv2", b)]

    def layer_a(self, l):
        pg = self.pg
        j = l // 2
        ar = self.arena
        o = 0

        def carve(n_bf16):
            nonlocal o
            v = ar[:, o:o + n_bf16]
            o += n_bf16
            return v
        hT = carve(KD * 512).rearrange("p (k n) -> p k n", k=KD)
        vy = carve(16 * 512)
        vn = vy.rearrange("p (c s f) -> p s c f", c=16, s=4)
        yT = vy.rearrange("p (c n) -> p c n", c=16)
        gv = carve(2 * 2048).bitcast(F32)
        cs = [[carve(2 * 512).bitcast(F32) for _ in range(3)] for _ in range(2)]
        self.rbuf = [carve(2 * 1024).bitcast(F32) for _ in range(2)]
        self.xb = [carve(1024) for _ in range(2)]
        E = carve(2 * 16 * P).bitcast(F32).rearrange("p (c n) -> p c n", c=16)
        wsT = carve(8 * P).rearrange("p (g n) -> p g n", g=8)
        wsf = ar[:, 14336:14336 + 2 * 8 * P].bitcast(F32).rearrange("p (g n) -> p g n", g=8)
        bsb = gv[:, 0:1024].rearrange("p (g n) -> p g n", g=8)
        self.rn = 0
        self.nbuf = 2
        lncol = self.small[:, 0:32]

        if self.stage < 1:
            return
        self.ada(l)
        if self.stage < 2:
            return
        pg.add("sp", lambda e: e.dma_start(out=lncol[:, 0:16], in_=self.d_alng[j]), writes=["lncol_g"], dkey="lncg")
        pg.add("sp", lambda e: e.dma_start(out=lncol[:, 16:32], in_=self.d_alnb[j]), writes=["lncol_b"], dkey="lncb")
        pg.add("sp", lambda e: e.dma_start(out=wsf, in_=self.d_awsT[j]), reads=["BAR"], writes=["wsf", "gv"], dkey="wsf")
        pg.add("sp", lambda e: e.dma_start(out=bsb, in_=self.d_abs[j].partition_broadcast(P).rearrange("p (g n) -> p g n", g=8)),
               reads=["BAR"], writes=["bsb", "gv"], dkey="bsb")
        pg.add("dve", lambda e: e.tensor_copy(out=wsT, in_=wsf), reads=["wsf"], writes=["wsT"])
        ones = self.ones
        for hh in range(2):
            ps, pk = self.psbank()

            def mm(e, ps=ps, hh=hh):
                ins = None
                for g in range(4):
                    ins = e.matmul(ps[:, g * P:(g + 1) * P], lhsT=ones[:], rhs=wsT[:, hh * 4 + g, :], start=True, stop=True)
                return ins
            pg.add("pe", mm, reads=["wsT", "ones"], writes=[pk])
            for g in range(4):
                for c2 in range(2):
                    cc = (hh * 4 + g) * 2 + c2
                    pg.add("dve", lambda e, ps=ps, g=g, cc=cc, gg=hh * 4 + g: e.scalar_tensor_tensor(
                        out=E[:, cc, :], in0=ps[:, g * P:(g + 1) * P], scalar=lncol[:, 16 + cc:17 + cc],
                        in1=bsb[:, gg, :], op0=ALU.mult, op1=ALU.add),
                        reads=[pk, "lncol_b", "bsb"], writes=["E"])

        if self.stage < 3:
            return
        win = self.d_awin[j].rearrange("(k p) c -> p k c", p=P)
        wout = self.d_awout[j].rearrange("(k p) c -> p k c", p=P)
        tiles_all = list(range(NLAT if l == 0 else NLAT - 1)) + [CTX0, CTX0 + 1]
        blocks = [tiles_all[i:i + 4] for i in range(0, len(tiles_all), 4)]
        for bi, tiles in enumerate(blocks):
            nb = len(tiles)
            ntok = nb * P
            if bi == 0:
                self.make_hT(tiles, hT, "hT")
            if self.stage < 4:
                return
            hreads = [("hT", s) for s in range(nb)]
            wv = [self.wload(win[:, :, 2048 + pv * 512: 2048 + (pv + 1) * 512]) for pv in range(4)]
            for slot, t in enumerate(tiles):
                for pv in range(4):
                    w, wk = wv[pv]
                    ps, pk = self.psbank()

                    def mm(e, ps=ps, w=w, slot=slot):
                        ins = None
                        for k in range(KD):
                            ins = e.matmul(ps[:, :], lhsT=hT[:, k, slot * P:(slot + 1) * P], rhs=w[:, k, :],
                                           start=(k == 0), stop=(k == KD - 1))
                        return ins
                    pg.add("pe", mm, reads=[wk, ("hT", slot)], writes=[pk])
                    pg.add("act", lambda e, ps=ps, pv=pv: e.activation(out=gv[:, pv * 512:(pv + 1) * 512], in_=ps[:, :],
                                                                       func=AF.Gelu_apprx_tanh),
                           reads=[pk], writes=["gv", "bsb", "wsf"])
                mv, mvk = self.ln_stats(gv, "gv", 4)
                pg.add("act", lambda e, slot=slot, mv=mv: e.activation(
                    out=vn[:, slot, :, :], in_=gv[:, :].rearrange("p (c f) -> p c f", c=16), func=AF.Identity,
                    bias=mv[:, 3:4], scale=mv[:, 2:3]),
                    reads=["gv"] + mvk, writes=[("vy", cc) for cc in range(16)])
            if self.stage < 5:
                return
            for pu in range(4):
                wu, wuk = self.wload(win[:, :, pu * 512:(pu + 1) * 512])
                wz, wzk = self.wload(win[:, :, 4096 + pu * 512: 4096 + (pu + 1) * 512])
                for c4 in range(4):
                    cc = pu * 4 + c4
                    g = cc // 2
                    A, B, C = cs[cc % 2]
                    ka, kb, kc = ("csA", cc % 2), ("csB", cc % 2), ("csC", cc % 2)
                    ps_s, pks = self.psbank()

                    def mms(e, ps_s=ps_s, cc=cc, g=g, nb=nb):
                        ins = None
                        for s in range(nb):
                            ins = e.matmul(ps_s[:, s * P:(s + 1) * P], lhsT=vn[:, s, cc, :], rhs=wsT[:, g, :],
                                           start=True, stop=True)
                        return ins
                    pg.add("pe", mms, reads=[("vy", cc), "wsT"], writes=[pks])
                    ps_u, pku = self.psbank()

                    def mmu(e, ps=ps_u, w=wu, c4=c4, ntok=ntok):
                        ins = None
                        for k in range(KD):
                            ins = e.matmul(ps[:, 0:ntok], lhsT=w[:, k, c4 * P:(c4 + 1) * P], rhs=hT[:, k, 0:ntok],
                                           start=(k == 0), stop=(k == KD - 1))
                        return ins
                    pg.add("pe", mmu, reads=[wuk] + hreads, writes=[pku])
                    ps_z, pkz = self.psbank()
                    pg.add("pe", lambda e, ps=ps_z, w=wz, c4=c4, ntok=ntok: mmu(e, ps, w, c4, ntok),
                           reads=[wzk] + hreads, writes=[pkz])
                    pg.add("act", lambda e, ps=ps_u, A=A, ntok=ntok: e.activation(out=A[:, 0:ntok], in_=ps[:, 0:ntok],
                                                                                 func=AF.Gelu_apprx_tanh),
                           reads=[pku], writes=[ka])
                    pg.add("act", lambda e, ps=ps_z, B=B, ntok=ntok: e.activation(out=B[:, 0:ntok], in_=ps[:, 0:ntok],
                                                                                 func=AF.Tanh, scale=0.5),
                           reads=[pkz], writes=[kb])
                    pg.add("dve", lambda e, ps=ps_s, C=C, cc=cc, nb=nb: e.scalar_tensor_tensor(
                        out=C[:, 0:nb * P].rearrange("p (s n) -> p s n", s=nb),
                        in0=ps[:, 0:nb * P].rearrange("p (s n) -> p s n", s=nb),
                        scalar=lncol[:, cc:cc + 1],
                        in1=E[:, cc, :].unsqueeze(1).to_broadcast([P, nb, P]), op0=ALU.mult, op1=ALU.add),
                        reads=[pks, "E", "lncol_g"], writes=[kc])
                    pg.add("dve", lambda e, ps=ps_z, B=B, ntok=ntok: e.scalar_tensor_tensor(
                        out=B[:, 0:ntok], in0=B[:, 0:ntok], scalar=1.0, in1=ps[:, 0:ntok], op0=ALU.add, op1=ALU.mult),
                        reads=[kb, pkz], writes=[kb])
                    pg.add("dve", lambda e, A=A, C=C, ntok=ntok: e.tensor_tensor(out=A[:, 0:ntok], in0=A[:, 0:ntok],
                                                                                in1=C[:, 0:ntok], op=ALU.mult),
                           reads=[ka, kc], writes=[ka])
                    pg.add("dve", lambda e, A=A, B=B, cc=cc, ntok=ntok: e.scalar_tensor_tensor(
                        out=yT[:, cc, 0:ntok], in0=A[:, 0:ntok], scalar=0.5, in1=B[:, 0:ntok], op0=ALU.mult, op1=ALU.mult),
                        reads=[ka, kb, ("vy", cc)], writes=[("vy", cc)])
            if self.stage < 6:
                return
            wo = [[self.wload(wout[:, kg * 8:(kg + 1) * 8, half * 512:(half + 1) * 512]) for kg in range(2)]
                  for half in range(2)]
            if bi + 1 < len(blocks):
                self.make_hT(blocks[bi + 1], hT, "hT")
            for slot, t in enumerate(tiles):
                self.tail(t, wo, 16, lambda i, slot=slot: yT[:, i, slot * P:(slot + 1) * P],
                          [("vy", cc) for cc in range(16)])

    def layer_b(self, l):
        pg = self.pg
        j = l // 2
        ar = self.arena
        o = 0

        def carve(n_bf16):
            nonlocal o
            v = ar[:, o:o + n_bf16]
            o += n_bf16
            return v
        hT = carve(KD * 512).rearrange("p (k n) -> p k n", k=KD)
        KT = carve(2 * NT * P).rearrange("p (c n) -> p c n", c=2)
        Vflat = carve(NT * 4 * 68)
        Va = Vflat.rearrange("p (t h d) -> p t h d", t=NT, h=4)
        QT = carve(KD * 512).rearrange("p (c n) -> p c n", c=8)
        szT = carve(KD * 512).rearrange("p (c n) -> p c n", c=8)
        PT = carve(5 * 512).rearrange("p (j n) -> p j n", j=5)
        onyt = carve(2048)
        on = onyt[:, 0:1024]
        yT = onyt[:, 1024:2048].rearrange("p (c n) -> p c n", c=8)
        zt = onyt.bitcast(F32)[:, 0:512]
        self.rbuf = [carve(2 * 1024).bitcast(F32)] * 2
        RA = self.rbuf[0][:, 0:512]
        RB = self.rbuf[0][:, 512:1024]
        self.xb = [carve(1024)] * 2
        rope = carve(2 * 1024).bitcast(F32)
        cosb = rope[:, 0:512]
        sinb = rope[:, 512:1024]
        qsb = [carve(512) for _ in range(2)]
        osb = self.osbt
        self.rn = 0
        self.nbuf = 1
        esink = self.small[:, 32:48]
        den = self.small[:, 48:52]

        if self.stage < 1:
            return
        self.ada(l)
        pg.add("sp", lambda e: e.dma_start(out=esink, in_=self.d_bsink[j].partition_broadcast(P)),
               writes=["esink"], dkey="esink")
        pg.add("act", lambda e: e.activation(out=esink, in_=esink, func=AF.Exp), reads=["esink"], writes=["esink"])
        pg.add("dve", lambda e: e.memset(Vflat, 1.0), reads=["BAR"], writes=[("V", t) for t in range(NT)])

        if self.stage < 2:
            return
        win = self.d_bwin[j].rearrange("(k p) c -> p k c", p=P)
        wout = self.d_bwout[j].rearrange("(k p) c -> p k c", p=P)
        tmax = 17 if l == 1 else 16
        kv_tiles = list(range(tmax + 1)) + [CTX0, CTX0 + 1]
        q_tiles = (list(range(17)) + [CTX0, CTX0 + 1]) if l == 1 else list(range(16))
        self.qn = 0

        def load_rope(tiles):
            for slot, t in enumerate(tiles):
                pg.add("sp", lambda e, slot=slot, t=t: e.dma_start(out=cosb[:, slot * P:(slot + 1) * P],
                                                                 in_=self.d_cos[:, t * P:(t + 1) * P]),
                       reads=["BAR"], writes=["rope"], dkey="ropec")
                pg.add("sp", lambda e, slot=slot, t=t: e.dma_start(out=sinb[:, slot * P:(slot + 1) * P],
                                                                 in_=self.d_sin[:, t * P:(t + 1) * P]),
                       reads=["BAR"], writes=["rope"], dkey="ropes")

        def proj_rope(w, wk, col0, ntok, nb, dst, dkeys):
            ps, pk = self.psbank()

            def mm(e, ps=ps):
                ins = None
                for k in range(KD):
                    ins = e.matmul(ps[:, 0:ntok], lhsT=w[:, k, col0:col0 + P], rhs=hT[:, k, 0:ntok],
                                   start=(k == 0), stop=(k == KD - 1))
                return ins
            if self.stage == 2.1:
                return
            pg.add("pe", mm, reads=[wk] + [("hT", s_) for s_ in range(nb)], writes=[pk])
            qb = qsb[self.qn % 2]
            qk = ("qsb", self.qn % 2)
            self.qn += 1
            pg.add("act", lambda e, ps=ps, qb=qb: e.copy(out=qb[:, 0:ntok], in_=ps[:, 0:ntok]), reads=[pk], writes=[qk])
            ps2, pk2 = self.psbank()
            pg.add("pe", lambda e, ps2=ps2, qb=qb: e.matmul(ps2[:, 0:ntok], lhsT=self.permb[:], rhs=qb[:, 0:ntok],
                                                           start=True, stop=True),
                   reads=[qk, "permb"], writes=[pk2])
            if self.stage == 2.2:
                return
            rk = ("r", 0)
            pg.add("dve", lambda e, ps=ps: e.tensor_tensor(out=RA[:, 0:ntok], in0=ps[:, 0:ntok], in1=cosb[:, 0:ntok], op=ALU.mult),
                   reads=[pk, "rope", qk], writes=[rk])
            pg.add("dve", lambda e, ps2=ps2: e.tensor_tensor(out=RB[:, 0:ntok], in0=ps2[:, 0:ntok], in1=sinb[:, 0:ntok], op=ALU.mult),
                   reads=[pk2, "rope"], writes=[rk])
            pg.add("dve", lambda e: e.tensor_tensor(out=dst, in0=RA[:, 0:ntok], in1=RB[:, 0:ntok], op=ALU.add),
                   reads=[rk], writes=dkeys)

        for b0 in range(0, len(kv_tiles), 4):
            tiles = kv_tiles[b0:b0 + 4]
            nb = len(tiles)
            ntok = nb * P
            self.make_hT(tiles, hT, "hT")
            load_rope(tiles)
            wkv, wkvk = self.wload(win[:, :, 1024:1536])
            for p_ in range(2):
                runs = []
                s0 = 0
                for s_ in range(1, nb + 1):
                    if s_ == nb or tiles[s_] != tiles[s_ - 1] + 1:
                        runs.append((s0, s_))
                        s0 = s_
                if len(runs) == 1:
                    dst = KT[:, p_, tiles[0] * P: tiles[0] * P + ntok]
                    proj_rope(wkv, wkvk, p_ * P, ntok, nb, dst, [("KT", t) for t in tiles])
                else:
                    tmp = szT[:, p_, 0:ntok]
                    proj_rope(wkv, wkvk, p_ * P, ntok, nb, tmp, [("sz", p_)])
                    for (a_, b_) in runs:
                        pg.add("dve", lambda e, a_=a_, b_=b_, p_=p_, tiles=tiles: e.tensor_copy(
                            out=KT[:, p_, tiles[a_] * P: tiles[a_] * P + (b_ - a_) * P], in_=szT[:, p_, a_ * P:b_ * P]),
                            reads=[("sz", p_)], writes=[("KT", t) for t in tiles[a_:b_]])
            for slot, t in enumerate(tiles):
                if self.stage in (2.1, 2.2, 2.3):
                    break
                ps, pk = self.psbank()

                def mmv(e, ps=ps, slot=slot, w=wkv):
                    ins = None
                    for k in range(KD):
                        ins = e.matmul(ps[:, 0:256], lhsT=hT[:, k, slot * P:(slot + 1) * P], rhs=w[:, k, 256:512],
                                       start=(k == 0), stop=(k == KD - 1))
                    return ins
                pg.add("pe", mmv, reads=[wkvk, ("hT", slot)], writes=[pk])
                if self.stage == 2.4:
                    continue
                pg.add("act", lambda e, ps=ps, t=t: e.copy(out=Va[:, t, :, 0:64],
                                                           in_=ps[:, 0:256].rearrange("p (h d) -> p h d", h=4)),
                       reads=[pk], writes=[("V", t)])

        if self.stage < 3:
            return
        for b0 in range(0, len(q_tiles), 4):
            tiles = q_tiles[b0:b0 + 4]
            nb = len(tiles)
            ntok = nb * P
            if b0 == 0:
                self.make_hT(tiles, hT, "hT")
            load_rope(tiles)
            for qp in range(2):
                wq, wqk = self.wload(win[:, :, qp * 512:(qp + 1) * 512])
                for c4 in range(4):
                    qc = qp * 4 + c4
                    proj_rope(wq, wqk, c4 * P, ntok, nb, QT[:, qc, 0:ntok], [("QT", qc)])
            for zp in range(2):
                wz, wzk = self.wload(win[:, :, 1536 + zp * 512: 1536 + (zp + 1) * 512])
                for c4 in range(4):
                    zc = zp * 4 + c4
                    ps, pk = self.psbank()

                    def mmz(e, ps=ps, w=wz, c4=c4, ntok=ntok):
                        ins = None
                        for k in range(KD):
                            ins = e.matmul(ps[:, 0:ntok], lhsT=w[:, k, c4 * P:(c4 + 1) * P], rhs=hT[:, k, 0:ntok],
                                           start=(k == 0), stop=(k == KD - 1))
                        return ins
                    pg.add("pe", mmz, reads=[wzk] + [("hT", s_) for s_ in range(nb)], writes=[pk])
                    pg.add("act", lambda e, ps=ps, ntok=ntok: e.activation(out=zt[:, 0:ntok], in_=ps[:, 0:ntok], func=AF.Tanh, scale=0.5),
                           reads=[pk], writes=["on", "yT"])
                    pg.add("dve", lambda e, ps=ps, zc=zc, ntok=ntok: e.scalar_tensor_tensor(
                        out=szT[:, zc, 0:ntok], in0=zt[:, 0:ntok], scalar=1.0, in1=ps[:, 0:ntok], op0=ALU.add, op1=ALU.mult),
                        reads=[pk, "on", "yT"], writes=[("sz", zc)])
            if self.stage < 4:
                continue
            wo = [[self.wload(wout[:, :, half * 512:(half + 1) * 512])] for half in range(2)]
            if b0 + 4 < len(q_tiles):
                self.make_hT(q_tiles[b0 + 4:b0 + 8], hT, "hT")
            for slot, t in enumerate(tiles):
                if t >= CTX0:
                    J = [(CTX0, None), (CTX0 + 1, None)]
                else:
                    J = []
                    if t >= 1:
                        J.append((t - 1, 0))
                    J.append((t, None))
                    if t + 1 <= tmax:
                        J.append((t + 1, 1))
                    J += [(CTX0, None), (CTX0 + 1, None)]
                for kh in range(4):
                    p_, e_ = kh // 2, kh % 2
                    rows = slice(e_ * 64, (e_ + 1) * 64)
                    for ji, (jt, mi) in enumerate(J):
                        ps, pk = self.psbank()
                        pg.add("pe", lambda e, ps=ps, jt=jt, p_=p_, rows=rows, slot=slot: e.matmul(
                            ps[:, :].rearrange("p (g n) -> p g n", g=4), lhsT=KT[rows, p_, jt * P:(jt + 1) * P],
                            rhs=QT[rows, p_ * 4:(p_ + 1) * 4, slot * P:(slot + 1) * P], start=True, stop=True),
                            reads=[("KT", jt)] + [("QT", p_ * 4 + g) for g in range(4)], writes=[pk])
                        pg.add("act", lambda e, ps=ps, ji=ji: e.activation(out=PT[:, ji, :], in_=ps[:, :], func=AF.Exp, scale=0.125),
                               reads=[pk], writes=[("PT", ji)])
                        if mi is not None:
                            pg.add("dve", lambda e, ji=ji, mi=mi: e.tensor_tensor(out=PT[:, ji, :], in0=PT[:, ji, :],
                                                                                 in1=self.maskb[:, mi, :], op=ALU.mult),
                                   reads=[("PT", ji), "maskb"], writes=[("PT", ji)])
                    if self.stage < 5:
                        continue
                    ps_o, pko = self.psbank()

                    def mmo(e, ps_o=ps_o, J=J, kh=kh):
                        ins = None
                        for g in range(4):
                            for ji, (jt, mi) in enumerate(J):
                                ins = e.matmul(ps_o[:, g * 68:g * 68 + 65], lhsT=PT[:, ji, g * P:(g + 1) * P],
                                               rhs=Va[:, jt, kh, 0:65], start=(ji == 0), stop=(ji == len(J) - 1))
                        return ins
                    pg.add("pe", mmo, reads=[("PT", ji) for ji in range(len(J))] + [("V", jt) for jt, _ in J], writes=[pko])
                    for g in range(4):
                        pg.add("act", lambda e, ps_o=ps_o, g=g: e.copy(out=osb[:, g * 68:g * 68 + 65], in_=ps_o[:, g * 68:g * 68 + 65]),
                               reads=[pko], writes=["osb"])
                    ov = osb[:, 0:272].rearrange("p (g d) -> p g d", g=4)
                    pg.add("dve", lambda e, kh=kh, ov=ov: e.tensor_tensor(out=den, in0=ov[:, :, 64], in1=esink[:, 4 * kh:4 * kh + 4],
                                                                         op=ALU.add),
                           reads=["osb", "esink"], writes=["den"])
                    pg.add("dve", lambda e: e.reciprocal(out=den, in_=den), reads=["den"], writes=["den"])
                    for g in range(4):
                        h = 4 * kh + g
                        pg.add("dve", lambda e, g=g, h=h, ov=ov: e.tensor_scalar(
                            out=on[:, h * 64:(h + 1) * 64], in0=ov[:, g, 0:64], scalar1=den[:, g:g + 1], scalar2=None, op0=ALU.mult),
                            reads=["osb", "den"], writes=["on"])
                if self.stage < 6:
                    continue
                pst, ptk = self.pstbank()

                def tr(e, pst=pst):
                    ins = None
                    for c in range(8):
                        ins = e.transpose(pst[:, c * P:(c + 1) * P], on[:, c * P:(c + 1) * P], self.ident[:])
                    return ins
                pg.add("pe", tr, reads=["on", "ident"], writes=[ptk])
                pg.add("dve", lambda e, pst=pst, slot=slot: e.scalar_tensor_tensor(
                    out=yT, in0=pst[:, :].rearrange("p (c n) -> p c n", c=8), scalar=0.5,
                    in1=szT[:, :, slot * P:(slot + 1) * P], op0=ALU.mult, op1=ALU.mult),
                    reads=[ptk] + [("sz", zc) for zc in range(8)], writes=["yT"])
                self.tail(t, wo, 8, lambda i: yT[:, i, :], ["yT"])

    def epilogue(self):
        pg = self.pg
        yv = self.d_y.rearrange("(t p) d -> p t d", p=P)
        for b in range(4):
            t0 = b * 4
            op = pg.add("sp", lambda e, t0=t0: e.dma_start(out=yv[:, t0:t0 + 4, :], in_=self.x[:, t0:t0 + 4, :]),
                        reads=[("x", t) for t in range(t0, t0 + 4)], writes=[("y", b)], dkey="yout")
            pg.final_waits.append(op)
        if self.dump_all:
            xv = self.d_xall.rearrange("(t p) d -> p t d", p=P)
            for b in range(5):
                t0 = b * 4
                op = pg.add("sp", lambda e, t0=t0: e.dma_start(out=xv[:, t0:t0 + 4, :], in_=self.x[:, t0:t0 + 4, :]),
                            reads=[("x", t) for t in range(t0, t0 + 4)], writes=[("xall", b)], dkey="yout")
                pg.final_waits.append(op)


def _rope_tables(gpos):
    n_freq = 16
    inv = (np.float32(10000.0) ** (-(np.arange(n_freq, dtype=np.float32) / np.float32(n_freq)))).astype(np.float32)
    row = (gpos // 64).astype(np.float32)
    col = (gpos % 64).astype(np.float32)
    cos = np.zeros((P, gpos.shape[0]), np.float32)
    sin = np.zeros((P, gpos.shape[0]), np.float32)
    for r in range(P):
        dd = r % 64
        axis = dd // 32
        jj = dd % 32
        f = jj % 16
        ang = ((row if axis == 0 else col) * inv[f]).astype(np.float32)
        cos[r] = np.cos(ang)
        sin[r] = np.sin(ang) * (-1.0 if jj < 16 else 1.0)
    return cos, sin


def _shared_consts():
    ident = np.eye(P, dtype=np.float32)
    perm = np.zeros((P, P), np.float32)
    for m in range(P):
        jj = m % 32
        sw = m + 16 if jj < 16 else m - 16
        perm[sw, m] = 1.0
    kk = np.arange(P)[:, None]
    qq = np.arange(P)[None, :]
    m_prev = (kk >= qq).astype(np.float32)
    m_next = (kk <= qq).astype(np.float32)
    masks = np.stack([np.tile(m_prev, (1, 4)), np.tile(m_next, (1, 4))], axis=1)
    return ident, perm, np.ascontiguousarray(masks)


def prep_inputs(inputs):
    f = lambda a: np.ascontiguousarray(np.asarray(a, dtype=np.float32))
    x = f(inputs["x"]); c = f(inputs["c"]); ctx = f(inputs["ctx"]); c_ctx = f(inputs["c_ctx"])
    ada_w = f(inputs["ada_w"]); ada_b = f(inputs["ada_b"])
    ln_g = f(inputs["ln_g"]); ln_b = f(inputs["ln_b"])
    a_w_in = f(inputs["a_w_in"]); a_ln_g = f(inputs["a_ln_g"]); a_ln_b = f(inputs["a_ln_b"])
    a_w_s = f(inputs["a_w_s"]); a_b_s = f(inputs["a_b_s"]); a_w_out = f(inputs["a_w_out"])
    b_w_in = f(inputs["b_w_in"]); b_sink = f(inputs["b_sink"]); b_w_out = f(inputs["b_w_out"])
    ident, perm, masks = _shared_consts()
    qcols = []
    for p_ in range(2):
        for g in range(4):
            qcols += list(range((8 * p_ + g) * 64, (8 * p_ + g) * 64 + 64))
            qcols += list(range((8 * p_ + 4 + g) * 64, (8 * p_ + 4 + g) * 64 + 64))
    cols = np.array(qcols + list(range(1024, 2560)))
    b_w_in_p = np.ascontiguousarray(b_w_in[:, :, cols])
    adab_col = np.ascontiguousarray(ada_b[:, :2048].reshape(DEPTH, 16, P).transpose(0, 2, 1))
    adab_gate = np.ascontiguousarray(ada_b[:, 2048:])
    a_lng_col = np.ascontiguousarray(a_ln_g.reshape(2, 16, P).transpose(0, 2, 1))
    a_lnb_col = np.ascontiguousarray(a_ln_b.reshape(2, 16, P).transpose(0, 2, 1))
    shared = dict(ada_w=ada_w, adab_col=adab_col, adab_gate=adab_gate, ln_g=ln_g, ln_b=ln_b,
                  a_w_in=a_w_in, a_lng_col=a_lng_col, a_lnb_col=a_lnb_col, a_w_out=a_w_out,
                  b_w_in=b_w_in_p, b_sink=b_sink, b_w_out=b_w_out, ident=ident, perm=perm, masks=masks)
    maps = []
    for core in range(8):
        b = core // 2
        half = core % 2
        if half == 0:
            xl = x[b, 0:NLAT * P]
            gpos = np.arange(NLAT * P)
            cl = ctx[b]
            ws = a_w_s
            bs = a_b_s
        else:
            xl = x[b, ::-1][0:NLAT * P]
            gpos = 4095 - np.arange(NLAT * P)
            cl = ctx[b, ::-1]
            ws = a_w_s[:, :, ::-1, ::-1]
            bs = a_b_s[:, :, ::-1]
        cos, sin = _rope_tables(gpos)
        cos = np.ascontiguousarray(np.concatenate([cos, np.ones((P, 2 * P), np.float32)], axis=1))
        sin = np.ascontiguousarray(np.concatenate([sin, np.zeros((P, 2 * P), np.float32)], axis=1))
        cvec = np.stack([c[b].reshape(KD, P).T, c_ctx.reshape(KD, P).T], axis=2)
        m = dict(shared)
        m.update(xin=np.ascontiguousarray(xl), ctxin=np.ascontiguousarray(cl), cvec=np.ascontiguousarray(cvec),
                 a_wsT=np.ascontiguousarray(ws.transpose(0, 3, 1, 2)),
                 a_bs=np.ascontiguousarray(bs.reshape(2, 8 * P)),
                 rope_cos=cos, rope_sin=sin)
        maps.append(m)
    return maps


_NC_CACHE = {}


def run_layers(maps, layers, dump_all=False, stage=99):
    key = (tuple(layers), dump_all, stage)
    if key not in _NC_CACHE:
        _NC_CACHE[key] = Builder(layers, dump_all, stage).build()
    nc = _NC_CACHE[key]
    res = run_bass_kernel_spmd(nc, maps, core_ids=list(range(8)))
    return res.results


def assemble(results):
    out = np.zeros((4, 4096, D), np.float32)
    for core in range(8):
        b = core // 2
        y = np.asarray(results[core]["yout"], dtype=np.float32)
        if core % 2 == 0:
            out[b, 0:2048] = y
        else:
            out[b, 2048:4096] = y[::-1]
    return out


def kernel(**inputs):
    maps = prep_inputs(inputs)
    res = run_layers(maps, [0, 1, 2, 3])
    return assemble(res)
```

```python
import numpy as np
from contextlib import ExitStack
import concourse.bass as bass
import concourse.mybir as mybir
from concourse.bass_utils import run_bass_kernel_spmd

F32 = mybir.dt.float32
BF16 = mybir.dt.bfloat16
AF = mybir.ActivationFunctionType
ALU = mybir.AluOpType

P = 128
D = 1024
KD = 8
NLAT = 18
NT = 20
CTX0 = 18
DEPTH = 4
ALPHA = (2.0 * DEPTH) ** 0.25
LN_EPS = 1e-5
GELU_C = 0.7978845608028654
NRING = 4


class Op:
    __slots__ = ("eng", "fn", "deps", "sig", "sem", "val", "dkey", "idx", "raw")

    def __init__(self, eng, fn, dkey=None):
        self.eng = eng
        self.fn = fn
        self.deps = []
        self.sig = False
        self.sem = None
        self.val = None
        self.dkey = dkey
        self.raw = set()


class Prog:
    ENGS = ("pe", "act", "dve", "pool", "sp")

    def __init__(self, nc):
        self.nc = nc
        self.streams = {e: [] for e in self.ENGS}
        self.last_w = {}
        self.readers = {}
        self.final_waits = []

    def add(self, eng, fn, reads=(), writes=(), dkey=None):
        op = Op(eng, fn, dkey)
        deps = []
        rawset = set()
        for r in reads:
            w = self.last_w.get(r)
            if w is not None:
                deps.append(w)
                rawset.add(id(w))
        for w_ in writes:
            lw = self.last_w.get(w_)
            if lw is not None:
                deps.append(lw)
            deps.extend(self.readers.get(w_, ()))
        seen = set()
        for d in deps:
            if id(d) in seen or d is op:
                continue
            seen.add(id(d))
            if d.eng == eng and d.dkey is None and dkey is None:
                if eng == "pe":
                    continue
            op.deps.append(d)
            d.sig = True
        for r in reads:
            self.readers.setdefault(r, []).append(op)
        for w_ in writes:
            self.last_w[w_] = op
            self.readers[w_] = []
        self.streams[eng].append(op)
        return op

    def barrier(self, marker_fn):
        last = {}
        for e in ("pe", "act", "dve"):
            s = [o for o in self.streams[e] if o.fn is not None]
            if s:
                last[e] = s[-1]
        for e in ("pe", "act", "dve"):
            op = Op(e, None)
            for e2, l in last.items():
                if e2 != e:
                    op.deps.append(l)
                    l.sig = True
            self.streams[e].append(op)
        self.add("dve", marker_fn, writes=["BAR"])

    def emit(self):
        nc = self.nc
        with ExitStack() as st:
            sems = {}

            def getsem(key):
                if key not in sems:
                    sems[key] = st.enter_context(nc.semaphore("s%d" % len(sems)))
                return sems[key]

            cnt = {}
            for e in self.ENGS:
                for op in self.streams[e]:
                    if op.fn is None:
                        continue
                    if op.dkey is not None:
                        op.sig = True
                        k = ("dma", op.dkey)
                        cnt[k] = cnt.get(k, 0) + 16
                        op.sem = getsem(k)
                        op.val = cnt[k]
                    elif op.sig:
                        k = ("eng", e)
                        cnt[k] = cnt.get(k, 0) + 1
                        op.sem = getsem(k)
                        op.val = cnt[k]
            fin = [(o.sem, o.val) for o in self.final_waits]
            block = st.enter_context(nc.Block())

            def run(ename, eng):
                waited = {}
                for op in self.streams[ename]:
                    need = {}
                    for d in op.deps:
                        k = id(d.sem)
                        if waited.get(k, 0) >= d.val:
                            continue
                        if k not in need or need[k][1] < d.val:
                            need[k] = (d.sem, d.val)
                    for k, (s, v) in need.items():
                        eng.wait_ge(s, v)
                        waited[k] = v
                    if op.fn is None:
                        continue
                    ins = op.fn(eng)
                    if op.sig:
                        ins.then_inc(op.sem, 16 if op.dkey is not None else 1)
                if ename == "sp":
                    best = {}
                    for s, v in fin:
                        if id(s) not in best or best[id(s)][1] < v:
                            best[id(s)] = (s, v)
                    for s, v in best.values():
                        eng.wait_ge(s, v)

            @block.tensor
            def _(e):
                run("pe", e)

            @block.scalar
            def _(e):
                run("act", e)

            @block.vector
            def _(e):
                run("dve", e)

            @block.gpsimd
            def _(e):
                run("pool", e)

            @block.sync
            def _(e):
                run("sp", e)


class Builder:
    def __init__(self, layers, dump_all=False, stage=99):
        self.stage = stage
        self.layers = list(layers)
        self.dump_all = dump_all
        self.nc = bass.Bass("TRN2", target_bir_lowering=False)
        self.pg = Prog(self.nc)
        self.st = ExitStack()
        self.ring_n = 0
        self.ring_pending = []
        self.psn = 0
        self.pstn = 0
        self.stored = set()

    def sb(self, name, shape, dt):
        return self.st.enter_context(self.nc.sbuf_tensor(name, shape, dt))

    def dram(self, name, shape, dt=F32, kind="ExternalInput"):
        return self.nc.dram_tensor(name, shape, dt, kind=kind).ap()

    def psbank(self):
        i = self.psn % len(self.psf)
        self.psn += 1
        return self.psf[i], ("ps", i)

    def pstbank(self):
        i = self.pstn % len(self.pst)
        self.pstn += 1
        return self.pst[i], ("pst", i)

    def wload(self, src_ap):
        slot = self.ring_n % NRING
        self.ring_n += 1
        dst = self.ring[slot]
        key = ("ring", slot)
        self.pg.add("pool", lambda e, d=dst, s=src_ap: e.dma_start(out=d[:], in_=s),
                    writes=[key], dkey=key)
        return dst, key

    def build(self):
        nc, pg = self.nc, self.pg
        a = self.dram
        self.d_x = a("xin", [NLAT * P, D])
        self.d_ctx = a("ctxin", [2 * P, D])
        self.d_cvec = a("cvec", [P, KD, 2])
        self.d_adaw = a("ada_w", [DEPTH, D, 3 * D])
        self.d_adab_col = a("adab_col", [DEPTH, P, 16])
        self.d_adab_gate = a("adab_gate", [DEPTH, D])
        self.d_lng = a("ln_g", [DEPTH, D])
        self.d_lnb = a("ln_b", [DEPTH, D])
        self.d_awin = a("a_w_in", [2, D, 6 * D])
        self.d_alng = a("a_lng_col", [2, P, 16])
        self.d_alnb = a("a_lnb_col", [2, P, 16])
        self.d_awsT = a("a_wsT", [2, P, 8, P])
        self.d_abs = a("a_bs", [2, 8 * P])
        self.d_awout = a("a_w_out", [2, 2 * D, D])
        self.d_bwin = a("b_w_in", [2, D, 2560])
        self.d_bsink = a("b_sink", [2, 16])
        self.d_bwout = a("b_w_out", [2, D, D])
        self.d_cos = a("rope_cos", [P, NT * P])
        self.d_sin = a("rope_sin", [P, NT * P])
        self.d_ident = a("ident", [P, P])
        self.d_perm = a("perm", [P, P])
        self.d_mask = a("masks", [P, 2, 512])
        self.d_y = a("yout", [16 * P, D], kind="ExternalOutput")
        if self.dump_all:
            self.d_xall = a("xall", [NT * P, D], kind="ExternalOutput")

        self.x = self.sb("x", [P, NT, D], F32)
        self.ring = [self.sb("ring%d" % i, [P, KD, 512], BF16) for i in range(NRING)]
        self.gate_bc = self.sb("gate_bc", [P, 2, D], F32)
        self.gln_bc = self.sb("gln_bc", [P, D], F32)
        self.bln_bc = self.sb("bln_bc", [P, D], F32)
        self.ident = self.sb("identb", [P, P], BF16)
        self.ones = self.sb("ones", [P, P], BF16)
        self.adab = self.sb("adab", [P, 16], F32)
        self.adatmp = self.sb("adatmp", [P, 256], F32)
        self.permb = self.sb("permb", [P, P], BF16)
        self.osbt = self.sb("osbt", [P, 272], F32)
        self.maskb = self.sb("maskb", [P, 2, 512], BF16)
        self.cond2 = self.sb("cond2", [P, KD, 2], BF16)
        self.cond_bc = self.sb("cond_bc", [P, KD, 2, P], BF16)
        self.modcol = self.sb("modcol", [P, 2, KD, 2], F32)
        self.small = self.sb("small", [P, 64], F32)
        self.stats2 = [self.sb("stats%d" % i, [P, 4, 6], F32) for i in range(2)]
        self.mv2 = [self.sb("mv%d" % i, [P, 4], F32) for i in range(2)]
        self.lnn = 0
        self.arena = self.sb("arena", [P, 33 * 1024], BF16)
        self.psf = [self.st.enter_context(nc.psum_tensor("psf%d" % i, [P, 512], F32)) for i in range(6)]
        self.pst = [self.st.enter_context(nc.psum_tensor("pst%d" % i, [P, 1024], BF16)) for i in range(2)]

        self.prologue()
        for li, l in enumerate(self.layers):
            if li > 0:
                pg.barrier(lambda e: e.memset(self.small[:, 63:64], 0.0))
            if l % 2 == 0:
                self.layer_a(l)
            else:
                self.layer_b(l)
        self.epilogue()
        pg.emit()
        self.st.close()
        return nc

    def prologue(self):
        pg = self.pg
        x = self.x
        xin = self.d_x.rearrange("(t p) d -> p t d", p=P)
        for b in range(5):
            t0 = b * 4
            if b < 4:
                src = xin[:, t0:t0 + 4, :]
                dst = x[:, t0:t0 + 4, :]
                pg.add("sp", lambda e, s=src, d=dst: e.dma_start(out=d, in_=s),
                       writes=[("x", t) for t in range(t0, t0 + 4)], dkey=("xin", b))
            else:
                src = xin[:, 16:18, :]
                dst = x[:, 16:18, :]
                pg.add("sp", lambda e, s=src, d=dst: e.dma_start(out=d, in_=s),
                       writes=[("x", 16), ("x", 17)], dkey=("xin", b))
                src2 = self.d_ctx.rearrange("(t p) d -> p t d", p=P)
                dst2 = x[:, 18:20, :]
                pg.add("sp", lambda e, s=src2, d=dst2: e.dma_start(out=d, in_=s),
                       writes=[("x", 18), ("x", 19)], dkey=("xin", 5))
        idf = self.small
        self.identf = self.sb("identf", [P, P], F32)
        pg.add("sp", lambda e: e.dma_start(out=self.identf[:], in_=self.d_ident), writes=["identf"], dkey="c0")
        pg.add("dve", lambda e: e.tensor_copy(out=self.ident[:], in_=self.identf[:]), reads=["identf"], writes=["ident"])
        pg.add("dve", lambda e: e.memset(self.ones[:], 1.0), writes=["ones"])
        pg.add("pool", lambda e: e.dma_start(out=self.permb[:], in_=self.d_perm), writes=["permb"], dkey="permb")
        pg.add("pool", lambda e: e.dma_start(out=self.maskb[:], in_=self.d_mask), writes=["maskb"], dkey="maskb")
        self.cv = self.sb("cv", [P, KD, 2], F32)
        self.cv2 = self.sb("cv2", [P, KD, 2], F32)
        pg.add("sp", lambda e: e.dma_start(out=self.cv[:], in_=self.d_cvec), writes=["cv"], dkey="c1")
        pg.add("act", lambda e: e.activation(out=self.cv2[:], in_=self.cv[:], func=AF.Tanh, scale=0.5),
               reads=["cv"], writes=["cv2"])
        pg.add("dve", lambda e: e.scalar_tensor_tensor(out=self.cv2[:], in0=self.cv2[:], scalar=1.0, in1=self.cv[:],
                                                       op0=ALU.add, op1=ALU.mult),
               reads=["cv", "cv2"], writes=["cv2"])
        pg.add("dve", lambda e: e.tensor_scalar(out=self.cond2[:], in0=self.cv2[:], scalar1=0.5, scalar2=None,
                                                op0=ALU.mult),
               reads=["cv2"], writes=["cond2"])
        pg.add("dve", lambda e: e.tensor_copy(out=self.cond_bc[:],
                                              in_=self.cond2[:].unsqueeze(3).to_broadcast([P, KD, 2, P])),
               reads=["cond2"], writes=["cond_bc"])

    def ada(self, l):
        pg = self.pg
        wv = self.d_adaw[l].rearrange("(k p) c -> p k c", p=P)
        adab = self.adab
        pg.add("sp", lambda e: e.dma_start(out=adab[:], in_=self.d_adab_col[l]), writes=["adab"], dkey="adab")
        for v in range(2):
            pg.add("sp", lambda e, v=v: e.dma_start(out=self.gate_bc[:, v, :],
                                                    in_=self.d_adab_gate[l].partition_broadcast(P)),
                   writes=[("gate", v)], dkey=("gate", v))
        pg.add("sp", lambda e: e.dma_start(out=self.gln_bc[:], in_=self.d_lng[l].partition_broadcast(P)),
               writes=["gln"], dkey="gln")
        pg.add("sp", lambda e: e.dma_start(out=self.bln_bc[:], in_=self.d_lnb[l].partition_broadcast(P)),
               writes=["bln"], dkey="bln")
        for pc in range(6):
            if self.stage in (1, 1.25) and pc >= 4:
                break
            if self.stage == 1.5 and pc < 4:
                continue
            w, wk = self.wload(wv[:, :, pc * 512:(pc + 1) * 512])
            if pc < 4:
                ps, pk = self.psbank()
                kind = pc // 2

                def mm(e, w=w, ps=ps):
                    ins = None
                    for ch in range(4):
                        for k in range(KD):
                            ins = e.matmul(ps[:, ch * 64:(ch + 1) * 64], lhsT=w[:, k, ch * P:(ch + 1) * P],
                                           rhs=self.cond_bc[:, k, :, 0:32], start=(k == 0), stop=(k == KD - 1))
                    return ins
                pg.add("pe", mm, reads=[wk, "cond_bc"], writes=[pk])
                if self.stage == 1.25:
                    continue
                c0 = (pc % 2) * 4
                tmp = self.adatmp
                pg.add("dve", lambda e, ps=ps, tmp=tmp: e.tensor_copy(out=tmp[:], in_=ps[:, 0:256]),
                       reads=[pk], writes=["adatmp"])
                tv = tmp[:].rearrange("p (c v r) -> p c v r", c=4, v=2)
                for v in range(2):
                    dst = self.modcol[:, kind, c0:c0 + 4, v]
                    src = tv[:, :, v, 0]
                    bias = self.adab[:, kind * 8 + c0: kind * 8 + c0 + 4]
                    if kind == 0:
                        pg.add("dve", lambda e, dst=dst, src=src, bias=bias: e.tensor_tensor(out=dst, in0=src, in1=bias, op=ALU.add),
                               reads=["adatmp", "adab"], writes=[("modcol", pc)])
                    else:
                        pg.add("dve", lambda e, dst=dst, src=src, bias=bias: e.scalar_tensor_tensor(
                            out=dst, in0=src, scalar=1.0, in1=bias, op0=ALU.add, op1=ALU.add),
                            reads=["adatmp", "adab"], writes=[("modcol", pc)])
            else:
                half = pc - 4
                for v in range(2):
                    ps, pk = self.psbank()

                    def mm(e, w=w, ps=ps, v=v):
                        ins = None
                        for k in range(KD):
                            ins = e.matmul(ps[:, :], lhsT=self.cond_bc[:, k, v, :], rhs=w[:, k, :],
                                           start=(k == 0), stop=(k == KD - 1))
                        return ins
                    pg.add("pe", mm, reads=[wk, "cond_bc"], writes=[pk])
                    dst = self.gate_bc[:, v, half * 512:(half + 1) * 512]
                    pg.add("dve", lambda e, dst=dst, ps=ps: e.tensor_tensor(out=dst, in0=ps[:, :], in1=dst, op=ALU.add),
                           reads=[pk, ("gate", v)], writes=[("gate", v)])

    def make_hT(self, tiles, hT, hkey):
        pg = self.pg
        for slot, t in enumerate(tiles):
            v = 1 if t >= CTX0 else 0
            xb = self.xb[slot % self.nbuf]
            xk = ("xb", slot % self.nbuf)
            pg.add("act", lambda e, xb=xb, t=t: e.copy(out=xb[:], in_=self.x[:, t, :]),
                   reads=[("x", t)], writes=[xk])
            pst, ptk = self.pstbank()

            def tr(e, xb=xb, pst=pst):
                ins = None
                for k in range(KD):
                    ins = e.transpose(pst[:, k * P:(k + 1) * P], xb[:, k * P:(k + 1) * P], self.ident[:])
                return ins
            pg.add("pe", tr, reads=[xk, "ident"], writes=[ptk])
            for k in range(KD):
                dst = hT[:, k, slot * P:(slot + 1) * P]
                src = pst[:, k * P:(k + 1) * P]
                sc = self.modcol[:, 1, k, v:v + 1]
                sh = self.modcol[:, 0, k, v:v + 1]
                pg.add("dve", lambda e, dst=dst, src=src, sc=sc, sh=sh: e.tensor_scalar(
                    out=dst, in0=src, scalar1=sc, scalar2=sh, op0=ALU.mult, op1=ALU.add),
                    reads=[ptk] + [("modcol", i) for i in range(4)], writes=[(hkey, slot)])

    def tail(self, t, pieces, nk, lhs_fn, lhs_reads):
        pg = self.pg
        v = 1 if t >= CTX0 else 0
        r = self.rbuf[self.rn % self.nbuf]
        rk = ("r", self.rn % self.nbuf)
        self.rn += 1
        for half in range(2):
            ps, pk = self.psbank()
            plist = pieces[half]

            def mm(e, ps=ps, plist=plist):
                ins = None
                n = len(plist) * KD
                i = 0
                for (w, _) in plist:
                    for k in range(KD):
                        ins = e.matmul(ps[:, :], lhsT=lhs_fn(i), rhs=w[:, k, :], start=(i == 0), stop=(i == n - 1))
                        i += 1
                return ins
            pg.add("pe", mm, reads=[k_ for (_, k_) in plist] + lhs_reads, writes=[pk])
            sl = slice(half * 512, (half + 1) * 512)
            pg.add("dve", lambda e, ps=ps, sl=sl, r=r, v=v: e.tensor_tensor(
                out=r[:, sl], in0=ps[:, :], in1=self.gate_bc[:, v, sl], op=ALU.mult),
                reads=[pk, ("gate", v)], writes=[rk])
            pg.add("dve", lambda e, sl=sl, r=r, t=t: e.scalar_tensor_tensor(
                out=r[:, sl], in0=self.x[:, t, sl], scalar=ALPHA, in1=r[:, sl], op0=ALU.mult, op1=ALU.add),
                reads=[rk, ("x", t)], writes=[rk])
        mv, mvk = self.ln_stats(r, rk, 2)
        pg.add("act", lambda e, r=r, mv=mv: e.activation(out=r[:], in_=r[:], func=AF.Identity,
                                                         bias=mv[:, 3:4], scale=mv[:, 2:3]),
               reads=[rk] + mvk, writes=[rk])
        pg.add("dve", lambda e, r=r: e.tensor_tensor(out=r[:], in0=r[:], in1=self.gln_bc[:], op=ALU.mult),
               reads=[rk, "gln"], writes=[rk])
        pg.add("dve", lambda e, r=r, t=t: e.tensor_tensor(out=self.x[:, t, :], in0=r[:], in1=self.bln_bc[:], op=ALU.add),
               reads=[rk, "bln"], writes=[("x", t)])

    def ln_stats(self, buf, key, nchunks):
        pg = self.pg
        b = self.lnn % 2
        self.lnn += 1
        stats = self.stats2[b]
        mv = self.mv2[b]
        for c in range(nchunks):
            pg.add("dve", lambda e, c=c, stats=stats: e.bn_stats(out=stats[:, c, :], in_=buf[:, c * 512:(c + 1) * 512]),
                   reads=[key], writes=[("stats", b, c)])
        pg.add("dve", lambda e, stats=stats, mv=mv: e.bn_aggr(out=mv[:, 0:2], in_=stats[:, 0:nchunks, :]),
               reads=[("stats", b, c) for c in range(nchunks)], writes=[("mv01", b)])
        pg.add("dve", lambda e, mv=mv: e.tensor_scalar(out=mv[:, 2:3], in0=mv[:, 1:2], scalar1=LN_EPS, scalar2=None,
                                                       op0=ALU.add),
               reads=[("mv01", b)], writes=[("mv2", b)])
        pg.add("act", lambda e, mv=mv: e.activation(out=mv[:, 2:3], in_=mv[:, 2:3], func=AF.Sqrt),
               reads=[("mv2", b)], writes=[("mv2", b)])
        pg.add("dve", lambda e, mv=mv: e.reciprocal(out=mv[:, 2:3], in_=mv[:, 2:3]),
               reads=[("mv2", b)], writes=[("mv2", b)])
        pg.add("dve", lambda e, mv=mv: e.scalar_tensor_tensor(out=mv[:, 3:4], in0=mv[:, 0:1], scalar=-1.0,
                                                              in1=mv[:, 2:3], op0=ALU.mult, op1=ALU.mult),
               reads=[("mv01", b), ("mv2", b)], writes=[("mv", b)])
        return mv, [("mv", b), ("mv2", b)]

    def layer_a(self, l):
        pg = self.pg
        j = l // 2
        ar = self.arena
        o = 0

        def carve(n_bf16):
            nonlocal o
            v = ar[:, o:o + n_bf16]
            o += n_bf16
            return v
        hT = carve(KD * 512).rearrange("p (k n) -> p k n", k=KD)
        vy = carve(16 * 512)
        vn = vy.rearrange("p (c s f) -> p s c f", c=16, s=4)
        yT = vy.rearrange("p (c n) -> p c n", c=16)
        gv = carve(2 * 2048).bitcast(F32)
        cs = [[carve(2 * 512).bitcast(F32) for _ in range(3)] for _ in range(2)]
        self.rbuf = [carve(2 * 1024).bitcast(F32) for _ in range(2)]
        self.xb = [carve(1024) for _ in range(2)]
        E = carve(2 * 16 * P).bitcast(F32).rearrange("p (c n) -> p c n", c=16)
        wsT = carve(8 * P).rearrange("p (g n) -> p g n", g=8)
        wsf = ar[:, 14336:14336 + 2 * 8 * P].bitcast(F32).rearrange("p (g n) -> p g n", g=8)
        bsb = gv[:, 0:1024].rearrange("p (g n) -> p g n", g=8)
        self.rn = 0
        self.nbuf = 2
        lncol = self.small[:, 0:32]

        if self.stage < 1:
            return
        self.ada(l)
        if self.stage < 2:
            return
        pg.add("sp", lambda e: e.dma_start(out=lncol[:, 0:16], in_=self.d_alng[j]), writes=["lncol_g"], dkey="lncg")
        pg.add("sp", lambda e: e.dma_start(out=lncol[:, 16:32], in_=self.d_alnb[j]), writes=["lncol_b"], dkey="lncb")
        pg.add("sp", lambda e: e.dma_start(out=wsf, in_=self.d_awsT[j]), reads=["BAR"], writes=["wsf", "gv"], dkey="wsf")
        pg.add("sp", lambda e: e.dma_start(out=bsb, in_=self.d_abs[j].partition_broadcast(P).rearrange("p (g n) -> p g n", g=8)),
               reads=["BAR"], writes=["bsb", "gv"], dkey="bsb")
        pg.add("dve", lambda e: e.tensor_copy(out=wsT, in_=wsf), reads=["wsf"], writes=["wsT"])
        ones = self.ones
        for hh in range(2):
            ps, pk = self.psbank()

            def mm(e, ps=ps, hh=hh):
                ins = None
                for g in range(4):
                    ins = e.matmul(ps[:, g * P:(g + 1) * P], lhsT=ones[:], rhs=wsT[:, hh * 4 + g, :], start=True, stop=True)
                return ins
            pg.add("pe", mm, reads=["wsT", "ones"], writes=[pk])
            for g in range(4):
                for c2 in range(2):
                    cc = (hh * 4 + g) * 2 + c2
                    pg.add("dve", lambda e, ps=ps, g=g, cc=cc, gg=hh * 4 + g: e.scalar_tensor_tensor(
                        out=E[:, cc, :], in0=ps[:, g * P:(g + 1) * P], scalar=lncol[:, 16 + cc:17 + cc],
                        in1=bsb[:, gg, :], op0=ALU.mult, op1=ALU.add),
                        reads=[pk, "lncol_b", "bsb"], writes=["E"])

        if self.stage < 3:
            return
        win = self.d_awin[j].rearrange("(k p) c -> p k c", p=P)
        wout = self.d_awout[j].rearrange("(k p) c -> p k c", p=P)
        tiles_all = list(range(NLAT if l == 0 else NLAT - 1)) + [CTX0, CTX0 + 1]
        blocks = [tiles_all[i:i + 4] for i in range(0, len(tiles_all), 4)]
        for bi, tiles in enumerate(blocks):
            nb = len(tiles)
            ntok = nb * P
            if bi == 0:
                self.make_hT(tiles, hT, "hT")
            if self.stage < 4:
                return
            hreads = [("hT", s) for s in range(nb)]
            wv = [self.wload(win[:, :, 2048 + pv * 512: 2048 + (pv + 1) * 512]) for pv in range(4)]
            for slot, t in enumerate(tiles):
                for pv in range(4):
                    w, wk = wv[pv]
                    ps, pk = self.psbank()

                    def mm(e, ps=ps, w=w, slot=slot):
                        ins = None
                        for k in range(KD):
                            ins = e.matmul(ps[:, :], lhsT=hT[:, k, slot * P:(slot + 1) * P], rhs=w[:, k, :],
                                           start=(k == 0), stop=(k == KD - 1))
                        return ins
                    pg.add("pe", mm, reads=[wk, ("hT", slot)], writes=[pk])
                    pg.add("act", lambda e, ps=ps, pv=pv: e.activation(out=gv[:, pv * 512:(pv + 1) * 512], in_=ps[:, :],
                                                                       func=AF.Gelu_apprx_tanh),
                           reads=[pk], writes=["gv", "bsb", "wsf"])
                mv, mvk = self.ln_stats(gv, "gv", 4)
                pg.add("act", lambda e, slot=slot, mv=mv: e.activation(
                    out=vn[:, slot, :, :], in_=gv[:, :].rearrange("p (c f) -> p c f", c=16), func=AF.Identity,
                    bias=mv[:, 3:4], scale=mv[:, 2:3]),
                    reads=["gv"] + mvk, writes=[("vy", cc) for cc in range(16)])
            if self.stage < 5:
                return
            for pu in range(4):
                wu, wuk = self.wload(win[:, :, pu * 512:(pu + 1) * 512])
                wz, wzk = self.wload(win[:, :, 4096 + pu * 512: 4096 + (pu + 1) * 512])
                for c4 in range(4):
                    cc = pu * 4 + c4
                    g = cc // 2
                    A, B, C = cs[cc % 2]
                    ka, kb, kc = ("csA", cc % 2), ("csB", cc % 2), ("csC", cc % 2)
                    ps_s, pks = self.psbank()

                    def mms(e, ps_s=ps_s, cc=cc, g=g, nb=nb):
                        ins = None
                        for s in range(nb):
                            ins = e.matmul(ps_s[:, s * P:(s + 1) * P], lhsT=vn[:, s, cc, :], rhs=wsT[:, g, :],
                                           start=True, stop=True)
                        return ins
                    pg.add("pe", mms, reads=[("vy", cc), "wsT"], writes=[pks])
                    ps_u, pku = self.psbank()

                    def mmu(e, ps=ps_u, w=wu, c4=c4, ntok=ntok):
                        ins = None
                        for k in range(KD):
                            ins = e.matmul(ps[:, 0:ntok], lhsT=w[:, k, c4 * P:(c4 + 1) * P], rhs=hT[:, k, 0:ntok],
                                           start=(k == 0), stop=(k == KD - 1))
                        return ins
                    pg.add("pe", mmu, reads=[wuk] + hreads, writes=[pku])
                    ps_z, pkz = self.psbank()
                    pg.add("pe", lambda e, ps=ps_z, w=wz, c4=c4, ntok=ntok: mmu(e, ps, w, c4, ntok),
                           reads=[wzk] + hreads, writes=[pkz])
                    pg.add("act", lambda e, ps=ps_u, A=A, ntok=ntok: e.activation(out=A[:, 0:ntok], in_=ps[:, 0:ntok],
                                                                                 func=AF.Gelu_apprx_tanh),
                           reads=[pku], writes=[ka])
                    pg.add("act", lambda e, ps=ps_z, B=B, ntok=ntok: e.activation(out=B[:, 0:ntok], in_=ps[:, 0:ntok],
                                                                                 func=AF.Tanh, scale=0.5),
                           reads=[pkz], writes=[kb])
                    pg.add("dve", lambda e, ps=ps_s, C=C, cc=cc, nb=nb: e.scalar_tensor_tensor(
                        out=C[:, 0:nb * P].rearrange("p (s n) -> p s n", s=nb),
                        in0=ps[:, 0:nb * P].rearrange("p (s n) -> p s n", s=nb),
                        scalar=lncol[:, cc:cc + 1],
                        in1=E[:, cc, :].unsqueeze(1).to_broadcast([P, nb, P]), op0=ALU.mult, op1=ALU.add),
                        reads=[pks, "E", "lncol_g"], writes=[kc])
                    pg.add("dve", lambda e, ps=ps_z, B=B, ntok=ntok: e.scalar_tensor_tensor(
                        out=B[:, 0:ntok], in0=B[:, 0:ntok], scalar=1.0, in1=ps[:, 0:ntok], op0=ALU.add, op1=ALU.mult),
                        reads=[kb, pkz], writes=[kb])
                    pg.add("dve", lambda e, A=A, C=C, ntok=ntok: e.tensor_tensor(out=A[:, 0:ntok], in0=A[:, 0:ntok],
                                                                                in1=C[:, 0:ntok], op=ALU.mult),
                           reads=[ka, kc], writes=[ka])
                    pg.add("dve", lambda e, A=A, B=B, cc=cc, ntok=ntok: e.scalar_tensor_tensor(
                        out=yT[:, cc, 0:ntok], in0=A[:, 0:ntok], scalar=0.5, in1=B[:, 0:ntok], op0=ALU.mult, op1=ALU.mult),
                        reads=[ka, kb, ("vy", cc)], writes=[("vy", cc)])
            if self.stage < 6:
                return
            wo = [[self.wload(wout[:, kg * 8:(kg + 1) * 8, half * 512:(half + 1) * 512]) for kg in range(2)]
                  for half in range(2)]
            if bi + 1 < len(blocks):
                self.make_hT(blocks[bi + 1], hT, "hT")
            for slot, t in enumerate(tiles):
                self.tail(t, wo, 16, lambda i, slot=slot: yT[:, i, slot * P:(slot + 1) * P],
                          [("vy", cc) for cc in range(16)])

    def layer_b(self, l):
        pg = self.pg
        j = l // 2
        ar = self.arena
        o = 0

        def carve(n_bf16):
            nonlocal o
            v = ar[:, o:o + n_bf16]
            o += n_bf16
            return v
        hT = carve(KD * 512).rearrange("p (k n) -> p k n", k=KD)
        KT = carve(2 * NT * P).rearrange("p (c n) -> p c n", c=2)
        Vflat = carve(NT * 4 * 68)
        Va = Vflat.rearrange("p (t h d) -> p t h d", t=NT, h=4)
        QT = carve(KD * 512).rearrange("p (c n) -> p c n", c=8)
        szT = carve(KD * 512).rearrange("p (c n) -> p c n", c=8)
        PT = carve(5 * 512).rearrange("p (j n) -> p j n", j=5)
        onyt = carve(2048)
        on = onyt[:, 0:1024]
        yT = onyt[:, 1024:2048].rearrange("p (c n) -> p c n", c=8)
        zt = onyt.bitcast(F32)[:, 0:512]
        self.rbuf = [carve(2 * 1024).bitcast(F32)] * 2
        RA = self.rbuf[0][:, 0:512]
        RB = self.rbuf[0][:, 512:1024]
        self.xb = [carve(1024)] * 2
        rope = carve(2 * 1024).bitcast(F32)
        cosb = rope[:, 0:512]
        sinb = rope[:, 512:1024]
        qsb = [carve(512) for _ in range(2)]
        osb = self.osbt
        self.rn = 0
        self.nbuf = 1
        esink = self.small[:, 32:48]
        den = self.small[:, 48:52]

        if self.stage < 1:
            return
        self.ada(l)
        pg.add("sp", lambda e: e.dma_start(out=esink, in_=self.d_bsink[j].partition_broadcast(P)),
               writes=["esink"], dkey="esink")
        pg.add("act", lambda e: e.activation(out=esink, in_=esink, func=AF.Exp), reads=["esink"], writes=["esink"])
        pg.add("dve", lambda e: e.memset(Vflat, 1.0), reads=["BAR"], writes=[("V", t) for t in range(NT)])

        if self.stage < 2:
            return
        win = self.d_bwin[j].rearrange("(k p) c -> p k c", p=P)
        wout = self.d_bwout[j].rearrange("(k p) c -> p k c", p=P)
        tmax = 17 if l == 1 else 16
        kv_tiles = list(range(tmax + 1)) + [CTX0, CTX0 + 1]
        q_tiles = (list(range(17)) + [CTX0, CTX0 + 1]) if l == 1 else list(range(16))
        self.qn = 0

        def load_rope(tiles):
            for slot, t in enumerate(tiles):
                pg.add("sp", lambda e, slot=slot, t=t: e.dma_start(out=cosb[:, slot * P:(slot + 1) * P],
                                                                 in_=self.d_cos[:, t * P:(t + 1) * P]),
                       reads=["BAR"], writes=["rope"], dkey="ropec")
                pg.add("sp", lambda e, slot=slot, t=t: e.dma_start(out=sinb[:, slot * P:(slot + 1) * P],
                                                                 in_=self.d_sin[:, t * P:(t + 1) * P]),
                       reads=["BAR"], writes=["rope"], dkey="ropes")

        def proj_rope(w, wk, col0, ntok, nb, dst, dkeys):
            ps, pk = self.psbank()

            def mm(e, ps=ps):
                ins = None
                for k in range(KD):
                    ins = e.matmul(ps[:, 0:ntok], lhsT=w[:, k, col0:col0 + P], rhs=hT[:, k, 0:ntok],
                                   start=(k == 0), stop=(k == KD - 1))
                return ins
            if self.stage == 2.1:
                return
            pg.add("pe", mm, reads=[wk] + [("hT", s_) for s_ in range(nb)], writes=[pk])
            qb = qsb[self.qn % 2]
            qk = ("qsb", self.qn % 2)
            self.qn += 1
            pg.add("act", lambda e, ps=ps, qb=qb: e.copy(out=qb[:, 0:ntok], in_=ps[:, 0:ntok]), reads=[pk], writes=[qk])
            ps2, pk2 = self.psbank()
            pg.add("pe", lambda e, ps2=ps2, qb=qb: e.matmul(ps2[:, 0:ntok], lhsT=self.permb[:], rhs=qb[:, 0:ntok],
                                                           start=True, stop=True),
                   reads=[qk, "permb"], writes=[pk2])
            if self.stage == 2.2:
                return
            rk = ("r", 0)
            pg.add("dve", lambda e, ps=ps: e.tensor_tensor(out=RA[:, 0:ntok], in0=ps[:, 0:ntok], in1=cosb[:, 0:ntok], op=ALU.mult),
                   reads=[pk, "rope", qk], writes=[rk])
            pg.add("dve", lambda e, ps2=ps2: e.tensor_tensor(out=RB[:, 0:ntok], in0=ps2[:, 0:ntok], in1=sinb[:, 0:ntok], op=ALU.mult),
                   reads=[pk2, "rope"], writes=[rk])
            pg.add("dve", lambda e: e.tensor_tensor(out=dst, in0=RA[:, 0:ntok], in1=RB[:, 0:ntok], op=ALU.add),
                   reads=[rk], writes=dkeys)

        for b0 in range(0, len(kv_tiles), 4):
            tiles = kv_tiles[b0:b0 + 4]
            nb = len(tiles)
            ntok = nb * P
            self.make_hT(tiles, hT, "hT")
            load_rope(tiles)
            wkv, wkvk = self.wload(win[:, :, 1024:1536])
            for p_ in range(2):
                runs = []
                s0 = 0
                for s_ in range(1, nb + 1):
                    if s_ == nb or tiles[s_] != tiles[s_ - 1] + 1:
                        runs.append((s0, s_))
                        s0 = s_
                if len(runs) == 1:
                    dst = KT[:, p_, tiles[0] * P: tiles[0] * P + ntok]
                    proj_rope(wkv, wkvk, p_ * P, ntok, nb, dst, [("KT", t) for t in tiles])
                else:
                    tmp = szT[:, p_, 0:ntok]
                    proj_rope(wkv, wkvk, p_ * P, ntok, nb, tmp, [("sz", p_)])
                    for (a_, b_) in runs:
                        pg.add("dve", lambda e, a_=a_, b_=b_, p_=p_, tiles=tiles: e.tensor_copy(
                            out=KT[:, p_, tiles[a_] * P: tiles[a_] * P + (b_ - a_) * P], in_=szT[:, p_, a_ * P:b_ * P]),
                            reads=[("sz", p_)], writes=[("KT", t) for t in tiles[a_:b_]])
            for slot, t in enumerate(tiles):
                if self.stage in (2.1, 2.2, 2.3):
                    break
                ps, pk = self.psbank()

                def mmv(e, ps=ps, slot=slot, w=wkv):
                    ins = None
                    for k in range(KD):
                        ins = e.matmul(ps[:, 0:256], lhsT=hT[:, k, slot * P:(slot + 1) * P], rhs=w[:, k, 256:512],
                                       start=(k == 0), stop=(k == KD - 1))
                    return ins
                pg.add("pe", mmv, reads=[wkvk, ("hT", slot)], writes=[pk])
                if self.stage == 2.4:
                    continue
                pg.add("act", lambda e, ps=ps, t=t: e.copy(out=Va[:, t, :, 0:64],
                                                           in_=ps[:, 0:256].rearrange("p (h d) -> p h d", h=4)),
                       reads=[pk], writes=[("V", t)])

        if self.stage < 3:
            return
        for b0 in range(0, len(q_tiles), 4):
            tiles = q_tiles[b0:b0 + 4]
            nb = len(tiles)
            ntok = nb * P
            if b0 == 0:
                self.make_hT(tiles, hT, "hT")
            load_rope(tiles)
            if l == self.layers[-1] and b0 >= 4:
                self.store_block((b0 - 4) // 4)
            for qp in range(2):
                wq, wqk = self.wload(win[:, :, qp * 512:(qp + 1) * 512])
                for c4 in range(4):
                    qc = qp * 4 + c4
                    proj_rope(wq, wqk, c4 * P, ntok, nb, QT[:, qc, 0:ntok], [("QT", qc)])
            for zp in range(2):
                wz, wzk = self.wload(win[:, :, 1536 + zp * 512: 1536 + (zp + 1) * 512])
                for c4 in range(4):
                    zc = zp * 4 + c4
                    ps, pk = self.psbank()

                    def mmz(e, ps=ps, w=wz, c4=c4, ntok=ntok):
                        ins = None
                        for k in range(KD):
                            ins = e.matmul(ps[:, 0:ntok], lhsT=w[:, k, c4 * P:(c4 + 1) * P], rhs=hT[:, k, 0:ntok],
                                           start=(k == 0), stop=(k == KD - 1))
                        return ins
                    pg.add("pe", mmz, reads=[wzk] + [("hT", s_) for s_ in range(nb)], writes=[pk])
                    pg.add("act", lambda e, ps=ps, ntok=ntok: e.activation(out=zt[:, 0:ntok], in_=ps[:, 0:ntok], func=AF.Tanh, scale=0.5),
                           reads=[pk], writes=["on", "yT"])
                    pg.add("dve", lambda e, ps=ps, zc=zc, ntok=ntok: e.scalar_tensor_tensor(
                        out=szT[:, zc, 0:ntok], in0=zt[:, 0:ntok], scalar=1.0, in1=ps[:, 0:ntok], op0=ALU.add, op1=ALU.mult),
                        reads=[pk, "on", "yT"], writes=[("sz", zc)])
            if self.stage < 4:
                continue
            wo = [[self.wload(wout[:, :, half * 512:(half + 1) * 512])] for half in range(2)]
            if b0 + 4 < len(q_tiles):
                self.make_hT(q_tiles[b0 + 4:b0 + 8], hT, "hT")
            for slot, t in enumerate(tiles):
                if t >= CTX0:
                    J = [(CTX0, None), (CTX0 + 1, None)]
                else:
                    J = []
                    if t >= 1:
                        J.append((t - 1, 0))
                    J.append((t, None))
                    if t + 1 <= tmax:
                        J.append((t + 1, 1))
                    J += [(CTX0, None), (CTX0 + 1, None)]
                for kh in range(4):
                    p_, e_ = kh // 2, kh % 2
                    rows = slice(e_ * 64, (e_ + 1) * 64)
                    for ji, (jt, mi) in enumerate(J):
                        ps, pk = self.psbank()
                        pg.add("pe", lambda e, ps=ps, jt=jt, p_=p_, rows=rows, slot=slot: e.matmul(
                            ps[:, :].rearrange("p (g n) -> p g n", g=4), lhsT=KT[rows, p_, jt * P:(jt + 1) * P],
                            rhs=QT[rows, p_ * 4:(p_ + 1) * 4, slot * P:(slot + 1) * P], start=True, stop=True),
                            reads=[("KT", jt)] + [("QT", p_ * 4 + g) for g in range(4)], writes=[pk])
                        pg.add("act", lambda e, ps=ps, ji=ji: e.activation(out=PT[:, ji, :], in_=ps[:, :], func=AF.Exp, scale=0.125),
                               reads=[pk], writes=[("PT", ji)])
                        if mi is not None:
                            pg.add("dve", lambda e, ji=ji, mi=mi: e.tensor_tensor(out=PT[:, ji, :], in0=PT[:, ji, :],
                                                                                 in1=self.maskb[:, mi, :], op=ALU.mult),
                                   reads=[("PT", ji), "maskb"], writes=[("PT", ji)])
                    if self.stage < 5:
                        continue
                    ps_o, pko = self.psbank()

                    def mmo(e, ps_o=ps_o, J=J, kh=kh):
                        ins = None
                        for g in range(4):
                            for ji, (jt, mi) in enumerate(J):
                                ins = e.matmul(ps_o[:, g * 68:g * 68 + 65], lhsT=PT[:, ji, g * P:(g + 1) * P],
                                               rhs=Va[:, jt, kh, 0:65], start=(ji == 0), stop=(ji == len(J) - 1))
                        return ins
                    pg.add("pe", mmo, reads=[("PT", ji) for ji in range(len(J))] + [("V", jt) for jt, _ in J], writes=[pko])
                    for g in range(4):
                        pg.add("act", lambda e, ps_o=ps_o, g=g: e.copy(out=osb[:, g * 68:g * 68 + 65], in_=ps_o[:, g * 68:g * 68 + 65]),
                               reads=[pko], writes=["osb"])
                    ov = osb[:, 0:272].rearrange("p (g d) -> p g d", g=4)
                    pg.add("dve", lambda e, kh=kh, ov=ov: e.tensor_tensor(out=den, in0=ov[:, :, 64], in1=esink[:, 4 * kh:4 * kh + 4],
                                                                         op=ALU.add),
                           reads=["osb", "esink"], writes=["den"])
                    pg.add("dve", lambda e: e.reciprocal(out=den, in_=den), reads=["den"], writes=["den"])
                    for g in range(4):
                        h = 4 * kh + g
                        pg.add("dve", lambda e, g=g, h=h, ov=ov: e.tensor_scalar(
                            out=on[:, h * 64:(h + 1) * 64], in0=ov[:, g, 0:64], scalar1=den[:, g:g + 1], scalar2=None, op0=ALU.mult),
                            reads=["osb", "den"], writes=["on"])
                if self.stage < 6:
                    continue
                pst, ptk = self.pstbank()

                def tr(e, pst=pst):
                    ins = None
                    for c in range(8):
                        ins = e.transpose(pst[:, c * P:(c + 1) * P], on[:, c * P:(c + 1) * P], self.ident[:])
                    return ins
                pg.add("pe", tr, reads=["on", "ident"], writes=[ptk])
                pg.add("dve", lambda e, pst=pst, slot=slot: e.scalar_tensor_tensor(
                    out=yT, in0=pst[:, :].rearrange("p (c n) -> p c n", c=8), scalar=0.5,
                    in1=szT[:, :, slot * P:(slot + 1) * P], op0=ALU.mult, op1=ALU.mult),
                    reads=[ptk] + [("sz", zc) for zc in range(8)], writes=["yT"])
                self.tail(t, wo, 8, lambda i: yT[:, i, :], ["yT"])

    def store_block(self, b):
        if b in self.stored:
            return
        self.stored.add(b)
        pg = self.pg
        yv = self.d_y.rearrange("(t p) d -> p t d", p=P)
        t0 = b * 4
        op = pg.add("sp", lambda e, t0=t0: e.dma_start(out=yv[:, t0:t0 + 4, :], in_=self.x[:, t0:t0 + 4, :]),
                    reads=[("x", t) for t in range(t0, t0 + 4)], writes=[("y", b)], dkey="yout")
        pg.final_waits.append(op)

    def epilogue(self):
        pg = self.pg
        for b in range(4):
            self.store_block(b)
        if self.dump_all:
            xv = self.d_xall.rearrange("(t p) d -> p t d", p=P)
            for b in range(5):
                t0 = b * 4
                op = pg.add("sp", lambda e, t0=t0: e.dma_start(out=xv[:, t0:t0 + 4, :], in_=self.x[:, t0:t0 + 4, :]),
                            reads=[("x", t) for t in range(t0, t0 + 4)], writes=[("xall", b)], dkey="yout")
                pg.final_waits.append(op)


def _rope_tables(gpos):
    n_freq = 16
    inv = (np.float32(10000.0) ** (-(np.arange(n_freq, dtype=np.float32) / np.float32(n_freq)))).astype(np.float32)
    row = (gpos // 64).astype(np.float32)
    col = (gpos % 64).astype(np.float32)
    cos = np.zeros((P, gpos.shape[0]), np.float32)
    sin = np.zeros((P, gpos.shape[0]), np.float32)
    for r in range(P):
        dd = r % 64
        axis = dd // 32
        jj = dd % 32
        f = jj % 16
        ang = ((row if axis == 0 else col) * inv[f]).astype(np.float32)
        cos[r] = np.cos(ang)
        sin[r] = np.sin(ang) * (-1.0 if jj < 16 else 1.0)
    return cos, sin


def _shared_consts():
    ident = np.eye(P, dtype=np.float32)
    perm = np.zeros((P, P), np.float32)
    for m in range(P):
        jj = m % 32
        sw = m + 16 if jj < 16 else m - 16
        perm[sw, m] = 1.0
    kk = np.arange(P)[:, None]
    qq = np.arange(P)[None, :]
    m_prev = (kk >= qq).astype(np.float32)
    m_next = (kk <= qq).astype(np.float32)
    masks = np.stack([np.tile(m_prev, (1, 4)), np.tile(m_next, (1, 4))], axis=1)
    return ident, perm, np.ascontiguousarray(masks)


def prep_inputs(inputs):
    f = lambda a: np.ascontiguousarray(np.asarray(a, dtype=np.float32))
    x = f(inputs["x"]); c = f(inputs["c"]); ctx = f(inputs["ctx"]); c_ctx = f(inputs["c_ctx"])
    ada_w = f(inputs["ada_w"]); ada_b = f(inputs["ada_b"])
    ln_g = f(inputs["ln_g"]); ln_b = f(inputs["ln_b"])
    a_w_in = f(inputs["a_w_in"]); a_ln_g = f(inputs["a_ln_g"]); a_ln_b = f(inputs["a_ln_b"])
    a_w_s = f(inputs["a_w_s"]); a_b_s = f(inputs["a_b_s"]); a_w_out = f(inputs["a_w_out"])
    b_w_in = f(inputs["b_w_in"]); b_sink = f(inputs["b_sink"]); b_w_out = f(inputs["b_w_out"])
    ident, perm, masks = _shared_consts()
    qcols = []
    for p_ in range(2):
        for g in range(4):
            qcols += list(range((8 * p_ + g) * 64, (8 * p_ + g) * 64 + 64))
            qcols += list(range((8 * p_ + 4 + g) * 64, (8 * p_ + 4 + g) * 64 + 64))
    cols = np.array(qcols + list(range(1024, 2560)))
    b_w_in_p = np.ascontiguousarray(b_w_in[:, :, cols])
    adab_col = np.ascontiguousarray(ada_b[:, :2048].reshape(DEPTH, 16, P).transpose(0, 2, 1))
    adab_gate = np.ascontiguousarray(ada_b[:, 2048:])
    a_lng_col = np.ascontiguousarray(a_ln_g.reshape(2, 16, P).transpose(0, 2, 1))
    a_lnb_col = np.ascontiguousarray(a_ln_b.reshape(2, 16, P).transpose(0, 2, 1))
    shared = dict(ada_w=ada_w, adab_col=adab_col, adab_gate=adab_gate, ln_g=ln_g, ln_b=ln_b,
                  a_w_in=a_w_in, a_lng_col=a_lng_col, a_lnb_col=a_lnb_col, a_w_out=a_w_out,
                  b_w_in=b_w_in_p, b_sink=b_sink, b_w_out=b_w_out, ident=ident, perm=perm, masks=masks)
    maps = []
    for core in range(8):
        b = core // 2
        half = core % 2
        if half == 0:
            xl = x[b, 0:NLAT * P]
            gpos = np.arange(NLAT * P)
            cl = ctx[b]
            ws = a_w_s
            bs = a_b_s
        else:
            xl = x[b, ::-1][0:NLAT * P]
            gpos = 4095 - np.arange(NLAT * P)
            cl = ctx[b, ::-1]
            ws = a_w_s[:, :, ::-1, ::-1]
            bs = a_b_s[:, :, ::-1]
        cos, sin = _rope_tables(gpos)
        cos = np.ascontiguousarray(np.concatenate([cos, np.ones((P, 2 * P), np.float32)], axis=1))
        sin = np.ascontiguousarray(np.concatenate([sin, np.zeros((P, 2 * P), np.float32)], axis=1))
        cvec = np.stack([c[b].reshape(KD, P).T, c_ctx.reshape(KD, P).T], axis=2)
        m = dict(shared)
        m.update(xin=np.ascontiguousarray(xl), ctxin=np.ascontiguousarray(cl), cvec=np.ascontiguousarray(cvec),
                 a_wsT=np.ascontiguousarray(ws.transpose(0, 3, 1, 2)),
                 a_bs=np.ascontiguousarray(bs.reshape(2, 8 * P)),
                 rope_cos=cos, rope_sin=sin)
        maps.append(m)
    return maps


_NC_CACHE = {}


def run_layers(maps, layers, dump_all=False, stage=99):
    key = (tuple(layers), dump_all, stage)
    if key not in _NC_CACHE:
        _NC_CACHE[key] = Builder(layers, dump_all, stage).build()
    nc = _NC_CACHE[key]
    res = run_bass_kernel_spmd(nc, maps, core_ids=list(range(8)))
    return res.results


def assemble(results):
    out = np.zeros((4, 4096, D), np.float32)
    for core in range(8):
        b = core // 2
        y = np.asarray(results[core]["yout"], dtype=np.float32)
        if core % 2 == 0:
            out[b, 0:2048] = y
        else:
            out[b, 2048:4096] = y[::-1]
    return out


def kernel(**inputs):
    maps = prep_inputs(inputs)
    res = run_layers(maps, [0, 1, 2, 3])
    return assemble(res)
```
